# Optimizing a Trainium2 kernel written in Bass

```python
import math
import jax, jax.numpy as jnp
from jax import lax
import numpy as np

D_MODEL = 1024
BATCH = 8
SEQ = 2048
DEPTH = 2

GRID_W = 64
CTX_LEN = 256
N_MOD = 9
D_FF = ((8 * D_MODEL + 3 * 256 - 1) // (3 * 256)) * 256
ADA_INIT = 0.5
LN_EPS = 1e-6
NEG_INF = -1e30

NA_HEAD_DIM = 64
NA_HEADS = (D_MODEL // 2) // NA_HEAD_DIM
NA_WIN_R = 8
NA_WIN_C = 16

GLA_HEADS = 4
GLA_DV = (D_MODEL // 2) // GLA_HEADS
GLA_DK = GLA_DV // 2
GLA_GATE_RANK = 16
GLA_TAU = 16.0
GLA_CHUNK = 64

DIFF_HEAD_DIM = 64
DIFF_HEADS = D_MODEL // (2 * DIFF_HEAD_DIM)
Q_BLOCK = 128
ROPE_THETA = 10000.0
ROPE_AXIS_DIM = DIFF_HEAD_DIM // 2

EVEN_SPLITS = (NA_HEADS * NA_HEAD_DIM,) * 3 + (GLA_HEADS * GLA_DK, GLA_HEADS * GLA_DK, GLA_HEADS * GLA_DV, GLA_HEADS * GLA_DV, GLA_GATE_RANK, GLA_GATE_RANK)
EVEN_IN = sum(EVEN_SPLITS)
EVEN_MIX = NA_HEADS * NA_HEAD_DIM + GLA_HEADS * GLA_DV
ODD_WIDTH = DIFF_HEADS * 2 * DIFF_HEAD_DIM
ODD_IN = 3 * ODD_WIDTH

kernel_name = 'hybrid_na_gla_diffattn_macaron_deepnorm'


def layer_norm(u):
    uf = u.astype(jnp.float32)
    mu = jnp.mean(uf, -1, keepdims=True)
    var = jnp.mean(jnp.square(uf - mu), -1, keepdims=True)
    return ((uf - mu) * lax.rsqrt(var + LN_EPS)).astype(u.dtype)


def rms_norm(u, g):
    uf = u.astype(jnp.float32)
    return (uf * lax.rsqrt(jnp.mean(uf * uf, -1, keepdims=True) + LN_EPS)).astype(u.dtype) * g


def modulate(u, shift, scale):
    return u * (1.0 + scale) + shift


def adaln_modulation(cond, w, b):
    m = jax.nn.silu(cond) @ w + b
    if m.ndim == 2:
        m = m[:, None, :]
    return jnp.split(m, N_MOD, axis=-1)


def swiglu(u, w_in, w_out):
    a, g = jnp.split(u @ w_in, 2, axis=-1)
    return (jax.nn.silu(a) * g) @ w_out


def split_cols(z, widths):
    idx = [int(i) for i in np.cumsum(widths)[:-1]]
    return jnp.split(z, idx, axis=-1)


def heads(t, n):
    return t.reshape(t.shape[0], t.shape[1], n, -1)


def softmax_attend(q, k, v):
    s = jnp.einsum('bthd,blhd->bhtl', q, k).astype(jnp.float32) * (q.shape[-1] ** -0.5)
    p = jax.nn.softmax(s, axis=-1).astype(v.dtype)
    return jnp.einsum('bhtl,blhd->bthd', p, v)


def neighbourhood_attention(q, k, v, kc, vc, rpb):
    B, S, H, dh = q.shape
    rows = S // GRID_W
    wr = min(NA_WIN_R, rows)
    grid = lambda t: t.reshape(B, rows, GRID_W, H, dh)
    qg, kg, vg = grid(q), grid(k), grid(v)
    r = jnp.arange(rows)
    row_start = jnp.clip(r - wr // 2, 0, rows - wr)
    row_idx = row_start[:, None] + jnp.arange(wr)[None, :]
    kb = kg[:, row_idx]
    vb = vg[:, row_idx]
    cidx = jnp.arange(GRID_W)
    col_start = jnp.clip(cidx - NA_WIN_C // 2, 0, GRID_W - NA_WIN_C)
    col_ok = (cidx[None, :] >= col_start[:, None]) & (cidx[None, :] < col_start[:, None] + NA_WIN_C)
    dr = row_idx - r[:, None] + NA_WIN_R - 1
    dc = jnp.clip(cidx[None, :] - cidx[:, None], -(NA_WIN_C - 1), NA_WIN_C - 1) + NA_WIN_C - 1
    bias = rpb[:, dr[:, None, :, None], dc[None, :, None, :]]
    scale = dh ** -0.5
    s_loc = jnp.einsum('brchd,briwhd->bhrciw', qg, kb).astype(jnp.float32) * scale + bias[None].astype(jnp.float32)
    s_loc = jnp.where(col_ok[:, None, :], s_loc, NEG_INF).reshape(B, H, rows, GRID_W, wr * GRID_W)
    s_ctx = jnp.einsum('brchd,blhd->bhrcl', qg, kc).astype(jnp.float32) * scale
    p = jax.nn.softmax(jnp.concatenate([s_loc, s_ctx], -1), axis=-1).astype(v.dtype)
    p_loc = p[..., :wr * GRID_W].reshape(B, H, rows, GRID_W, wr, GRID_W)
    p_ctx = p[..., wr * GRID_W:]
    o = jnp.einsum('bhrciw,briwhd->brchd', p_loc, vb) + jnp.einsum('bhrcl,blhd->brchd', p_ctx, vc)
    return o.reshape(B, S, H, dh)


def gla_chunked(q, k, v, log_g, state0, include_diag, with_output):
    B, T, H, dk = q.shape
    dv = v.shape[-1]
    n = T // GLA_CHUNK
    f32 = jnp.float32
    chunk = lambda t: t.astype(f32).reshape(B, n, GLA_CHUNK, H, t.shape[-1])
    qc, kc, vc, gc = chunk(q), chunk(k), chunk(v), chunk(log_g)
    b = jnp.cumsum(gc, axis=2)
    b_last = b[:, :, -1:]
    u = jnp.einsum('bnshd,bnshe->bnhde', kc * jnp.exp(b_last - b), vc)
    decay = jnp.exp(b_last[:, :, 0])

    def step(state, inp):
        dec, uc = inp
        return dec[..., None] * state + uc, state

    s_final, s_in = lax.scan(step, state0.astype(f32), (jnp.moveaxis(decay, 1, 0), jnp.moveaxis(u, 1, 0)))
    if not with_output:
        return None, s_final
    s_in = jnp.moveaxis(s_in, 0, 1)
    q_t = qc * jnp.exp(b)
    k_t = kc * jnp.exp(-b)
    mask = jnp.tril(jnp.ones((GLA_CHUNK, GLA_CHUNK), bool), 0 if include_diag else -1)
    att = jnp.where(mask, jnp.einsum('bnthd,bnshd->bnhts', q_t, k_t), 0.0)
    o = jnp.einsum('bnhts,bnshe->bnthe', att, vc) + jnp.einsum('bnthd,bnhde->bnthe', q_t, s_in)
    return o.reshape(B, T, H, dv).astype(v.dtype), s_final


def na_gla_mixer(xin, hin, w_in, w_out, rpb, w_gf, b_gf, w_gb, b_gb, norm_g, with_ctx_out):
    B, S, _ = xin.shape
    nq_x, nk_x, nv_x, gq_x, gk_x, gv_x, gr_x, zf_x, zb_x = split_cols(xin @ w_in, EVEN_SPLITS)
    nq_h, nk_h, nv_h, gq_h, gk_h, gv_h, gr_h, zf_h, zb_h = split_cols(hin @ w_in, EVEN_SPLITS)
    kh_na, vh_na = heads(nk_h, NA_HEADS), heads(nv_h, NA_HEADS)
    na_x = neighbourhood_attention(heads(nq_x, NA_HEADS), heads(nk_x, NA_HEADS), heads(nv_x, NA_HEADS), kh_na, vh_na, rpb)

    def gla_inputs(gq, gk, gv, zf, zb):
        q = heads(gq, GLA_HEADS) * (GLA_DK ** -0.5)
        lf = heads(jax.nn.log_sigmoid((zf @ w_gf + b_gf).astype(jnp.float32)) / GLA_TAU, GLA_HEADS)
        lb = heads(jax.nn.log_sigmoid((zb @ w_gb + b_gb).astype(jnp.float32)) / GLA_TAU, GLA_HEADS)
        return q, heads(gk, GLA_HEADS), heads(gv, GLA_HEADS), lf, lb

    flip = lambda t: jnp.flip(t, axis=1)
    s0 = jnp.zeros((B, GLA_HEADS, GLA_DK, GLA_DV), jnp.float32)
    qh, kh, vh, lfh, lbh = gla_inputs(gq_h, gk_h, gv_h, zf_h, zb_h)
    oh_f, sh_f = gla_chunked(qh, kh, vh, lfh, s0, True, with_ctx_out)
    oh_b, sh_b = gla_chunked(flip(qh), flip(kh), flip(vh), flip(lbh), s0, False, with_ctx_out)
    qx, kx, vx, lfx, lbx = gla_inputs(gq_x, gk_x, gv_x, zf_x, zb_x)
    ox_f, _ = gla_chunked(qx, kx, vx, lfx, sh_f, True, True)
    ox_b, _ = gla_chunked(flip(qx), flip(kx), flip(vx), flip(lbx), sh_b, False, True)

    def gla_out(o_f, o_b_flipped, gr):
        o = rms_norm(o_f + flip(o_b_flipped), norm_g) * jax.nn.silu(heads(gr, GLA_HEADS))
        return o.reshape(o.shape[0], o.shape[1], GLA_HEADS * GLA_DV)

    y_x = jnp.concatenate([na_x.reshape(B, S, -1), gla_out(ox_f, ox_b, gr_x)], -1) @ w_out
    y_h = None
    if with_ctx_out:
        na_h = softmax_attend(heads(nq_h, NA_HEADS), kh_na, vh_na)
        y_h = jnp.concatenate([na_h.reshape(B, hin.shape[1], -1), gla_out(oh_f, oh_b, gr_h)], -1) @ w_out
    return y_x, y_h


def axial_rope_tables(n_tokens):
    t = jnp.arange(n_tokens)
    row = (t // GRID_W).astype(jnp.float32)
    col = (t % GRID_W).astype(jnp.float32)
    inv = ROPE_THETA ** (-jnp.arange(0, ROPE_AXIS_DIM, 2, dtype=jnp.float32) / ROPE_AXIS_DIM)
    ar, ac = row[:, None] * inv, col[:, None] * inv
    ang = jnp.concatenate([ar, ar, ac, ac], -1)
    return jnp.cos(ang), jnp.sin(ang)


def apply_axial_rope(u, cos, sin):
    shape = (1, u.shape[1]) + (1,) * (u.ndim - 3) + (u.shape[-1],)
    cos = cos.reshape(shape).astype(u.dtype)
    sin = sin.reshape(shape).astype(u.dtype)
    half = u.shape[-1] // 2

    def rot(w):
        m = w.shape[-1] // 2
        return jnp.concatenate([-w[..., m:], w[..., :m]], -1)

    return u * cos + jnp.concatenate([rot(u[..., :half]), rot(u[..., half:])], -1) * sin


def differential_softmax_attend(q, k, v, lam):
    s = jnp.einsum('bqhmd,bkhmd->bhmqk', q, k).astype(jnp.float32) * (DIFF_HEAD_DIM ** -0.5)
    p = jax.nn.softmax(s, axis=-1)
    pd = (p[:, :, 0] - lam * p[:, :, 1]).astype(v.dtype)
    return jnp.einsum('bhqk,bkhe->bqhe', pd, v)


def diff_attention_mixer(xin, hin, w_in, w_out, lq1, lk1, lq2, lk2, subln_g, lambda_init, with_ctx_out):
    B, S, _ = xin.shape
    qk_heads = lambda t: t.reshape(t.shape[0], t.shape[1], DIFF_HEADS, 2, DIFF_HEAD_DIM)
    v_heads = lambda t: t.reshape(t.shape[0], t.shape[1], DIFF_HEADS, 2 * DIFF_HEAD_DIM)
    qx, kx, vx = jnp.split(xin @ w_in, 3, axis=-1)
    cos, sin = axial_rope_tables(S)
    qx = apply_axial_rope(qk_heads(qx), cos, sin)
    kx = apply_axial_rope(qk_heads(kx), cos, sin)
    vx = v_heads(vx)
    if with_ctx_out:
        qh, kh, vh = jnp.split(hin @ w_in, 3, axis=-1)
        qh = qk_heads(qh)
    else:
        kh, vh = jnp.split(hin @ w_in[:, ODD_WIDTH:], 2, axis=-1)
    kh, vh = qk_heads(kh), v_heads(vh)
    f32 = jnp.float32
    lam = (jnp.exp(jnp.sum(lq1.astype(f32) * lk1.astype(f32))) - jnp.exp(jnp.sum(lq2.astype(f32) * lk2.astype(f32))) + lambda_init)
    k_all = jnp.concatenate([kx, kh], axis=1)
    v_all = jnp.concatenate([vx, vh], axis=1)
    nb = S // Q_BLOCK
    q_blocks = jnp.moveaxis(qx.reshape(B, nb, Q_BLOCK, DIFF_HEADS, 2, DIFF_HEAD_DIM), 1, 0)
    o_blocks = lax.map(lambda qb: differential_softmax_attend(qb, k_all, v_all, lam), q_blocks)
    ox = jnp.moveaxis(o_blocks, 0, 1).reshape(B, S, DIFF_HEADS, 2 * DIFF_HEAD_DIM)

    def finish(o):
        o = rms_norm(o, subln_g) * (1.0 - lambda_init)
        return o.reshape(o.shape[0], o.shape[1], ODD_WIDTH) @ w_out

    y_x = finish(ox)
    y_h = finish(differential_softmax_attend(qh, kh, vh, lam)) if with_ctx_out else None
    return y_x, y_h


def setup_inputs(seed: int = 0) -> dict:
    key = jax.random.key(seed)
    ks = iter(jax.random.split(key, 64))
    f32 = jnp.float32
    beta = (8.0 * DEPTH) ** -0.25

    def normal(shape, scale):
        return jax.random.normal(next(ks), shape, f32) * scale

    inp = {}
    inp['x'] = normal((BATCH, SEQ, D_MODEL), 1.0)
    inp['c'] = normal((BATCH, D_MODEL), 1.0)
    inp['ctx'] = normal((BATCH, CTX_LEN, D_MODEL), 1.0)
    inp['c_ctx'] = normal((D_MODEL,), 1.0)
    for i in range(DEPTH):
        p = 'l%d_' % i
        inp[p + 'w_ada'] = normal((D_MODEL, N_MOD * D_MODEL), ADA_INIT * D_MODEL ** -0.5)
        inp[p + 'b_ada'] = normal((N_MOD * D_MODEL,), 0.02)
        inp[p + 'ffn1_w_in'] = normal((D_MODEL, 2 * D_FF), D_MODEL ** -0.5)
        inp[p + 'ffn1_w_out'] = normal((D_FF, D_MODEL), beta * D_FF ** -0.5)
        if i % 2 == 0:
            inp[p + 'mix_w_in'] = normal((D_MODEL, EVEN_IN), D_MODEL ** -0.5)
            inp[p + 'mix_w_out'] = normal((EVEN_MIX, D_MODEL), beta * EVEN_MIX ** -0.5)
            inp[p + 'na_rpb'] = normal((NA_HEADS, 2 * NA_WIN_R - 1, 2 * NA_WIN_C - 1), 0.05)
            inp[p + 'gla_w_gate_f'] = normal((GLA_GATE_RANK, GLA_HEADS * GLA_DK), GLA_GATE_RANK ** -0.5)
            inp[p + 'gla_b_gate_f'] = normal((GLA_HEADS * GLA_DK,), 0.1)
            inp[p + 'gla_w_gate_b'] = normal((GLA_GATE_RANK, GLA_HEADS * GLA_DK), GLA_GATE_RANK ** -0.5)
            inp[p + 'gla_b_gate_b'] = normal((GLA_HEADS * GLA_DK,), 0.1)
            inp[p + 'gla_norm_g'] = 1.0 + normal((GLA_DV,), 0.02)
        else:
            inp[p + 'mix_w_in'] = normal((D_MODEL, ODD_IN), D_MODEL ** -0.5)
            inp[p + 'mix_w_out'] = normal((ODD_WIDTH, D_MODEL), beta * ODD_WIDTH ** -0.5)
            inp[p + 'lambda_q1'] = normal((DIFF_HEAD_DIM,), 0.1)
            inp[p + 'lambda_k1'] = normal((DIFF_HEAD_DIM,), 0.1)
            inp[p + 'lambda_q2'] = normal((DIFF_HEAD_DIM,), 0.1)
            inp[p + 'lambda_k2'] = normal((DIFF_HEAD_DIM,), 0.1)
            inp[p + 'subln_g'] = 1.0 + normal((2 * DIFF_HEAD_DIM,), 0.02)
        inp[p + 'ffn2_w_in'] = normal((D_MODEL, 2 * D_FF), D_MODEL ** -0.5)
        inp[p + 'ffn2_w_out'] = normal((D_FF, D_MODEL), beta * D_FF ** -0.5)
    return inp


def reference(x, c, ctx, c_ctx,
              l0_w_ada, l0_b_ada, l0_ffn1_w_in, l0_ffn1_w_out, l0_mix_w_in, l0_mix_w_out, l0_na_rpb,
              l0_gla_w_gate_f, l0_gla_b_gate_f, l0_gla_w_gate_b, l0_gla_b_gate_b, l0_gla_norm_g,
              l0_ffn2_w_in, l0_ffn2_w_out,
              l1_w_ada, l1_b_ada, l1_ffn1_w_in, l1_ffn1_w_out, l1_mix_w_in, l1_mix_w_out,
              l1_lambda_q1, l1_lambda_k1, l1_lambda_q2, l1_lambda_k2, l1_subln_g,
              l1_ffn2_w_in, l1_ffn2_w_out):
    common = (
        (l0_w_ada, l0_b_ada, l0_ffn1_w_in, l0_ffn1_w_out, l0_ffn2_w_in, l0_ffn2_w_out),
        (l1_w_ada, l1_b_ada, l1_ffn1_w_in, l1_ffn1_w_out, l1_ffn2_w_in, l1_ffn2_w_out),
    )
    mixers = (
        (l0_mix_w_in, l0_mix_w_out, l0_na_rpb, l0_gla_w_gate_f, l0_gla_b_gate_f, l0_gla_w_gate_b, l0_gla_b_gate_b, l0_gla_norm_g),
        (l1_mix_w_in, l1_mix_w_out, l1_lambda_q1, l1_lambda_k1, l1_lambda_q2, l1_lambda_k2, l1_subln_g),
    )
    alpha = (2.0 * DEPTH) ** 0.25
    h = ctx
    for i in range(DEPTH):
        w_ada, b_ada, f1_in, f1_out, f2_in, f2_out = common[i]
        last = i == DEPTH - 1
        mx = adaln_modulation(c, w_ada, b_ada)
        mh = adaln_modulation(c_ctx, w_ada, b_ada)
        x = layer_norm(alpha * x + mx[2] * (0.5 * swiglu(modulate(x, mx[0], mx[1]), f1_in, f1_out)))
        h = layer_norm(alpha * h + mh[2] * (0.5 * swiglu(modulate(h, mh[0], mh[1]), f1_in, f1_out)))
        xin = modulate(x, mx[3], mx[4])
        hin = modulate(h, mh[3], mh[4])
        if i % 2 == 0:
            y_x, y_h = na_gla_mixer(xin, hin, *mixers[i], with_ctx_out=not last)
        else:
            lambda_init = 0.8 - 0.6 * math.exp(-0.3 * i)
            y_x, y_h = diff_attention_mixer(xin, hin, *mixers[i], lambda_init=lambda_init, with_ctx_out=not last)
        x = layer_norm(alpha * x + mx[5] * y_x)
        x = layer_norm(alpha * x + mx[8] * (0.5 * swiglu(modulate(x, mx[6], mx[7]), f2_in, f2_out)))
        if not last:
            h = layer_norm(alpha * h + mh[5] * y_h)
            h = layer_norm(alpha * h + mh[8] * (0.5 * swiglu(modulate(h, mh[6], mh[7]), f2_in, f2_out)))
    return x
```

```python
import contextlib
import math
import numpy as np
import concourse.bass as bass
import concourse.mybir as mybir
from concourse.bass_utils import run_bass_kernel_spmd

F32 = mybir.dt.float32
BF16 = mybir.dt.bfloat16
AF = mybir.ActivationFunctionType
ALU = mybir.AluOpType
AX = mybir.AxisListType

D = 1024
SEQ = 2048
CTX = 256
T = SEQ + CTX
NT = T // 128
DFF = 2816
NHC = DFF // 128
ALPHA = 4.0 ** 0.25
LN_EPS = 1e-6
NEG = -1e30

COMPUTE = ("pe", "act", "dve", "pool")
REG = {}


def _isz(dt):
    return 2 if dt == BF16 else 4


class Op:
    __slots__ = ("eng", "fn", "tl", "seq", "waits", "signal", "clock", "count", "is_dma", "order")

    def __init__(self, eng, fn):
        self.eng = eng
        self.fn = fn
        self.tl = None
        self.seq = 0
        self.waits = []
        self.signal = False
        self.clock = None
        self.count = 0
        self.is_dma = False
        self.order = 0


def _region(ap):
    t = ap.tensor
    key, base, isz = REG[t.name]
    isz = _isz(ap.dtype)
    pat = ap.ap
    off = int(ap.offset)
    if key == "DR":
        ext = 1
        for st, cnt in pat:
            ext += (cnt - 1) * abs(st)
        return (t.name, 0, 1, off, off + ext)
    row = 1
    for s in list(t.shape)[1:]:
        row *= s
    p0 = off // row
    f0 = off % row
    st0, c0 = pat[0]
    pstep = max(1, abs(st0) // row) if c0 > 1 else 1
    p1 = p0 + (c0 - 1) * pstep + 1
    ext = 1
    for st, cnt in pat[1:]:
        ext += (cnt - 1) * abs(st)
    if key == "SB":
        return ("SB", p0, p1, base + f0 * isz, base + (f0 + ext) * isz)
    b0 = base + f0 * isz
    b1 = base + (f0 + ext) * isz
    return ("PS", 0, 128, (b0 // 2048) * 2048, ((b1 - 1) // 2048 + 1) * 2048)


class Kern:
    def __init__(self, nc, same_engine_sync=True, max_lanes=48):
        self.nc = nc
        self.ops = {e: [] for e in ("pe", "act", "dve", "pool", "sp")}
        self.known = {e: {} for e in self.ops}
        self.tl_len = {}
        self.hist = {}
        self.same_engine_sync = same_engine_sync
        self.lanes = []
        self.max_lanes = max_lanes
        self.lane_issued = {}
        self.lane_last = {}
        self.out_dmas = []
        self.nwaits = 0

    def _deps_for(self, op, reads, writes):
        deps = []
        for ap, is_w in [(a, False) for a in reads] + [(a, True) for a in writes]:
            name, p0, p1, f0, f1 = _region(ap)
            lst = self.hist.get(name, [])
            keep = []
            for ent in lst:
                (q0, q1, g0, g1), eop, ew = ent
                ov = not (q1 <= p0 or p1 <= q0 or g1 <= f0 or f1 <= g0)
                if name == "PS":
                    if ov and eop is not op and (is_w or ew or eop.tl != op.tl):
                        deps.append((eop, ew, is_w))
                    if ov and q0 >= p0 and q1 <= p1 and g0 >= f0 and g1 <= f1 and eop is not op:
                        continue
                    keep.append(ent)
                    continue
                if ov and (is_w or ew) and eop is not op:
                    deps.append((eop, ew, is_w))
                if is_w and ov and q0 >= p0 and q1 <= p1 and g0 >= f0 and g1 <= f1:
                    continue
                if (not is_w) and (not ew) and (not eop.is_dma) and eop.tl == op.tl \
                        and (q0, q1, g0, g1) == (p0, p1, f0, f1):
                    continue
                keep.append(ent)
            keep.append([(p0, p1, f0, f1), op, is_w])
            self.hist[name] = keep
        return deps

    def _wait_on(self, op, d, known):
        if known.get(d.tl, 0) >= d.seq:
            return
        d.signal = True
        op.waits.append(d)
        self.nwaits += 1
        for k, v in d.clock.items():
            if known.get(k, 0) < v:
                known[k] = v

    def _add(self, eng, fn, reads, writes, is_dma=False):
        op = Op(eng, fn)
        op.is_dma = is_dma
        self.norder = getattr(self, 'norder', 0) + 1
        op.order = self.norder
        known = self.known[eng]
        if is_dma:
            lane = None
            for L in self.lanes:
                if known.get(L, 0) >= self.lane_issued[L]:
                    lane = L
                    break
            if lane is None:
                if len(self.lanes) < self.max_lanes:
                    lane = "lane%d" % len(self.lanes)
                    self.lanes.append(lane)
                    self.lane_issued[lane] = 0
                else:
                    lane = min(self.lanes, key=lambda L: self.lane_last[L].order)
                    self._wait_on(op, self.lane_last[lane], known)
            op.tl = lane
        else:
            op.tl = eng
        deps = self._deps_for(op, reads, writes)
        if eng == "pe" and getattr(self, "_pe_pending", None) is not None:
            self._wait_on(op, self._pe_pending, known)
            self._pe_pending = None
        for d, d_w, me_w in deps:
            if d.tl == op.tl and not is_dma:
                if eng == "pe" and d_w and me_w:
                    continue
                if not self.same_engine_sync:
                    continue
            self._wait_on(op, d, known)
        n = self.tl_len.get(op.tl, 0) + 1
        self.tl_len[op.tl] = n
        op.seq = n
        if is_dma:
            self.lane_issued[op.tl] = n
            self.lane_last[op.tl] = op
            op.signal = True
        ck = dict(known)
        ck[op.tl] = n
        op.clock = ck
        self.ops[eng].append(op)
        return op

    @staticmethod
    def _pe_mode(stat):
        shp = list(stat.shape)
        k = shp[0]
        m = 1
        for s_ in shp[1:]:
            m *= s_
        r = lambda v: 32 if v <= 32 else (64 if v <= 64 else 128)
        return (r(k), r(m))

    def _pe_drain(self, stat):
        mode = self._pe_mode(stat)
        last = getattr(self, "_pe_last", None)
        self._pe_pending = None
        if last is not None and getattr(self, "_pe_lastmode", None) != mode:
            self._pe_pending = last
        self._pe_lastmode = mode

    def matmul(self, out, lhsT, rhs, start=True, stop=True, **kw):
        self._pe_drain(lhsT)
        op = self._add("pe", lambda e: e.matmul(out, lhsT, rhs, start=start, stop=stop, **kw),
                       [lhsT, rhs], [out])
        self._pe_last = op
        return op

    def transpose(self, out, in_, ident):
        self._pe_drain(in_)
        op = self._add("pe", lambda e: e.transpose(out, in_, ident), [in_, ident], [out])
        self._pe_last = op
        return op

    def act(self, out, in_, func, bias=None, scale=1.0, accum_out=None, eng="act"):
        reads = [in_]
        kw = {}
        if bias is not None:
            kw["bias"] = bias
            if not isinstance(bias, (int, float)):
                reads.append(bias)
        if not isinstance(scale, (int, float)):
            reads.append(scale)
        kw["scale"] = scale
        writes = [out]
        if accum_out is not None:
            kw["accum_out"] = accum_out
            writes.append(accum_out)
        return self._add(eng, lambda e: e.activation(out, in_, func, **kw), reads, writes)

    def tt(self, out, in0, in1, op, eng="dve"):
        return self._add(eng, lambda e: e.tensor_tensor(out, in0, in1, op), [in0, in1], [out])

    def ts(self, out, in0, s1, s2, op0, op1=None, eng="dve", accum_out=None):
        reads = [in0]
        for s in (s1, s2):
            if s is not None and not isinstance(s, (int, float)):
                reads.append(s)
        writes = [out]
        kw = {}
        if accum_out is not None:
            kw["accum_out"] = accum_out
            writes.append(accum_out)
        if op1 is None:
            return self._add(eng, lambda e: e.tensor_scalar(out, in0, s1, None, op0, **kw), reads, writes)
        return self._add(eng, lambda e: e.tensor_scalar(out, in0, s1, s2, op0, op1, **kw), reads, writes)

    def stt(self, out, in0, scalar, in1, op0, op1, eng="dve"):
        reads = [in0, in1]
        if not isinstance(scalar, (int, float)):
            reads.append(scalar)
        return self._add(eng, lambda e: e.scalar_tensor_tensor(out, in0, scalar, in1, op0, op1), reads, [out])

    def copy(self, out, in_, eng="dve"):
        if eng == "act":
            return self._add(eng, lambda e: e.copy(out, in_), [in_], [out])
        return self._add(eng, lambda e: e.tensor_copy(out, in_), [in_], [out])

    def memset(self, ap, val, eng="pool"):
        return self._add(eng, lambda e: e.memset(ap, val), [], [ap])

    def recip(self, out, in_):
        return self._add("dve", lambda e: e.reciprocal(out, in_), [in_], [out])

    def reduce(self, out, in_, op, axis=AX.X, eng="dve"):
        return self._add(eng, lambda e: e.tensor_reduce(out, in_, axis, op), [in_], [out])

    def bn_stats(self, out, in_):
        return self._add("dve", lambda e: e.bn_stats(out, in_), [in_], [out])

    def bn_aggr(self, out, in_):
        return self._add("dve", lambda e: e.bn_aggr(out, in_), [in_], [out])

    def dma(self, out, in_, q="sp", is_output=False, **kw):
        op = self._add(q, lambda e: e.dma_start(out=out, in_=in_, **kw), [in_], [out], is_dma=True)
        if is_output:
            self.out_dmas.append(op)
        return op

    def emit(self):
        nc = self.nc
        fin = Op("sp", None)
        fin.tl = "sp"
        for L in self.lanes:
            fin.waits.append(self.lane_last[L])
        self.ops["sp"].append(fin)
        for e in COMPUTE:
            c = 0
            for op in self.ops[e]:
                if op.is_dma:
                    continue
                if op.signal:
                    c += 1
                    op.count = c
        with contextlib.ExitStack() as es:
            sems = {}
            for e in COMPUTE:
                sems[e] = es.enter_context(nc.semaphore("s_" + e))
            for L in self.lanes:
                sems[L] = es.enter_context(nc.semaphore("s_" + L))
            block = es.enter_context(nc.Block())

            def run(eng_name):
                def body(e):
                    for op in self.ops[eng_name]:
                        for d in op.waits:
                            if d.is_dma:
                                e.wait_ge(sems[d.tl], 16 * d.seq)
                            else:
                                e.wait_ge(sems[d.tl], d.count)
                        if op.fn is None:
                            continue
                        ins = op.fn(e)
                        if op.is_dma:
                            ins.then_inc(sems[op.tl], 16)
                        elif op.signal:
                            ins.then_inc(sems[op.tl], 1)
                return body

            block.tensor(run("pe"))
            block.scalar(run("act"))
            block.vector(run("dve"))
            block.gpsimd(run("pool"))
            block.sync(run("sp"))


class Arena:
    def __init__(self, nc, lo=16640, hi=229376):
        self.nc = nc
        self.lo = lo
        self.hi = hi
        self.top = lo
        self.n = 0
        self.peak = lo

    def alloc(self, name, free_shape, dtype, parts=128):
        n = 1
        for s in free_shape:
            n *= s
        nbytes = n * _isz(dtype)
        off = (self.top + 63) // 64 * 64
        assert off + nbytes <= self.hi, "SBUF arena overflow: %s needs %d at %d" % (name, nbytes, off)
        self.n += 1
        h = self.nc.alloc_sbuf_tensor_at("%s_%d" % (name, self.n), [parts] + list(free_shape), dtype, offset=off)
        REG[h.name] = ("SB", off, _isz(dtype))
        self.top = off + nbytes
        self.peak = max(self.peak, self.top)
        return h

    def mark(self):
        return self.top

    def release(self, m):
        self.top = m


W_NAMES = {
    0: ["w_ada", "b_ada", "ffn1_w_in", "ffn1_w_out", "mix_w_in", "mix_w_out", "ffn2_w_in", "ffn2_w_out"],
    1: ["w_ada", "b_ada", "ffn1_w_in", "ffn1_w_out", "mix_w_in", "mix_w_out", "ffn2_w_in", "ffn2_w_out"],
}
W_SHAPES = {
    "w_ada": [D, 9 * D], "b_ada": [1, 9 * D], "ffn1_w_in": [D, 2 * DFF], "ffn1_w_out": [DFF, D],
    "ffn2_w_in": [D, 2 * DFF], "ffn2_w_out": [DFF, D], "mix_w_out": [D, D],
}


class Builder:
    def __init__(self, upto="all", debug=False):
        self.upto = upto
        self.debug = debug
        REG.clear()
        self.nc = nc = bass.Bass("TRN2", target_bir_lowering=False)
        self.K = Kern(nc)
        self.A = Arena(nc)
        self.din = {}
        self.cnt = 0

    def dram_in(self, name, shape, dtype=F32):
        h = self.nc.dram_tensor(name, list(shape), dtype, kind="ExternalInput")
        REG[h.name] = ("DR", 0, _isz(dtype))
        self.din[name] = h.ap()
        return h.ap()

    def dram_out(self, name, shape, dtype=F32):
        h = self.nc.dram_tensor(name, list(shape), dtype, kind="ExternalOutput")
        REG[h.name] = ("DR", 0, _isz(dtype))
        return h.ap()

    def dram_tmp(self, name, shape, dtype=F32):
        kind = "ExternalOutput" if self.debug else "Internal"
        h = self.nc.dram_tensor(name, list(shape), dtype, kind=kind)
        REG[h.name] = ("DR", 0, _isz(dtype))
        return h.ap()

    def W(self, l, nm):
        if (l, nm) not in self.Wd:
            if nm == "mix_w_in":
                shp = [D, 3104] if l == 0 else [D, 3072]
            else:
                shp = W_SHAPES[nm]
            self.Wd[(l, nm)] = self.dram_in("l%d_%s" % (l, nm), shp)
        return self.Wd[(l, nm)]

    def psum(self, name, shape, dtype):
        h = self.nc.alloc_psum_tensor(name, list(shape), dtype)
        REG[h.name] = ("PS", int(self.nc.lookup_mloc(h).bank) * 2048, _isz(dtype))
        return h

    def build(self):
        nc, K, A = self.nc, self.K, self.A
        x_d = self.dram_in("x", [SEQ, D])
        ctx_d = self.dram_in("ctx", [CTX, D])
        c_d = self.dram_in("c", [1, D])
        cctx_d = self.dram_in("c_ctx", [1, D])
        self.Wd = {}
        ident_d = self.dram_in("ident", [128, 128])
        self.out_d = self.dram_out("out", [SEQ, D])
        self.XS = self.dram_tmp("xs", [T, D])
        self.MODD = [self.dram_tmp("modd%d" % l, [2, 9 * D]) for l in (0, 1)]

        self.psA = self.psum("psA", [128, 8 * 512], F32)
        self.psT = self.psA[:, 6 * 512:8 * 512].bitcast(BF16)

        self.ident_bf = A.alloc("ident_bf", [128], BF16)
        self.ident_f = A.alloc("ident_f", [128], F32)
        K.dma(self.ident_f[:], ident_d[:, :])
        K.dma(self.ident_bf[:], ident_d[:, :], q="pool")
        self.modT = [A.alloc("modT%d" % l, [72, 2], F32) for l in (0, 1)]
        self.ones_f = A.alloc("ones_f", [128], F32, parts=1)
        K.memset(self.ones_f[:], 1.0)
        self.grow = [A.alloc("grow", [D], F32, parts=1)] * 2
        self.xinT = A.alloc("xinT", [8, T], BF16)
        self.base_mark = A.mark()

        import os
        if os.environ.get('NOMOD'):
            K.memset(self.modT[0][:], 1.0)
            K.memset(self.modT[1][:], 1.0)
        else:
            self.stage_mod(0)
            if not os.environ.get('MOD1'):
                self.stage_mod(1)
        if self.upto == "mod":
            return self.finish()
        self.stage_first_xin()
        if self.upto == "xin":
            dbg = self.dram_out("dbg_xinT", [128, 8 * T], BF16)
            import os
            if not os.environ.get('XIN_NODBG'):
                K.dma(dbg[:, :], self.xinT[:].rearrange("p a b -> p (a b)"))
            return self.finish()
        self.ffn(0, "ffn1", sub=0, tiles=list(range(NT)), first=True, nxt=(0, 1))
        if self.upto == "l0ffn1":
            return self.finish()
        self.mixer0()
        if self.upto in ("l0na", "l0gla", "l0mix"):
            return self.finish()
        self.ffn(0, "ffn2", sub=2, tiles=list(range(NT)), first=False, nxt=(1, 0))
        if self.upto == "l0":
            return self.finish()
        self.ffn(1, "ffn1", sub=0, tiles=list(range(NT)), first=False, nxt=(1, 1))
        if self.upto == "l1ffn1":
            return self.finish()
        self.mixer1()
        if self.upto in ("l1diff", "l1mix"):
            return self.finish()
        self.ffn(1, "ffn2", sub=2, tiles=list(range(16)), first=False, nxt="final")
        return self.finish()

    def finish(self):
        self.K.emit()
        return self.nc

    def bank(self, i, n=512):
        return self.psA[:, i * 512:i * 512 + n]

    def stage_mod(self, l):
        K, A = self.K, self.A
        m0 = A.mark()
        w_ada = self.W(l, "w_ada")
        b_ada = self.W(l, "b_ada")
        craw = A.alloc("craw", [8, 2], F32)
        sc = A.alloc("sc", [8, 2], F32)
        K.dma(craw[:, :, 0], self.din["c"].rearrange("o (kc p) -> p (o kc)", p=128), allow_slow_non_contiguous=True)
        K.dma(craw[:, :, 1], self.din["c_ctx"].rearrange("o (kc p) -> p (o kc)", p=128), allow_slow_non_contiguous=True)
        K.act(sc[:], craw[:], AF.Silu)
        modrow = A.alloc("modrow", [9 * D], F32, parts=2)
        brow = A.alloc("brow", [9 * D], F32, parts=2)
        K.dma(brow[0:1, :], b_ada[:, :])
        K.dma(brow[1:2, :], b_ada[:, :])
        wb = [A.alloc("wada%d" % i, [8, 1024], F32) for i in range(2)]
        import os
        for cg in range(int(os.environ.get('MOD_NCG', 9))):
            buf = wb[cg % 2]
            for hf in range(2):
                K.dma(buf[:, hf * 4:(hf + 1) * 4, :],
                      w_ada[hf * 512:(hf + 1) * 512, cg * 1024:(cg + 1) * 1024].rearrange("(kc p) n -> p kc n", p=128),
                      q=("sp", "act", "pool", "sp")[(2 * cg + hf) % 4])
            for half in range(2):
                ps = self.psA[0:2, (4 + half) * 512:(5 + half) * 512]
                for kc in range(0 if os.environ.get('MOD_NOMM') else 8):
                    K.matmul(ps, sc[:, kc, :], buf[:, kc, half * 512:(half + 1) * 512],
                             start=(kc == 0), stop=(kc == 7))
                sl = slice(cg * 1024 + half * 512, cg * 1024 + half * 512 + 512)
                K.tt(modrow[:, sl], ps, brow[:, sl], ALU.add)
        for idx in (1, 4, 7):
            K.ts(modrow[:, idx * D:(idx + 1) * D], modrow[:, idx * D:(idx + 1) * D], 1.0, None, ALU.add, eng="pool")
        for idx in (2, 5, 8):
            K.ts(modrow[:, idx * D:(idx + 1) * D], modrow[:, idx * D:(idx + 1) * D],
                 (1.0 if idx == 5 else 0.5) / ALPHA, None, ALU.mult, eng="pool")
        K.dma(self.MODD[l][:, :], modrow[:])
        pst = self.psA[:, 4 * 512:4 * 512 + 144]
        import os
        for j in range(int(os.environ.get('MOD_NT', 72))):
            K.transpose(pst[:, 2 * j:2 * j + 2], modrow[:, j * 128:(j + 1) * 128], self.ident_f[0:2, 0:2])
        K.copy(self.modT[l][:].rearrange("p j r -> p (j r)"), pst)
        A.release(m0)

    def load_gate_tiles(self, l, idx, tiles2):
        K = self.K
        for r in range(2):
            K.dma(self.grow[r][:], self.MODD[l][r:r + 1, idx * D:(idx + 1) * D])
            for half in range(2):
                ps = self.bank(4 + half)
                K.matmul(ps, self.ones_f[0:1, :], self.grow[r][0:1, half * 512:(half + 1) * 512])
                K.copy(tiles2[r][:, half * 512:(half + 1) * 512], ps, eng="act")

    def make_xinT(self, xb, t, l, sub):
        K = self.K
        r = 0 if t < 16 else 1
        self.cnt += 1
        pt = self.psT[:, (self.cnt % 2) * 1024:(self.cnt % 2) * 1024 + 1024]
        for kc in range(8):
            K.transpose(pt[:, kc * 128:(kc + 1) * 128], xb[:, kc * 128:(kc + 1) * 128], self.ident_bf[:])
        mt = self.modT[l]
        for kc in range(8):
            if kc < 8:
                K.act(self.xinT[:, kc, t * 128:(t + 1) * 128], pt[:, kc * 128:(kc + 1) * 128], AF.Identity,
                      bias=mt[:, (3 * sub) * 8 + kc, r:r + 1], scale=mt[:, (3 * sub + 1) * 8 + kc, r:r + 1])
            else:
                K.ts(self.xinT[:, kc, t * 128:(t + 1) * 128], pt[:, kc * 128:(kc + 1) * 128],
                     mt[:, (3 * sub + 1) * 8 + kc, r:r + 1], mt[:, (3 * sub) * 8 + kc, r:r + 1], ALU.mult, ALU.add)

    def src_rows(self, t, first):
        if first:
            if t < 16:
                return self.din["x"][t * 128:(t + 1) * 128, :]
            return self.din["ctx"][(t - 16) * 128:(t - 15) * 128, :]
        return self.XS[t * 128:(t + 1) * 128, :]

    def stage_first_xin(self):
        K, A = self.K, self.A
        m0 = A.mark()
        xf = [A.alloc("xf%d" % i, [D], F32) for i in range(2)]
        xb = [A.alloc("xb%d" % i, [D], BF16) for i in range(2)]
        import os
        for t in range(int(os.environ.get('XIN_TILES', NT))):
            K.dma(xf[t % 2][:], self.src_rows(t, True))
            K.copy(xb[t % 2][:], xf[t % 2][:], eng="pool")
            self.make_xinT(xb[t % 2], t, 0, 0)
        A.release(m0)

    def epi_alloc(self):
        A = self.A
        E = {}
        E["gate"] = [A.alloc("gate%d" % r, [D], F32) for r in range(2)]
        E["xr"] = [A.alloc("xr%d" % i, [D], F32) for i in range(4)]
        E["t1"] = [A.alloc("t1_%d" % i, [D], F32) for i in range(2)]
        E["xn"] = [A.alloc("xn%d" % i, [D], F32) for i in range(2)]
        E["xb"] = [A.alloc("xbb%d" % i, [D], BF16) for i in range(2)]
        E["st"] = [A.alloc("st%d" % i, [16], F32) for i in range(2)]
        E["i"] = 0
        return E

    def epi_front(self, E, Y, t, xr):
        K = self.K
        i = E["i"]
        E["i"] += 1
        r = 0 if t < 16 else 1
        t1 = E["t1"][i % 2]
        K.tt(t1[:], Y, E["gate"][r][:], ALU.mult)
        K.tt(xr[:], t1[:], xr[:], ALU.add, eng="pool")
        return i

    def epi_back(self, E, i, t, xr, partial, dst, nxt):
        K = self.K
        if partial:
            K.dma(dst, xr[:], q="sp")
            return
        st = E["st"][i % 2]
        K.bn_stats(st[:, 0:6], xr[:, 0:512])
        K.bn_stats(st[:, 6:12], xr[:, 512:1024])
        K.bn_aggr(st[:, 12:14], st[:, 0:12].rearrange("p (a b) -> p a b", b=6))
        K.ts(st[:, 14:15], st[:, 13:14], LN_EPS / (ALPHA * ALPHA), None, ALU.add)
        K.act(st[:, 14:15], st[:, 14:15], AF.Sqrt)
        K.recip(st[:, 14:15], st[:, 14:15])
        K.stt(st[:, 15:16], st[:, 12:13], -1.0, st[:, 14:15], ALU.mult, ALU.mult)
        xn = E["xn"][i % 2]
        K.ts(xn[:], xr[:], st[:, 14:15], st[:, 15:16], ALU.mult, ALU.add)
        K.dma(dst, xn[:], q="sp", is_output=(nxt == "final"))
        if nxt is None or nxt == "final":
            return
        xb = E["xb"][i % 2]
        K.act(xb[:], xr[:], AF.Identity, bias=st[:, 15:16], scale=st[:, 14:15])
        self.make_xinT(xb, t, nxt[0], nxt[1])

    def run_tiles(self, E, tiles, srcs, mm, partial, dsts, nxt):
        K = self.K
        LAG = 2
        NX = 4
        npre = 1
        for k in range(min(npre, len(tiles))):
            K.dma(E["xr"][k % NX][:], srcs[k], q="sp")
        pend = []
        for k, t in enumerate(tiles):
            Y = self.psA[:, (k % 2) * 1024:(k % 2) * 1024 + 1024]
            mm(t, Y)
            if k + npre < len(tiles):
                K.dma(E["xr"][(k + npre) % NX][:], srcs[k + npre], q="sp")
            if len(pend) >= LAG:
                pi_, pt_, pk_ = pend.pop(0)
                self.epi_back(E, pi_, pt_, E["xr"][pk_ % NX], partial, dsts[pk_], nxt)
            i = self.epi_front(E, Y, t, E["xr"][k % NX])
            pend.append((i, t, k))
        for (pi_, pt_, pk_) in pend:
            self.epi_back(E, pi_, pt_, E["xr"][pk_ % NX], partial, dsts[pk_], nxt)

    def outproj(self, l, catT, tiles, nxt, wo=None):
        K, A = self.K, self.A
        m0 = A.mark()
        w_out = self.W(l, "mix_w_out")
        if wo is None:
            wo = A.alloc("wo", [8, D], BF16)
            K.dma(wo[:], w_out.rearrange("(c q) f -> q c f", q=128), q="pool")
        E = self.epi_alloc()
        self.load_gate_tiles(l, 5, E["gate"])
        srcs = [self.src_rows(t, False) for t in tiles]
        dsts = [self.XS[t * 128:(t + 1) * 128, :] for t in tiles]

        def mm(t, Y):
            for half in range(2):
                for c in range(8):
                    K.matmul(Y[:, half * 512:(half + 1) * 512], catT[:, c, t * 128:(t + 1) * 128],
                             wo[:, c, half * 512:(half + 1) * 512], start=(c == 0), stop=(c == 7))
        self.run_tiles(E, tiles, srcs, mm, False, dsts, nxt)
        A.release(m0)

    def proj_fm(self, dst, wt, ncols, scale, groups, bank_i):
        K = self.K
        for gi, (t0, nt) in enumerate(groups):
            ps = self.psA[0:ncols, ((bank_i + gi) % 4) * 512:((bank_i + gi) % 4) * 512 + nt]
            for kc in range(8):
                K.matmul(ps, wt[:, kc, 0:ncols], self.xinT[:, kc, t0:t0 + nt], start=(kc == 0), stop=(kc == 7))
            if scale == 1.0:
                K.copy(dst[0:ncols, t0:t0 + nt], ps, eng="act")
            else:
                K.act(dst[0:ncols, t0:t0 + nt], ps, AF.Identity, scale=scale)

    def na_stage(self, catT):
        K, A = self.K, self.A
        m0 = A.mark()
        w_in = self.W(0, "mix_w_in")
        tab_d = self.dram_in("na_tab", [3, 128, 8 * 14 * 64])
        groups = [(g * 512, 512) for g in range(4)] + [(2048, 256)]
        wq = A.alloc("wq", [8, 128], BF16)
        wk = A.alloc("wk", [8, 128], BF16)
        wv = A.alloc("wv", [8, 128], BF16)
        qT = A.alloc("qT", [T], BF16)
        kT = A.alloc("kT", [T], BF16)
        Vaug = A.alloc("Vaug", [NT, 2, 128], BF16)
        btab = A.alloc("btab", [3, 2, 14 * 64], BF16)
        PT = [A.alloc("PT%d" % i, [140 * 64], BF16) for i in range(2)]
        PTc = [A.alloc("PTc%d" % i, [2, T], BF16) for i in range(2)]
        Rr = [A.alloc("Rr%d" % i, [512], F32) for i in range(2)]
        K.memset(Vaug[:], 1.0)
        qTz = [A.alloc("qTz%d" % i, [T], BF16) for i in range(2)]
        K.memset(qTz[0][64:128, :], 0.0)
        K.memset(qTz[1][0:64, :], 0.0)

        def rrange(i):
            if i <= 7:
                return 0, i + 4
            if i <= 23:
                return i - 3, i + 4
            return i - 3, 31
        tiles_geo = []
        off = 0
        for j in range(16):
            lo = min(rrange(2 * j)[0], rrange(2 * j + 1)[0])
            hi = max(rrange(2 * j)[1], rrange(2 * j + 1)[1])
            var = 0 if j < 4 else (1 if j < 12 else 2)
            tiles_geo.append((lo, hi, off, var))
            off += (hi - lo + 1) * 64
        assert off == 140 * 64
        hcount = 0
        bk = 0
        for hp in range(4):
            K.dma(wq[:], w_in[:, hp * 128:(hp + 1) * 128].rearrange("(kc q) n -> q kc n", q=128), q="pool")
            K.dma(wk[:], w_in[:, 512 + hp * 128:512 + (hp + 1) * 128].rearrange("(kc q) n -> q kc n", q=128), q="pool")
            K.dma(wv[:], w_in[:, 1024 + hp * 128:1024 + (hp + 1) * 128].rearrange("(kc q) n -> q kc n", q=128), q="pool")
            for v in range(3):
                K.dma(btab[:, v, :, :], tab_d[v, :, hp * 2 * 896:(hp + 1) * 2 * 896].rearrange("p (h f) -> p h f", h=2), q="pool")
            self.proj_fm(qT, wq, 128, 0.125, groups, 0)
            self.proj_fm(kT, wk, 128, 1.0, groups, 1)
            K.copy(qTz[0][0:64, :], qT[0:64, :], eng="pool")
            K.copy(qTz[1][64:128, :], qT[64:128, :], eng="pool")
            for g4 in range(0, NT, 4):
                nt4 = min(4, NT - g4)
                ps = self.bank(bk % 4)
                bk += 1
                for tt_ in range(nt4):
                    t = g4 + tt_
                    for kc in range(8):
                        K.matmul(ps[:, tt_ * 128:(tt_ + 1) * 128], self.xinT[:, kc, t * 128:(t + 1) * 128], wv[:, kc, :],
                                 start=(kc == 0), stop=(kc == 7))
                K.copy(Vaug[:, g4:g4 + nt4, :, 0:64],
                       ps[:, 0:nt4 * 128].rearrange("p (a h d) -> p a h d", h=2, d=64), eng="act")
            for h2 in range(2):
                hb = h2 * 64
                hcount += 1
                pt = PT[hcount % 2]
                ptc = PTc[hcount % 2]
                for j in range(16):
                    lo, hi, poff, var = tiles_geo[j]
                    r = lo
                    while r <= hi:
                        nr = min(8, hi - r + 1)
                        ps = self.bank(bk % 4, nr * 64)
                        bk += 1
                        K.matmul(ps, kT[:, j * 128:(j + 1) * 128], qTz[h2][:, r * 64:(r + nr) * 64],
                                 start=True, stop=False)
                        d0 = (r - 2 * j + 3) + 3
                        K.matmul(ps, self.ident_bf[:], btab[:, var, h2, d0 * 64:(d0 + nr) * 64], start=False, stop=True)
                        K.act(pt[:, poff + (r - lo) * 64:poff + (r - lo + nr) * 64], ps, AF.Exp)
                        r += nr
                for ct in range(2):
                    for (t0, nt) in groups:
                        ps = self.bank(bk % 4, nt)
                        bk += 1
                        K.matmul(ps, kT[:, 2048 + ct * 128:2048 + (ct + 1) * 128], qTz[h2][:, t0:t0 + nt])
                        K.act(ptc[:, ct, t0:t0 + nt], ps, AF.Exp)
                for qb, (t0, nt) in enumerate(groups):
                    O = self.bank(4 + (bk % 2), nt)
                    bk += 1
                    K.matmul(O, Vaug[:, 16, h2, :], ptc[:, 0, t0:t0 + nt], start=True, stop=False)
                    last_is_ctx = (qb == 4)
                    K.matmul(O, Vaug[:, 17, h2, :], ptc[:, 1, t0:t0 + nt], start=False, stop=last_is_ctx)
                    if qb < 4:
                        R0, R1 = 8 * qb, 8 * qb + 7
                        js = [j for j in range(16) if not (tiles_geo[j][1] < R0 or tiles_geo[j][0] > R1)]
                        for jj, j in enumerate(js):
                            lo, hi, poff, var = tiles_geo[j]
                            a, b = max(lo, R0), min(hi, R1)
                            K.matmul(O[:, (a - R0) * 64:(b - R0 + 1) * 64], Vaug[:, j, h2, :],
                                     pt[:, poff + (a - lo) * 64:poff + (b - lo + 1) * 64],
                                     start=False, stop=(jj == len(js) - 1))
                    rr = Rr[bk % 2]
                    K.recip(rr[64:128, 0:nt], O[64:128, :])
                    K.tt(catT[hb:hb + 64, hp, t0:t0 + nt], O[0:64, :], rr[64:128, 0:nt], ALU.mult)
        A.release(m0)


    def gla_stage(self, catT):
        K, A = self.K, self.A
        m0 = A.mark()
        w_in = self.W(0, "mix_w_in")
        wg_d = [self.dram_in("gla_wgf", [17, 256]), self.dram_in("gla_wgb", [17, 256])]
        ng_d = self.dram_in("gla_ng", [1, 128])
        tri_d = self.dram_in("gla_tri", [4, 128, 128])
        msk_d = self.dram_in("gla_msk", [2, 128, 128])
        groups = [(g * 512, 512) for g in range(4)] + [(2048, 256)]
        bkc = [0]

        def nb(n=512):
            bkc[0] += 1
            return self.bank(bkc[0] % 6, n)

        tri = A.alloc("tri", [4, 128], F32)
        K.dma(tri[:], tri_d.rearrange("a s t -> s a t"))
        msk4 = A.alloc("msk4", [2, 2, 128], BF16)
        for dirn in range(2):
            for hh in range(2):
                K.dma(msk4[:, dirn, hh, :], msk_d[dirn, :, :], q="pool")
        ngrow = A.alloc("ngrow", [128], F32, parts=1)
        K.dma(ngrow[:], ng_d[:, :])
        ng2 = A.alloc("ng2", [2, 128], F32)
        psn = nb(128)
        K.matmul(psn, self.ones_f[0:1, :], ngrow[0:1, :])
        for hh in range(2):
            K.copy(ng2[:, hh, :], psn, eng="act")
        wg = [A.alloc("wg%d" % i, [256], F32) for i in range(2)]
        for i in range(2):
            K.memset(wg[i][:], 0.0)
            K.dma(wg[i][0:16, :], wg_d[i][0:16, :])
            K.dma(wg[i][32:33, :], wg_d[i][16:17, :])
        zT = [A.alloc("zT%d" % i, [T], F32) for i in range(2)]
        mz = A.mark()
        wz = A.alloc("wz", [8, 32], BF16)
        K.dma(wz[:], w_in[:, 3072:3104].rearrange("(kc q) n -> q kc n", q=128), q="pool")
        for i in range(2):
            K.memset(zT[i][:], 0.0)
            K.memset(zT[i][32:33, :], 1.0)
            for (t0, nt) in groups:
                ps = nb(nt)[0:16, :]
                for kc in range(8):
                    K.matmul(ps, wz[:, kc, i * 16:(i + 1) * 16], self.xinT[:, kc, t0:t0 + nt], start=(kc == 0), stop=(kc == 7))
                K.copy(zT[i][0:16, t0:t0 + nt], ps, eng="act")
        A.release(mz)

        for p in range(2):
            mp = A.mark()
            vg = A.alloc("vg", [NT, 256], BF16)
            sg = A.alloc("sg", [NT, 256], BF16)
            qtz = A.alloc("qtz", [2, 2, T], BF16)
            ktT = A.alloc("ktT", [2, T], BF16)
            Sin = A.alloc("Sin", [2, NT, 128], BF16)
            S = A.alloc("S", [128], F32)
            K.memset(qtz[64:128, :, 0, :], 0.0)
            K.memset(qtz[0:64, :, 1, :], 0.0, eng="dve")
            m1 = A.mark()
            qgT = A.alloc("qgT", [T], BF16)
            kgT = A.alloc("kgT", [T], BF16)
            kg = A.alloc("kg", [NT, 128], BF16)
            mw = A.mark()
            wgq = A.alloc("wgq", [8, 128], BF16)
            wgk = A.alloc("wgk", [8, 128], BF16)
            wgv = A.alloc("wgv", [8, 256], BF16)
            wgr = A.alloc("wgr", [8, 256], BF16)
            for (wt, c0, n) in ((wgq, 1536 + p * 128, 128), (wgk, 1792 + p * 128, 128),
                                (wgv, 2048 + p * 256, 256), (wgr, 2560 + p * 256, 256)):
                K.dma(wt[:], w_in[:, c0:c0 + n].rearrange("(kc q) n -> q kc n", q=128), q="pool")
            self.proj_fm(qgT, wgq, 128, 0.125, groups, 0)
            self.proj_fm(kgT, wgk, 128, 1.0, groups, 2)
            for t in range(NT):
                tsl = slice(t * 128, (t + 1) * 128)
                pk = nb(128)
                for kc in range(8):
                    K.matmul(pk, self.xinT[:, kc, tsl], wgk[:, kc, :], start=(kc == 0), stop=(kc == 7))
                K.copy(kg[:, t, :], pk, eng="act")
                pv = nb(256)
                for kc in range(8):
                    K.matmul(pv, self.xinT[:, kc, tsl], wgv[:, kc, :], start=(kc == 0), stop=(kc == 7))
                K.copy(vg[:, t, :], pv, eng="dve")
                pr = nb(256)
                for kc in range(8):
                    K.matmul(pr, self.xinT[:, kc, tsl], wgr[:, kc, :], start=(kc == 0), stop=(kc == 7))
                K.act(sg[:, t, :], pr, AF.Silu)
            A.release(mw)
            tmps = {}
            for nm, dt_ in (("e1", F32), ("sp", F32), ("Eb", F32), ("Enb", F32), ("Ed", F32), ("ku", BF16)):
                tmps[nm] = [A.alloc("%s%d" % (nm, i), [128], dt_) for i in range(2)]
            its = []
            for dirn in range(2):
                order = ([16, 17] + list(range(16))) if dirn == 0 else ([17, 16] + list(range(15, -1, -1)))
                its += [(dirn, n, i_ == 0) for i_, n in enumerate(order)]

            def gA(k):
                dirn, n, _ = its[k]
                triA = tri[:, 0 if dirn == 0 else 1, :]
                triB = tri[:, 2 if dirn == 0 else 3, :]
                tsl = slice(n * 128, (n + 1) * 128)
                e1, sp, Eb, Enb, Ed, ku = [tmps[kk][k % 2] for kk in ("e1", "sp", "Eb", "Enb", "Ed", "ku")]
                pz = nb(128)
                K.matmul(pz, zT[dirn][:, tsl], wg[dirn][:, p * 128:(p + 1) * 128])
                K.act(e1[:], pz, AF.Exp, scale=-1.0)
                K.act(sp[:], e1[:], AF.Ln, bias=1.0)
                pb = nb(128)
                K.matmul(pb, sp[:], triA)
                pd = nb(128)
                K.matmul(pd, triB, sp[:])
                K.act(Eb[:], pb, AF.Exp)
                K.act(Enb[:], pb, AF.Exp, scale=-1.0)
                K.act(Ed[:], pd, AF.Exp)
                for hh in range(2):
                    hb = hh * 64
                    K.tt(qtz[hb:hb + 64, dirn, hh, tsl], qgT[hb:hb + 64, tsl], Eb[hb:hb + 64, :], ALU.mult)
                K.tt(ktT[:, dirn, tsl], kgT[:, tsl], Enb[:], ALU.mult)
                K.tt(ku[:], kg[:, n, :], Ed[:], ALU.mult, eng="pool")

            def gB(k):
                dirn, n, first = its[k]
                lastcol = 127 if dirn == 0 else 0
                Eb, ku = tmps["Eb"][k % 2], tmps["ku"][k % 2]
                if first:
                    K.memset(S[:], 0.0)
                K.copy(Sin[:, dirn, n, :], S[:], eng="pool")
                pu = nb(256)
                K.matmul(pu, ku[:], vg[:, n, :])
                for hh in range(2):
                    hb = hh * 64
                    K.stt(S[hb:hb + 64, :], S[hb:hb + 64, :], Eb[hb:hb + 64, lastcol:lastcol + 1],
                          pu[hb:hb + 64, hh * 128:(hh + 1) * 128], ALU.mult, ALU.add)
            gA(0)
            for k in range(len(its)):
                if k + 1 < len(its):
                    gA(k + 1)
                gB(k)
            A.release(m1)
            attT = [A.alloc("attT%d" % i, [2, 2, 128], BF16) for i in range(2)]
            ss = [A.alloc("ss%d" % i, [2], F32) for i in range(2)]
            junk = A.alloc("junk", [128], BF16)
            tmpo = [A.alloc("tmpo%d" % i, [256], F32) for i in range(2)]
            og = [A.alloc("og%d" % i, [256], BF16) for i in range(2)]
            Obank = {}

            def hA(n):
                tsl = slice(n * 128, (n + 1) * 128)
                at = attT[n % 2]
                for dirn in range(2):
                    pa = nb(256)
                    for hh in range(2):
                        K.matmul(pa[:, hh * 128:(hh + 1) * 128], ktT[:, dirn, tsl], qtz[:, dirn, hh, tsl],
                                 start=(hh == 0), stop=(hh == 1))
                    K.tt(at[:, dirn, :, :], pa.rearrange("p (h t) -> p h t", h=2), msk4[:, dirn, :, :], ALU.mult)
                O = nb(256)
                Obank[n] = O
                first = True
                for hh in range(2):
                    for dirn in range(2):
                        K.matmul(O[:, hh * 128:(hh + 1) * 128], at[:, dirn, hh, :], vg[:, n, hh * 128:(hh + 1) * 128],
                                 start=first, stop=False)
                        first = False
                        K.matmul(O[:, hh * 128:(hh + 1) * 128], qtz[:, dirn, hh, tsl], Sin[:, dirn, n, :],
                                 start=False, stop=(dirn == 1))

            def hB(n):
                tsl = slice(n * 128, (n + 1) * 128)
                O = Obank.pop(n)
                s_ = ss[n % 2]
                for hh in range(2):
                    K.act(junk[:], O[:, hh * 128:(hh + 1) * 128], AF.Square, accum_out=s_[:, hh:hh + 1])
                K.ts(s_[:], s_[:], 1.0 / 128.0, LN_EPS, ALU.mult, ALU.add)
                K.act(s_[:], s_[:], AF.Sqrt)
                K.recip(s_[:], s_[:])
                tm = tmpo[n % 2]
                for hh in range(2):
                    K.stt(tm[:, hh * 128:(hh + 1) * 128], O[:, hh * 128:(hh + 1) * 128], s_[:, hh:hh + 1],
                          sg[:, n, hh * 128:(hh + 1) * 128], ALU.mult, ALU.mult)
                o_ = og[n % 2]
                K.tt(o_[:], tm[:], ng2[:].rearrange("p a b -> p (a b)"), ALU.mult, eng="pool")
                self.cnt += 1
                pt = self.psT[:, (self.cnt % 2) * 1024:(self.cnt % 2) * 1024 + 256]
                for hh in range(2):
                    K.transpose(pt[:, hh * 128:(hh + 1) * 128], o_[:, hh * 128:(hh + 1) * 128], self.ident_bf[:])
                K.copy(catT[:, 4 + 2 * p:6 + 2 * p, tsl], pt.rearrange("p (h t) -> p h t", h=2), eng="act")
            hA(0)
            for n in range(NT):
                if n + 1 < NT:
                    hA(n + 1)
                hB(n)
            A.release(mp)
        A.release(m0)


    def diff_stage(self, catT):
        K, A = self.K, self.A
        m0 = A.mark()
        LI = 0.8 - 0.6 * math.exp(-0.3)
        w_in = self.W(1, "mix_w_in")
        cos_d = self.dram_in("rope_cos", [128, SEQ])
        sin_d = self.dram_in("rope_sin", [128, SEQ])
        perm_d = self.dram_in("rope_perm", [128, 128])
        lam_d = self.dram_in("lamv", [1, 256])
        sub_d = self.dram_in("subln_g", [1, 128])
        xgroups = [(g * 512, 512) for g in range(4)]
        cosT = A.alloc("cosT", [SEQ], F32)
        sinT = A.alloc("sinT", [SEQ], F32)
        K.dma(cosT[:], cos_d[:, :])
        K.dma(sinT[:], sin_d[:, :])
        perm = A.alloc("perm", [128], BF16)
        K.dma(perm[:], perm_d[:, :], q="pool")
        lamv = A.alloc("lamv", [256], F32, parts=1)
        K.dma(lamv[:], lam_d[:, :])
        prod = A.alloc("prod", [128], F32, parts=1)
        K.tt(prod[0:1, 0:64], lamv[0:1, 0:64], lamv[0:1, 64:128], ALU.mult)
        K.tt(prod[0:1, 64:128], lamv[0:1, 128:192], lamv[0:1, 192:256], ALU.mult)
        s12 = A.alloc("s12", [4], F32, parts=1)
        K.reduce(s12[0:1, 0:2], prod[0:1, :].rearrange("p (a b) -> p a b", a=2), ALU.add)
        K.act(s12[0:1, 0:2], s12[0:1, 0:2], AF.Exp)
        K.tt(s12[0:1, 2:3], s12[0:1, 0:1], s12[0:1, 1:2], ALU.subtract)
        K.ts(s12[0:1, 3:4], s12[0:1, 2:3], -1.0, -LI, ALU.mult, ALU.add)
        nlam = A.alloc("nlam", [1], F32)
        psl = self.bank(5, 2)
        K.matmul(psl[:, 0:1], self.ones_f[0:1, :], s12[0:1, 3:4])
        K.copy(nlam[:], psl[:, 0:1], eng="act")
        subrow = A.alloc("subrow", [128], F32, parts=1)
        K.dma(subrow[:], sub_d[:, :])
        gsub = A.alloc("gsub", [128], F32)
        psg = self.bank(4, 128)
        K.matmul(psg, self.ones_f[0:1, :], subrow[0:1, :])
        K.act(gsub[:], psg, AF.Identity, scale=(1.0 - LI))

        wq = [A.alloc("dwq%d" % i, [8, 128], BF16) for i in range(2)]
        wk = [A.alloc("dwk%d" % i, [8, 128], BF16) for i in range(2)]
        wv = [A.alloc("dwv%d" % i, [8, 128], BF16) for i in range(2)]
        qTz = [[A.alloc("dqTz%d_%d" % (bb, i), [SEQ], BF16) for i in range(2)] for bb in range(2)]
        kT = [A.alloc("dkT%d" % bb, [T], BF16) for bb in range(2)]
        Vaug = [A.alloc("dVaug%d" % bb, [NT, 128], BF16) for bb in range(2)]
        for bb in range(2):
            K.memset(qTz[bb][0][64:128, :], 0.0)
            K.memset(qTz[bb][1][0:64, :], 0.0)
        ub = [A.alloc("ub%d" % i, [512], BF16) for i in range(2)]
        t1 = [A.alloc("rt1_%d" % i, [512], F32) for i in range(2)]
        t2 = [A.alloc("rt2_%d" % i, [512], F32) for i in range(2)]
        Pt = [A.alloc("Pt%d" % i, [512], BF16) for i in range(4)]
        o1 = A.alloc("do1", [512], F32)
        o2s = [A.alloc("do2_%d" % i, [512], F32) for i in range(2)]
        sqs = [A.alloc("dsq%d" % i, [512], BF16) for i in range(2)]
        deferred = []
        fi = 0
        Rt = A.alloc("dRt", [512], F32)
        rs = A.alloc("drs", [512], F32)
        onesb = A.alloc("onesb", [128], BF16)
        K.memset(onesb[:], 1.0)
        gcol = A.alloc("gcol", [1], F32)
        K.dma(gcol[:], sub_d.rearrange("o p -> p o"), allow_slow_non_contiguous=True)
        gvec = A.alloc("gvec", [1], F32)
        K.ts(gvec[:], gcol[:], (1.0 - LI), None, ALU.mult)
        bk = 0
        pi = 0
        rc = [0]

        def proj_steps(h):
            b = h % 2
            steps = []

            def wload():
                for (wt, c0) in ((wq[b], h * 128), (wk[b], 1024 + h * 128), (wv[b], 2048 + h * 128)):
                    K.dma(wt[:], w_in[:, c0:c0 + 128].rearrange("(kc q) n -> q kc n", q=128), q="pool")
            steps.append(wload)
            for which in range(2):
                for (t0, nt) in xgroups:
                    def rope(which=which, t0=t0, nt=nt):
                        wt = (wq, wk)[which][b]
                        rc[0] += 1
                        ri = rc[0]
                        ps = self.bank(7, nt)
                        for kc in range(8):
                            K.matmul(ps, wt[:, kc, :], self.xinT[:, kc, t0:t0 + nt], start=(kc == 0), stop=(kc == 7))
                        u_b, a1, a2 = ub[ri % 2], t1[ri % 2], t2[ri % 2]
                        K.copy(u_b[:], ps, eng="dve")
                        K.tt(a1[:], ps, cosT[:, t0:t0 + nt], ALU.mult)
                        pr = self.bank(7, nt)
                        K.matmul(pr, perm[:], u_b[:])
                        K.tt(a2[:], pr, sinT[:, t0:t0 + nt], ALU.mult)
                        if which == 0:
                            for m in range(2):
                                K.tt(qTz[b][m][m * 64:(m + 1) * 64, t0:t0 + nt], a1[m * 64:(m + 1) * 64, :],
                                     a2[m * 64:(m + 1) * 64, :], ALU.add, eng="pool")
                        else:
                            K.tt(kT[b][:, t0:t0 + nt], a1[:], a2[:], ALU.add, eng="pool")
                    steps.append(rope)

            def ctxk():
                ps = self.bank(7, 256)
                for kc in range(8):
                    K.matmul(ps, wk[b][:, kc, :], self.xinT[:, kc, 2048:2304], start=(kc == 0), stop=(kc == 7))
                K.copy(kT[b][:, 2048:2304], ps, eng="dve")
            steps.append(ctxk)
            for g4 in range(0, NT, 4):
                def vproj(g4=g4):
                    nt4 = min(4, NT - g4)
                    ps = self.bank(7)
                    for tt_ in range(nt4):
                        t = g4 + tt_
                        for kc in range(8):
                            K.matmul(ps[:, tt_ * 128:(tt_ + 1) * 128], self.xinT[:, kc, t * 128:(t + 1) * 128], wv[b][:, kc, :],
                                     start=(kc == 0), stop=(kc == 7))
                    K.copy(Vaug[b][:, g4:g4 + nt4, 0:128], ps[:, 0:nt4 * 128].rearrange("p (a e) -> p a e", e=128), eng="dve")
                steps.append(vproj)
            return steps

        for st_ in proj_steps(0):
            st_()
        for h in range(8):
            b = h % 2
            nsteps = proj_steps(h + 1) if h < 7 else []
            for qg in range(4):
                q0 = qg * 512
                for m in range(2):
                    bk += 1
                    O = self.bank(2 + 2 * (bk % 2))
                    Dn = self.bank(3 + 2 * (bk % 2))

                    def S_(kt):
                        ps_ = self.bank((0, 1, 6)[kt % 3])
                        K.matmul(ps_, kT[b][:, kt * 128:(kt + 1) * 128], qTz[b][m][:, q0:q0 + 512])
                        return ps_
                    sq_ = [S_(0), S_(1)]
                    for kt in range(NT):
                        cur = sq_.pop(0)
                        pi += 1
                        P = Pt[pi % 4]
                        K.act(P[:], cur, AF.Exp, scale=0.125)
                        if kt + 2 < NT:
                            sq_.append(S_(kt + 2))
                        K.matmul(O, Vaug[b][:, kt, 0:128], P[:], start=(kt == 0), stop=(kt == NT - 1))
                        K.matmul(Dn, onesb[:], P[:], start=(kt == 0), stop=(kt == NT - 1))
                        if kt == 8 and deferred:
                            deferred.pop(0)()
                        if kt in (3, 13) and nsteps:
                            nsteps.pop(0)()
                    K.recip(Rt[:], Dn)
                    if m == 0:
                        K.tt(o1[:], O, Rt[:], ALU.mult)
                    else:
                        o2 = o2s[fi % 2]
                        sq = sqs[fi % 2]
                        fi += 1
                        K.ts(Rt[:], Rt[:], nlam[:, 0:1], None, ALU.mult)
                        K.tt(o2[:], O, Rt[:], ALU.mult)
                        K.tt(o2[:], o2[:], o1[:], ALU.add, eng="pool")
                        K.tt(sq[:], o2[:], o2[:], ALU.mult, eng="pool")

                        def fin(o2=o2, sq=sq, Dn=Dn, h=h, q0=q0):
                            K.matmul(Dn, onesb[:], sq[:])
                            K.ts(rs[:], Dn, 1.0 / 128.0, LN_EPS, ALU.mult, ALU.add)
                            K.act(rs[:], rs[:], AF.Ln)
                            K.act(rs[:], rs[:], AF.Exp, scale=-0.5)
                            K.stt(catT[:, h, q0:q0 + 512], o2[:], gvec[:, 0:1], rs[:], ALU.mult, ALU.mult)
                        deferred.append(fin)
            while nsteps:
                nsteps.pop(0)()
        while deferred:
            deferred.pop(0)()
        A.release(m0)

    def mixer1(self):
        K, A = self.K, self.A
        m0 = A.mark()
        catT = A.alloc("catT1", [8, T], BF16)
        wo1 = A.alloc("wo1", [8, D], BF16)
        K.dma(wo1[:], self.W(1, "mix_w_out").rearrange("(c q) f -> q c f", q=128), q="pool")
        self.diff_stage(catT)
        if self.upto == "l1diff":
            dbg = self.dram_out("dbg_catT", [128, 8 * T], BF16)
            K.dma(dbg[:, :], catT[:].rearrange("p a b -> p (a b)"))
            return
        self.outproj(1, catT, list(range(16)), nxt=(1, 2), wo=wo1)
        A.release(m0)

    def mixer0(self):
        K, A = self.K, self.A
        m0 = A.mark()
        catT = A.alloc("catT", [8, T], BF16)
        self.na_stage(catT)
        if self.upto == "l0na":
            dbg = self.dram_out("dbg_catT", [128, 8 * T], BF16)
            K.dma(dbg[:, :], catT[:].rearrange("p a b -> p (a b)"))
            return
        self.gla_stage(catT)
        if self.upto == "l0gla":
            dbg = self.dram_out("dbg_catT", [128, 8 * T], BF16)
            K.dma(dbg[:, :], catT[:].rearrange("p a b -> p (a b)"))
            return
        self.outproj(0, catT, list(range(NT)), nxt=(0, 2))
        A.release(m0)

    def ffn(self, l, which, sub, tiles, first, nxt):
        K, A = self.K, self.A
        m0 = A.mark()
        w_in = self.W(l, which + "_w_in")
        w_out = self.W(l, which + "_w_out")
        HT = A.alloc("HT", [11, T], BF16)
        woutb = [A.alloc("wout%d" % p, [11, D], BF16) for p in range(2)]
        wa = [A.alloc("wa%d" % i, [8, 256], BF16) for i in range(2)]
        wg = [A.alloc("wg%d" % i, [8, 256], BF16) for i in range(2)]
        sa = [A.alloc("sa%d" % i, [512], F32) for i in range(2)]
        E = self.epi_alloc()
        self.load_gate_tiles(l, 3 * sub + 2, E["gate"])
        groups = []
        if 0 in tiles:
            groups += [(g * 512, 512) for g in range(4)]
        if 16 in tiles:
            groups += [(2048, 256)]
        ci = 0
        gi = 0
        allg = []
        for p in range(2):
            c0 = 11 * p
            allg += [(p, c0 + 2 * i, 2) for i in range(5)] + [(p, c0 + 10, 1)]
        issued = set()

        def issue(gidx):
            if gidx in issued or gidx >= len(allg):
                return
            issued.add(gidx)
            _, cs_, n_ = allg[gidx]
            K.dma(wa[gidx % 2][:, :, 0:n_ * 128], w_in[:, cs_ * 128:(cs_ + n_) * 128].rearrange("(kc q) n -> q kc n", q=128), q="pool")
            K.dma(wg[gidx % 2][:, :, 0:n_ * 128], w_in[:, DFF + cs_ * 128:DFF + (cs_ + n_) * 128].rearrange("(kc q) n -> q kc n", q=128), q="pool")
        issue(0)
        K.dma(woutb[0][:], w_out[0:1408, :].rearrange("(hc q) f -> q hc f", q=128), q="pool")
        for p in range(2):
            c0 = 11 * p
            for gl in range(6):
                gidx = 6 * p + gl
                _, cs, n = allg[gidx]
                issue(gidx)
                issue(gidx + 1) if gl < 5 else None
                a_t = wa[gidx % 2]
                g_t = wg[gidx % 2]
                if p == 0 and gl == 2:
                    K.dma(woutb[1][:], w_out[1408:2816, :].rearrange("(hc q) f -> q hc f", q=128), q="pool")
                if gl == 5:
                    issue(gidx + 1)
                for j in range(n):
                    hl = cs + j - c0
                    for (t0, nt) in groups:
                        pa = self.bank(2 * (gi % 2), nt)
                        pg = self.bank(2 * (gi % 2) + 1, nt)
                        s_t = sa[gi % 2]
                        gi += 1
                        for kc in range(8):
                            K.matmul(pa, a_t[:, kc, j * 128:(j + 1) * 128], self.xinT[:, kc, t0:t0 + nt],
                                     start=(kc == 0), stop=(kc == 7))
                        for kc in range(8):
                            K.matmul(pg, g_t[:, kc, j * 128:(j + 1) * 128], self.xinT[:, kc, t0:t0 + nt],
                                     start=(kc == 0), stop=(kc == 7))
                        K.act(s_t[:, 0:nt], pa, AF.Silu)
                        K.tt(HT[:, hl, t0:t0 + nt], s_t[:, 0:nt], pg, ALU.mult)
            partial = (p == 0)
            srcs = [self.src_rows(t, first and p == 0) for t in tiles]
            if partial or nxt != "final":
                dsts = [self.XS[t * 128:(t + 1) * 128, :] for t in tiles]
            else:
                dsts = [self.out_d[t * 128:(t + 1) * 128, :] for t in tiles]

            def mm(t, Y, p=p):
                for half in range(2):
                    for hl in range(11):
                        K.matmul(Y[:, half * 512:(half + 1) * 512], HT[:, hl, t * 128:(t + 1) * 128],
                                 woutb[p][:, hl, half * 512:(half + 1) * 512], start=(hl == 0), stop=(hl == 10))
            self.run_tiles(E, tiles, srcs, mm, partial, dsts, nxt)
        A.release(m0)


_CACHE = {}


def _get_program(upto="all", debug=False):
    key = (upto, debug)
    if key not in _CACHE:
        b = Builder(upto=upto, debug=debug)
        nc = b.build()
        _CACHE[key] = (nc, b)
    return _CACHE[key]


def _na_table(rpb):
    rpb = np.asarray(rpb, dtype=np.float32)
    kc = np.arange(64)[:, None]
    qc = np.arange(64)[None, :]
    col_start = np.clip(qc - 8, 0, 48)
    col_ok = (kc >= col_start) & (kc < col_start + 16)
    dc = np.clip(kc - qc, -15, 15) + 15
    tab = np.full((3, 2, 64, 8, 14, 64), NEG, dtype=np.float32)
    for v in range(3):
        for ki in range(2):
            for idx in range(14):
                drr = idx - 3
                dr = (10 if ki == 0 else 11) - drr
                if dr < 0 or dr > 14:
                    continue
                if v in (1, 2) and drr < ki:
                    continue
                if v in (0, 1) and drr > ki + 7:
                    continue
                g = rpb[:, dr][:, dc]
                g = np.where(col_ok[None], g, np.float32(NEG))
                tab[v, ki, :, :, idx, :] = g.transpose(1, 0, 2)
    return np.ascontiguousarray(tab.reshape(3, 128, 8 * 14 * 64))


ALL_INPUT_NAMES = (
    "x", "c", "ctx", "c_ctx",
    "l0_w_ada", "l0_b_ada", "l0_ffn1_w_in", "l0_ffn1_w_out", "l0_mix_w_in", "l0_mix_w_out", "l0_na_rpb",
    "l0_gla_w_gate_f", "l0_gla_b_gate_f", "l0_gla_w_gate_b", "l0_gla_b_gate_b", "l0_gla_norm_g",
    "l0_ffn2_w_in", "l0_ffn2_w_out",
    "l1_w_ada", "l1_b_ada", "l1_ffn1_w_in", "l1_ffn1_w_out", "l1_mix_w_in", "l1_mix_w_out",
    "l1_lambda_q1", "l1_lambda_k1", "l1_lambda_q2", "l1_lambda_k2", "l1_subln_g",
    "l1_ffn2_w_in", "l1_ffn2_w_out",
)


def _in_maps(inputs, n_cores=8, names=None):
    ident = np.eye(128, dtype=np.float32)
    maps = []
    shared = {}
    for l in (0, 1):
        for nm in W_NAMES[l]:
            a = np.ascontiguousarray(inputs["l%d_%s" % (l, nm)], dtype=np.float32)
            if nm == "b_ada":
                a = a.reshape(1, -1)
            shared["l%d_%s" % (l, nm)] = a
    shared["ident"] = ident
    if names is None or "gla_tri" in names:
        si = np.arange(128)[:, None]
        ti = np.arange(128)[None, :]
        g = np.float32(-1.0 / 16.0)
        shared["gla_tri"] = np.stack([(si <= ti) * g, (si >= ti) * g, (si > ti) * g, (si < ti) * g]).astype(np.float32)
        shared["gla_msk"] = np.stack([(si <= ti), (si > ti)]).astype(np.float32)
        shared["gla_wgf"] = np.concatenate([np.asarray(inputs["l0_gla_w_gate_f"], np.float32),
                                            np.asarray(inputs["l0_gla_b_gate_f"], np.float32).reshape(1, -1)], 0)
        shared["gla_wgb"] = np.concatenate([np.asarray(inputs["l0_gla_w_gate_b"], np.float32),
                                            np.asarray(inputs["l0_gla_b_gate_b"], np.float32).reshape(1, -1)], 0)
        shared["gla_ng"] = np.asarray(inputs["l0_gla_norm_g"], np.float32).reshape(1, 128)
    if names is None or "rope_cos" in names:
        t = np.arange(SEQ)
        row = (t // 64).astype(np.float32)
        col = (t % 64).astype(np.float32)
        inv = (np.float32(10000.0) ** (-np.arange(0, 32, 2, dtype=np.float32) / np.float32(32))).astype(np.float32)
        ar = row[:, None] * inv
        ac = col[:, None] * inv
        ang = np.concatenate([ar, ar, ac, ac], -1).astype(np.float32)
        sgn = np.where((np.arange(64) % 32) < 16, -1.0, 1.0).astype(np.float32)
        cosT = np.cos(ang).astype(np.float32).T
        sinT = (np.sin(ang).astype(np.float32) * sgn[None, :]).T
        shared["rope_cos"] = np.ascontiguousarray(np.concatenate([cosT, cosT], 0))
        shared["rope_sin"] = np.ascontiguousarray(np.concatenate([sinT, sinT], 0))
        pm = np.zeros((128, 128), np.float32)
        for dst in range(128):
            src = dst + 16 if (dst % 32) < 16 else dst - 16
            pm[src, dst] = 1.0
        shared["rope_perm"] = pm
        shared["lamv"] = np.concatenate([np.asarray(inputs["l1_lambda_" + k], np.float32).reshape(-1)
                                         for k in ("q1", "k1", "q2", "k2")]).reshape(1, 256)
        shared["subln_g"] = np.asarray(inputs["l1_subln_g"], np.float32).reshape(1, 128)
    if names is None or "na_tab" in names:
        shared["na_tab"] = _na_table(inputs["l0_na_rpb"])
    shared["c_ctx"] = np.ascontiguousarray(inputs["c_ctx"], dtype=np.float32).reshape(1, D)
    for b in range(n_cores):
        m = dict(shared)
        m["x"] = np.ascontiguousarray(inputs["x"][b], dtype=np.float32)
        m["ctx"] = np.ascontiguousarray(inputs["ctx"][b], dtype=np.float32)
        m["c"] = np.ascontiguousarray(inputs["c"][b], dtype=np.float32).reshape(1, D)
        if names is not None:
            m = {k: v for k, v in m.items() if k in names}
        maps.append(m)
    return maps


def kernel(**inputs):
    nc, b = _get_program()
    maps = _in_maps(inputs, names=set(b.din.keys()))
    res = run_bass_kernel_spmd(nc, maps, core_ids=list(range(8)))
    out = np.stack([np.asarray(r["out"], dtype=np.float32) for r in res.results], axis=0)
    return out
```

```python
import contextlib
import math
import numpy as np
import concourse.bass as bass
import concourse.mybir as mybir
from concourse.bass_utils import run_bass_kernel_spmd

F32 = mybir.dt.float32
BF16 = mybir.dt.bfloat16
AF = mybir.ActivationFunctionType
ALU = mybir.AluOpType
AX = mybir.AxisListType

D = 1024
SEQ = 2048
CTX = 256
T = SEQ + CTX
NT = T // 128
DFF = 2816
NHC = DFF // 128
ALPHA = 4.0 ** 0.25
LN_EPS = 1e-6
NEG = -1e30

COMPUTE = ("pe", "act", "dve", "pool")
REG = {}


def _isz(dt):
    return 2 if dt == BF16 else 4


class Op:
    __slots__ = ("eng", "fn", "tl", "seq", "waits", "signal", "clock", "count", "is_dma", "order")

    def __init__(self, eng, fn):
        self.eng = eng
        self.fn = fn
        self.tl = None
        self.seq = 0
        self.waits = []
        self.signal = False
        self.clock = None
        self.count = 0
        self.is_dma = False
        self.order = 0


def _region(ap):
    t = ap.tensor
    key, base, isz = REG[t.name]
    isz = _isz(ap.dtype)
    pat = ap.ap
    off = int(ap.offset)
    if key == "DR":
        ext = 1
        for st, cnt in pat:
            ext += (cnt - 1) * abs(st)
        return (t.name, 0, 1, off, off + ext)
    row = 1
    for s in list(t.shape)[1:]:
        row *= s
    p0 = off // row
    f0 = off % row
    st0, c0 = pat[0]
    pstep = max(1, abs(st0) // row) if c0 > 1 else 1
    p1 = p0 + (c0 - 1) * pstep + 1
    ext = 1
    for st, cnt in pat[1:]:
        ext += (cnt - 1) * abs(st)
    if key == "SB":
        return ("SB", p0, p1, base + f0 * isz, base + (f0 + ext) * isz)
    b0 = base + f0 * isz
    b1 = base + (f0 + ext) * isz
    return ("PS", 0, 128, (b0 // 2048) * 2048, ((b1 - 1) // 2048 + 1) * 2048)


class Kern:
    def __init__(self, nc, same_engine_sync=True, max_lanes=48):
        self.nc = nc
        self.ops = {e: [] for e in ("pe", "act", "dve", "pool", "sp")}
        self.known = {e: {} for e in self.ops}
        self.tl_len = {}
        self.hist = {}
        self.same_engine_sync = same_engine_sync
        self.lanes = []
        self.max_lanes = max_lanes
        self.lane_issued = {}
        self.lane_last = {}
        self.out_dmas = []
        self.nwaits = 0

    def _deps_for(self, op, reads, writes):
        deps = []
        for ap, is_w in [(a, False) for a in reads] + [(a, True) for a in writes]:
            name, p0, p1, f0, f1 = _region(ap)
            lst = self.hist.get(name, [])
            keep = []
            for ent in lst:
                (q0, q1, g0, g1), eop, ew = ent
                ov = not (q1 <= p0 or p1 <= q0 or g1 <= f0 or f1 <= g0)
                if name == "PS":
                    if ov and eop is not op and (is_w or ew or eop.tl != op.tl):
                        deps.append((eop, ew, is_w))
                    if ov and q0 >= p0 and q1 <= p1 and g0 >= f0 and g1 <= f1 and eop is not op:
                        continue
                    keep.append(ent)
                    continue
                if ov and (is_w or ew) and eop is not op:
                    deps.append((eop, ew, is_w))
                if is_w and ov and q0 >= p0 and q1 <= p1 and g0 >= f0 and g1 <= f1:
                    continue
                if (not is_w) and (not ew) and (not eop.is_dma) and eop.tl == op.tl \
                        and (q0, q1, g0, g1) == (p0, p1, f0, f1):
                    continue
                keep.append(ent)
            keep.append([(p0, p1, f0, f1), op, is_w])
            self.hist[name] = keep
        return deps

    def _wait_on(self, op, d, known):
        if known.get(d.tl, 0) >= d.seq:
            return
        d.signal = True
        op.waits.append(d)
        self.nwaits += 1
        for k, v in d.clock.items():
            if known.get(k, 0) < v:
                known[k] = v

    def _add(self, eng, fn, reads, writes, is_dma=False):
        op = Op(eng, fn)
        op.is_dma = is_dma
        self.norder = getattr(self, 'norder', 0) + 1
        op.order = self.norder
        known = self.known[eng]
        if is_dma:
            lane = None
            for L in self.lanes:
                if known.get(L, 0) >= self.lane_issued[L]:
                    lane = L
                    break
            if lane is None:
                if len(self.lanes) < self.max_lanes:
                    lane = "lane%d" % len(self.lanes)
                    self.lanes.append(lane)
                    self.lane_issued[lane] = 0
                else:
                    lane = min(self.lanes, key=lambda L: self.lane_last[L].order)
                    self._wait_on(op, self.lane_last[lane], known)
            op.tl = lane
        else:
            op.tl = eng
        deps = self._deps_for(op, reads, writes)
        if eng == "pe" and getattr(self, "_pe_pending", None) is not None:
            self._wait_on(op, self._pe_pending, known)
            self._pe_pending = None
        for d, d_w, me_w in deps:
            if d.tl == op.tl and not is_dma:
                if eng == "pe" and d_w and me_w:
                    continue
                if not self.same_engine_sync:
                    continue
            self._wait_on(op, d, known)
        n = self.tl_len.get(op.tl, 0) + 1
        self.tl_len[op.tl] = n
        op.seq = n
        if is_dma:
            self.lane_issued[op.tl] = n
            self.lane_last[op.tl] = op
            op.signal = True
        ck = dict(known)
        ck[op.tl] = n
        op.clock = ck
        self.ops[eng].append(op)
        return op

    @staticmethod
    def _pe_mode(stat):
        shp = list(stat.shape)
        k = shp[0]
        m = 1
        for s_ in shp[1:]:
            m *= s_
        r = lambda v: 32 if v <= 32 else (64 if v <= 64 else 128)
        return (r(k), r(m))

    def _pe_drain(self, stat):
        mode = self._pe_mode(stat)
        last = getattr(self, "_pe_last", None)
        self._pe_pending = None
        if last is not None and getattr(self, "_pe_lastmode", None) != mode:
            self._pe_pending = last
        self._pe_lastmode = mode

    def matmul(self, out, lhsT, rhs, start=True, stop=True, **kw):
        self._pe_drain(lhsT)
        op = self._add("pe", lambda e: e.matmul(out, lhsT, rhs, start=start, stop=stop, **kw),
                       [lhsT, rhs], [out])
        self._pe_last = op
        return op

    def transpose(self, out, in_, ident):
        self._pe_drain(in_)
        op = self._add("pe", lambda e: e.transpose(out, in_, ident), [in_, ident], [out])
        self._pe_last = op
        return op

    def act(self, out, in_, func, bias=None, scale=1.0, accum_out=None, eng="act"):
        reads = [in_]
        kw = {}
        if bias is not None:
            kw["bias"] = bias
            if not isinstance(bias, (int, float)):
                reads.append(bias)
        if not isinstance(scale, (int, float)):
            reads.append(scale)
        kw["scale"] = scale
        writes = [out]
        if accum_out is not None:
            kw["accum_out"] = accum_out
            writes.append(accum_out)
        return self._add(eng, lambda e: e.activation(out, in_, func, **kw), reads, writes)

    def tt(self, out, in0, in1, op, eng="dve"):
        return self._add(eng, lambda e: e.tensor_tensor(out, in0, in1, op), [in0, in1], [out])

    def ts(self, out, in0, s1, s2, op0, op1=None, eng="dve", accum_out=None):
        reads = [in0]
        for s in (s1, s2):
            if s is not None and not isinstance(s, (int, float)):
                reads.append(s)
        writes = [out]
        kw = {}
        if accum_out is not None:
            kw["accum_out"] = accum_out
            writes.append(accum_out)
        if op1 is None:
            return self._add(eng, lambda e: e.tensor_scalar(out, in0, s1, None, op0, **kw), reads, writes)
        return self._add(eng, lambda e: e.tensor_scalar(out, in0, s1, s2, op0, op1, **kw), reads, writes)

    def stt(self, out, in0, scalar, in1, op0, op1, eng="dve"):
        reads = [in0, in1]
        if not isinstance(scalar, (int, float)):
            reads.append(scalar)
        return self._add(eng, lambda e: e.scalar_tensor_tensor(out, in0, scalar, in1, op0, op1), reads, [out])

    def copy(self, out, in_, eng="dve"):
        if eng == "act":
            return self._add(eng, lambda e: e.copy(out, in_), [in_], [out])
        return self._add(eng, lambda e: e.tensor_copy(out, in_), [in_], [out])

    def memset(self, ap, val, eng="pool"):
        return self._add(eng, lambda e: e.memset(ap, val), [], [ap])

    def recip(self, out, in_):
        return self._add("dve", lambda e: e.reciprocal(out, in_), [in_], [out])

    def reduce(self, out, in_, op, axis=AX.X, eng="dve"):
        return self._add(eng, lambda e: e.tensor_reduce(out, in_, axis, op), [in_], [out])

    def bn_stats(self, out, in_):
        return self._add("dve", lambda e: e.bn_stats(out, in_), [in_], [out])

    def bn_aggr(self, out, in_):
        return self._add("dve", lambda e: e.bn_aggr(out, in_), [in_], [out])

    def dma(self, out, in_, q="sp", is_output=False, **kw):
        op = self._add(q, lambda e: e.dma_start(out=out, in_=in_, **kw), [in_], [out], is_dma=True)
        if is_output:
            self.out_dmas.append(op)
        return op

    def emit(self):
        nc = self.nc
        fin = Op("sp", None)
        fin.tl = "sp"
        for L in self.lanes:
            fin.waits.append(self.lane_last[L])
        self.ops["sp"].append(fin)
        for e in COMPUTE:
            c = 0
            for op in self.ops[e]:
                if op.is_dma:
                    continue
                if op.signal:
                    c += 1
                    op.count = c
        with contextlib.ExitStack() as es:
            sems = {}
            for e in COMPUTE:
                sems[e] = es.enter_context(nc.semaphore("s_" + e))
            for L in self.lanes:
                sems[L] = es.enter_context(nc.semaphore("s_" + L))
            block = es.enter_context(nc.Block())

            def run(eng_name):
                def body(e):
                    for op in self.ops[eng_name]:
                        for d in op.waits:
                            if d.is_dma:
                                e.wait_ge(sems[d.tl], 16 * d.seq)
                            else:
                                e.wait_ge(sems[d.tl], d.count)
                        if op.fn is None:
                            continue
                        ins = op.fn(e)
                        if op.is_dma:
                            ins.then_inc(sems[op.tl], 16)
                        elif op.signal:
                            ins.then_inc(sems[op.tl], 1)
                return body

            block.tensor(run("pe"))
            block.scalar(run("act"))
            block.vector(run("dve"))
            block.gpsimd(run("pool"))
            block.sync(run("sp"))


class Arena:
    def __init__(self, nc, lo=16640, hi=229376):
        self.nc = nc
        self.lo = lo
        self.hi = hi
        self.top = lo
        self.n = 0
        self.peak = lo

    def alloc(self, name, free_shape, dtype, parts=128):
        n = 1
        for s in free_shape:
            n *= s
        nbytes = n * _isz(dtype)
        off = (self.top + 63) // 64 * 64
        assert off + nbytes <= self.hi, "SBUF arena overflow: %s needs %d at %d" % (name, nbytes, off)
        self.n += 1
        h = self.nc.alloc_sbuf_tensor_at("%s_%d" % (name, self.n), [parts] + list(free_shape), dtype, offset=off)
        REG[h.name] = ("SB", off, _isz(dtype))
        self.top = off + nbytes
        self.peak = max(self.peak, self.top)
        return h

    def mark(self):
        return self.top

    def release(self, m):
        self.top = m


W_NAMES = {
    0: ["w_ada", "b_ada", "ffn1_w_in", "ffn1_w_out", "mix_w_in", "mix_w_out", "ffn2_w_in", "ffn2_w_out"],
    1: ["w_ada", "b_ada", "ffn1_w_in", "ffn1_w_out", "mix_w_in", "mix_w_out", "ffn2_w_in", "ffn2_w_out"],
}
W_SHAPES = {
    "w_ada": [D, 9 * D], "b_ada": [1, 9 * D], "ffn1_w_in": [D, 2 * DFF], "ffn1_w_out": [DFF, D],
    "ffn2_w_in": [D, 2 * DFF], "ffn2_w_out": [DFF, D], "mix_w_out": [D, D],
}


class Builder:
    def __init__(self, upto="all", debug=False):
        self.upto = upto
        self.debug = debug
        REG.clear()
        self.nc = nc = bass.Bass("TRN2", target_bir_lowering=False)
        self.K = Kern(nc)
        self.A = Arena(nc)
        self.din = {}
        self.cnt = 0

    def dram_in(self, name, shape, dtype=F32):
        h = self.nc.dram_tensor(name, list(shape), dtype, kind="ExternalInput")
        REG[h.name] = ("DR", 0, _isz(dtype))
        self.din[name] = h.ap()
        return h.ap()

    def dram_out(self, name, shape, dtype=F32):
        h = self.nc.dram_tensor(name, list(shape), dtype, kind="ExternalOutput")
        REG[h.name] = ("DR", 0, _isz(dtype))
        return h.ap()

    def dram_tmp(self, name, shape, dtype=F32):
        kind = "ExternalOutput" if self.debug else "Internal"
        h = self.nc.dram_tensor(name, list(shape), dtype, kind=kind)
        REG[h.name] = ("DR", 0, _isz(dtype))
        return h.ap()

    def W(self, l, nm):
        if (l, nm) not in self.Wd:
            if nm == "mix_w_in":
                shp = [D, 3104] if l == 0 else [D, 3072]
            else:
                shp = W_SHAPES[nm]
            self.Wd[(l, nm)] = self.dram_in("l%d_%s" % (l, nm), shp)
        return self.Wd[(l, nm)]

    def psum(self, name, shape, dtype):
        h = self.nc.alloc_psum_tensor(name, list(shape), dtype)
        REG[h.name] = ("PS", int(self.nc.lookup_mloc(h).bank) * 2048, _isz(dtype))
        return h

    def build(self):
        nc, K, A = self.nc, self.K, self.A
        x_d = self.dram_in("x", [SEQ, D])
        ctx_d = self.dram_in("ctx", [CTX, D])
        c_d = self.dram_in("c", [1, D])
        cctx_d = self.dram_in("c_ctx", [1, D])
        self.Wd = {}
        ident_d = self.dram_in("ident", [128, 128])
        self.out_d = self.dram_out("out", [SEQ, D])
        self.XS = self.dram_tmp("xs", [T, D])
        self.MODD = [self.dram_tmp("modd%d" % l, [2, 9 * D]) for l in (0, 1)]

        self.psA = self.psum("psA", [128, 8 * 512], F32)
        self.psT = self.psA[:, 6 * 512:8 * 512].bitcast(BF16)

        self.ident_bf = A.alloc("ident_bf", [128], BF16)
        self.ident_f = A.alloc("ident_f", [128], F32)
        K.dma(self.ident_f[:], ident_d[:, :])
        K.dma(self.ident_bf[:], ident_d[:, :], q="pool")
        self.modT = [A.alloc("modT%d" % l, [72, 2], F32) for l in (0, 1)]
        self.ones_f = A.alloc("ones_f", [128], F32, parts=1)
        K.memset(self.ones_f[:], 1.0)
        self.grow = [A.alloc("grow", [D], F32, parts=1)] * 2
        self.xinT = A.alloc("xinT", [8, T], BF16)
        self.base_mark = A.mark()

        self.stage_mod(0)
        self.stage_mod(1)
        if self.upto == "mod":
            return self.finish()
        self.stage_first_xin()
        if self.upto == "xin":
            dbg = self.dram_out("dbg_xinT", [128, 8 * T], BF16)
            K.dma(dbg[:, :], self.xinT[:].rearrange("p a b -> p (a b)"))
            return self.finish()
        self.ffn(0, "ffn1", sub=0, tiles=list(range(NT)), first=True, nxt=(0, 1))
        if self.upto == "l0ffn1":
            return self.finish()
        self.mixer0()
        if self.upto in ("l0na", "l0gla", "l0mix"):
            return self.finish()
        self.ffn(0, "ffn2", sub=2, tiles=list(range(NT)), first=False, nxt=(1, 0))
        if self.upto == "l0":
            return self.finish()
        self.ffn(1, "ffn1", sub=0, tiles=list(range(NT)), first=False, nxt=(1, 1))
        if self.upto == "l1ffn1":
            return self.finish()
        self.mixer1()
        if self.upto in ("l1diff", "l1mix"):
            return self.finish()
        self.ffn(1, "ffn2", sub=2, tiles=list(range(16)), first=False, nxt="final")
        return self.finish()

    def finish(self):
        self.K.emit()
        return self.nc

    def bank(self, i, n=512):
        return self.psA[:, i * 512:i * 512 + n]

    def stage_mod(self, l):
        K, A = self.K, self.A
        m0 = A.mark()
        w_ada = self.W(l, "w_ada")
        b_ada = self.W(l, "b_ada")
        craw = A.alloc("craw", [8, 2], F32)
        sc = A.alloc("sc", [8, 2], F32)
        K.dma(craw[:, :, 0], self.din["c"].rearrange("o (kc p) -> p (o kc)", p=128), allow_slow_non_contiguous=True)
        K.dma(craw[:, :, 1], self.din["c_ctx"].rearrange("o (kc p) -> p (o kc)", p=128), allow_slow_non_contiguous=True)
        K.act(sc[:], craw[:], AF.Silu)
        modrow = A.alloc("modrow", [9 * D], F32, parts=2)
        brow = A.alloc("brow", [9 * D], F32, parts=2)
        K.dma(brow[0:1, :], b_ada[:, :])
        K.dma(brow[1:2, :], b_ada[:, :])
        wb = [A.alloc("wada%d" % i, [8, 1024], F32) for i in range(2)]
        for cg in range(9):
            buf = wb[cg % 2]
            for hf in range(2):
                K.dma(buf[:, hf * 4:(hf + 1) * 4, :],
                      w_ada[hf * 512:(hf + 1) * 512, cg * 1024:(cg + 1) * 1024].rearrange("(kc p) n -> p kc n", p=128),
                      q=("sp", "act", "pool", "sp")[(2 * cg + hf) % 4])
            for half in range(2):
                ps = self.psA[0:2, (4 + half) * 512:(5 + half) * 512]
                for kc in range(8):
                    K.matmul(ps, sc[:, kc, :], buf[:, kc, half * 512:(half + 1) * 512],
                             start=(kc == 0), stop=(kc == 7))
                sl = slice(cg * 1024 + half * 512, cg * 1024 + half * 512 + 512)
                K.tt(modrow[:, sl], ps, brow[:, sl], ALU.add)
        for idx in (1, 4, 7):
            K.ts(modrow[:, idx * D:(idx + 1) * D], modrow[:, idx * D:(idx + 1) * D], 1.0, None, ALU.add, eng="pool")
        for idx in (2, 5, 8):
            K.ts(modrow[:, idx * D:(idx + 1) * D], modrow[:, idx * D:(idx + 1) * D],
                 (1.0 if idx == 5 else 0.5) / ALPHA, None, ALU.mult, eng="pool")
        K.dma(self.MODD[l][:, :], modrow[:])
        pst = self.psA[:, 4 * 512:4 * 512 + 144]
        for j in range(72):
            K.transpose(pst[:, 2 * j:2 * j + 2], modrow[:, j * 128:(j + 1) * 128], self.ident_f[0:2, 0:2])
        K.copy(self.modT[l][:].rearrange("p j r -> p (j r)"), pst)
        A.release(m0)

    def load_gate_tiles(self, l, idx, tiles2):
        K = self.K
        for r in range(2):
            K.dma(self.grow[r][:], self.MODD[l][r:r + 1, idx * D:(idx + 1) * D])
            for half in range(2):
                ps = self.bank(4 + half)
                K.matmul(ps, self.ones_f[0:1, :], self.grow[r][0:1, half * 512:(half + 1) * 512])
                K.copy(tiles2[r][:, half * 512:(half + 1) * 512], ps, eng="act")

    def make_xinT(self, xb, t, l, sub):
        K = self.K
        r = 0 if t < 16 else 1
        self.cnt += 1
        pt = self.psT[:, (self.cnt % 2) * 1024:(self.cnt % 2) * 1024 + 1024]
        for kc in range(8):
            K.transpose(pt[:, kc * 128:(kc + 1) * 128], xb[:, kc * 128:(kc + 1) * 128], self.ident_bf[:])
        mt = self.modT[l]
        for kc in range(8):
            if kc < 8:
                K.act(self.xinT[:, kc, t * 128:(t + 1) * 128], pt[:, kc * 128:(kc + 1) * 128], AF.Identity,
                      bias=mt[:, (3 * sub) * 8 + kc, r:r + 1], scale=mt[:, (3 * sub + 1) * 8 + kc, r:r + 1])
            else:
                K.ts(self.xinT[:, kc, t * 128:(t + 1) * 128], pt[:, kc * 128:(kc + 1) * 128],
                     mt[:, (3 * sub + 1) * 8 + kc, r:r + 1], mt[:, (3 * sub) * 8 + kc, r:r + 1], ALU.mult, ALU.add)

    def src_rows(self, t, first):
        if first:
            if t < 16:
                return self.din["x"][t * 128:(t + 1) * 128, :]
            return self.din["ctx"][(t - 16) * 128:(t - 15) * 128, :]
        return self.XS[t * 128:(t + 1) * 128, :]

    def stage_first_xin(self):
        K, A = self.K, self.A
        m0 = A.mark()
        xf = [A.alloc("xf%d" % i, [D], F32) for i in range(2)]
        xb = [A.alloc("xb%d" % i, [D], BF16) for i in range(2)]
        for t in range(NT):
            K.dma(xf[t % 2][:], self.src_rows(t, True))
            K.copy(xb[t % 2][:], xf[t % 2][:], eng="pool")
            self.make_xinT(xb[t % 2], t, 0, 0)
        A.release(m0)

    def epi_alloc(self):
        A = self.A
        E = {}
        E["gate"] = [A.alloc("gate%d" % r, [D], F32) for r in range(2)]
        E["xr"] = [A.alloc("xr%d" % i, [D], F32) for i in range(4)]
        E["t1"] = [A.alloc("t1_%d" % i, [D], F32) for i in range(2)]
        E["xn"] = [A.alloc("xn%d" % i, [D], F32) for i in range(2)]
        E["xb"] = [A.alloc("xbb%d" % i, [D], BF16) for i in range(2)]
        E["st"] = [A.alloc("st%d" % i, [16], F32) for i in range(2)]
        E["i"] = 0
        return E

    def epi_front(self, E, Y, t, xr):
        K = self.K
        i = E["i"]
        E["i"] += 1
        r = 0 if t < 16 else 1
        t1 = E["t1"][i % 2]
        K.tt(t1[:], Y, E["gate"][r][:], ALU.mult)
        K.tt(xr[:], t1[:], xr[:], ALU.add, eng="pool")
        return i

    def epi_back(self, E, i, t, xr, partial, dst, nxt):
        K = self.K
        if partial:
            K.dma(dst, xr[:], q="sp")
            return
        st = E["st"][i % 2]
        K.bn_stats(st[:, 0:6], xr[:, 0:512])
        K.bn_stats(st[:, 6:12], xr[:, 512:1024])
        K.bn_aggr(st[:, 12:14], st[:, 0:12].rearrange("p (a b) -> p a b", b=6))
        K.ts(st[:, 14:15], st[:, 13:14], LN_EPS / (ALPHA * ALPHA), None, ALU.add)
        K.act(st[:, 14:15], st[:, 14:15], AF.Sqrt)
        K.recip(st[:, 14:15], st[:, 14:15])
        K.stt(st[:, 15:16], st[:, 12:13], -1.0, st[:, 14:15], ALU.mult, ALU.mult)
        xn = E["xn"][i % 2]
        K.ts(xn[:], xr[:], st[:, 14:15], st[:, 15:16], ALU.mult, ALU.add)
        K.dma(dst, xn[:], q="sp", is_output=(nxt == "final"))
        if nxt is None or nxt == "final":
            return
        xb = E["xb"][i % 2]
        K.act(xb[:], xr[:], AF.Identity, bias=st[:, 15:16], scale=st[:, 14:15])
        self.make_xinT(xb, t, nxt[0], nxt[1])

    def run_tiles(self, E, tiles, srcs, mm, partial, dsts, nxt):
        K = self.K
        LAG = 2
        NX = 4
        npre = 1
        for k in range(min(npre, len(tiles))):
            K.dma(E["xr"][k % NX][:], srcs[k], q="sp")
        pend = []
        for k, t in enumerate(tiles):
            Y = self.psA[:, (k % 2) * 1024:(k % 2) * 1024 + 1024]
            mm(t, Y)
            if k + npre < len(tiles):
                K.dma(E["xr"][(k + npre) % NX][:], srcs[k + npre], q="sp")
            if len(pend) >= LAG:
                pi_, pt_, pk_ = pend.pop(0)
                self.epi_back(E, pi_, pt_, E["xr"][pk_ % NX], partial, dsts[pk_], nxt)
            i = self.epi_front(E, Y, t, E["xr"][k % NX])
            pend.append((i, t, k))
        for (pi_, pt_, pk_) in pend:
            self.epi_back(E, pi_, pt_, E["xr"][pk_ % NX], partial, dsts[pk_], nxt)

    def outproj(self, l, catT, tiles, nxt, wo=None):
        K, A = self.K, self.A
        m0 = A.mark()
        w_out = self.W(l, "mix_w_out")
        if wo is None:
            wo = A.alloc("wo", [8, D], BF16)
            K.dma(wo[:], w_out.rearrange("(c q) f -> q c f", q=128), q="pool")
        E = self.epi_alloc()
        self.load_gate_tiles(l, 5, E["gate"])
        srcs = [self.src_rows(t, False) for t in tiles]
        dsts = [self.XS[t * 128:(t + 1) * 128, :] for t in tiles]

        def mm(t, Y):
            for half in range(2):
                for c in range(8):
                    K.matmul(Y[:, half * 512:(half + 1) * 512], catT[:, c, t * 128:(t + 1) * 128],
                             wo[:, c, half * 512:(half + 1) * 512], start=(c == 0), stop=(c == 7))
        self.run_tiles(E, tiles, srcs, mm, False, dsts, nxt)
        A.release(m0)

    def proj_fm(self, dst, wt, ncols, scale, groups, bank_i):
        K = self.K
        for gi, (t0, nt) in enumerate(groups):
            ps = self.psA[0:ncols, ((bank_i + gi) % 4) * 512:((bank_i + gi) % 4) * 512 + nt]
            for kc in range(8):
                K.matmul(ps, wt[:, kc, 0:ncols], self.xinT[:, kc, t0:t0 + nt], start=(kc == 0), stop=(kc == 7))
            if scale == 1.0:
                K.copy(dst[0:ncols, t0:t0 + nt], ps, eng="act")
            else:
                K.act(dst[0:ncols, t0:t0 + nt], ps, AF.Identity, scale=scale)

    def na_stage(self, catT):
        K, A = self.K, self.A
        m0 = A.mark()
        w_in = self.W(0, "mix_w_in")
        tab_d = self.dram_in("na_tab", [3, 128, 8 * 14 * 64])
        groups = [(g * 512, 512) for g in range(4)] + [(2048, 256)]
        wq = A.alloc("wq", [8, 128], BF16)
        wk = A.alloc("wk", [8, 128], BF16)
        wv = A.alloc("wv", [8, 128], BF16)
        qT = A.alloc("qT", [T], BF16)
        kT = A.alloc("kT", [T], BF16)
        Vaug = A.alloc("Vaug", [NT, 2, 128], BF16)
        btab = A.alloc("btab", [3, 2, 14 * 64], BF16)
        PT = [A.alloc("PT%d" % i, [140 * 64], BF16) for i in range(2)]
        PTc = [A.alloc("PTc%d" % i, [2, T], BF16) for i in range(2)]
        Rr = [A.alloc("Rr%d" % i, [512], F32) for i in range(2)]
        K.memset(Vaug[:], 1.0)
        qTz = [A.alloc("qTz%d" % i, [T], BF16) for i in range(2)]
        K.memset(qTz[0][64:128, :], 0.0)
        K.memset(qTz[1][0:64, :], 0.0)

        def rrange(i):
            if i <= 7:
                return 0, i + 4
            if i <= 23:
                return i - 3, i + 4
            return i - 3, 31
        tiles_geo = []
        off = 0
        for j in range(16):
            lo = min(rrange(2 * j)[0], rrange(2 * j + 1)[0])
            hi = max(rrange(2 * j)[1], rrange(2 * j + 1)[1])
            var = 0 if j < 4 else (1 if j < 12 else 2)
            tiles_geo.append((lo, hi, off, var))
            off += (hi - lo + 1) * 64
        assert off == 140 * 64
        hcount = 0
        bk = 0
        for hp in range(4):
            K.dma(wq[:], w_in[:, hp * 128:(hp + 1) * 128].rearrange("(kc q) n -> q kc n", q=128), q="pool")
            K.dma(wk[:], w_in[:, 512 + hp * 128:512 + (hp + 1) * 128].rearrange("(kc q) n -> q kc n", q=128), q="pool")
            K.dma(wv[:], w_in[:, 1024 + hp * 128:1024 + (hp + 1) * 128].rearrange("(kc q) n -> q kc n", q=128), q="pool")
            for v in range(3):
                K.dma(btab[:, v, :, :], tab_d[v, :, hp * 2 * 896:(hp + 1) * 2 * 896].rearrange("p (h f) -> p h f", h=2), q="pool")
            self.proj_fm(qT, wq, 128, 0.125, groups, 0)
            self.proj_fm(kT, wk, 128, 1.0, groups, 1)
            K.copy(qTz[0][0:64, :], qT[0:64, :], eng="pool")
            K.copy(qTz[1][64:128, :], qT[64:128, :], eng="pool")
            for g4 in range(0, NT, 4):
                nt4 = min(4, NT - g4)
                ps = self.bank(bk % 4)
                bk += 1
                for tt_ in range(nt4):
                    t = g4 + tt_
                    for kc in range(8):
                        K.matmul(ps[:, tt_ * 128:(tt_ + 1) * 128], self.xinT[:, kc, t * 128:(t + 1) * 128], wv[:, kc, :],
                                 start=(kc == 0), stop=(kc == 7))
                K.copy(Vaug[:, g4:g4 + nt4, :, 0:64],
                       ps[:, 0:nt4 * 128].rearrange("p (a h d) -> p a h d", h=2, d=64), eng="act")
            for h2 in range(2):
                hb = h2 * 64
                hcount += 1
                pt = PT[hcount % 2]
                ptc = PTc[hcount % 2]
                for j in range(16):
                    lo, hi, poff, var = tiles_geo[j]
                    r = lo
                    while r <= hi:
                        nr = min(8, hi - r + 1)
                        ps = self.bank(bk % 4, nr * 64)
                        bk += 1
                        K.matmul(ps, kT[:, j * 128:(j + 1) * 128], qTz[h2][:, r * 64:(r + nr) * 64],
                                 start=True, stop=False)
                        d0 = (r - 2 * j + 3) + 3
                        K.matmul(ps, self.ident_bf[:], btab[:, var, h2, d0 * 64:(d0 + nr) * 64], start=False, stop=True)
                        K.act(pt[:, poff + (r - lo) * 64:poff + (r - lo + nr) * 64], ps, AF.Exp)
                        r += nr
                for ct in range(2):
                    for (t0, nt) in groups:
                        ps = self.bank(bk % 4, nt)
                        bk += 1
                        K.matmul(ps, kT[:, 2048 + ct * 128:2048 + (ct + 1) * 128], qTz[h2][:, t0:t0 + nt])
                        K.act(ptc[:, ct, t0:t0 + nt], ps, AF.Exp)
                for qb, (t0, nt) in enumerate(groups):
                    O = self.bank(4 + (bk % 2), nt)
                    bk += 1
                    K.matmul(O, Vaug[:, 16, h2, :], ptc[:, 0, t0:t0 + nt], start=True, stop=False)
                    last_is_ctx = (qb == 4)
                    K.matmul(O, Vaug[:, 17, h2, :], ptc[:, 1, t0:t0 + nt], start=False, stop=last_is_ctx)
                    if qb < 4:
                        R0, R1 = 8 * qb, 8 * qb + 7
                        js = [j for j in range(16) if not (tiles_geo[j][1] < R0 or tiles_geo[j][0] > R1)]
                        for jj, j in enumerate(js):
                            lo, hi, poff, var = tiles_geo[j]
                            a, b = max(lo, R0), min(hi, R1)
                            K.matmul(O[:, (a - R0) * 64:(b - R0 + 1) * 64], Vaug[:, j, h2, :],
                                     pt[:, poff + (a - lo) * 64:poff + (b - lo + 1) * 64],
                                     start=False, stop=(jj == len(js) - 1))
                    rr = Rr[bk % 2]
                    K.recip(rr[64:128, 0:nt], O[64:128, :])
                    K.tt(catT[hb:hb + 64, hp, t0:t0 + nt], O[0:64, :], rr[64:128, 0:nt], ALU.mult)
        A.release(m0)


    def gla_stage(self, catT):
        K, A = self.K, self.A
        m0 = A.mark()
        w_in = self.W(0, "mix_w_in")
        wg_d = [self.dram_in("gla_wgf", [17, 256]), self.dram_in("gla_wgb", [17, 256])]
        ng_d = self.dram_in("gla_ng", [1, 128])
        tri_d = self.dram_in("gla_tri", [4, 128, 128])
        msk_d = self.dram_in("gla_msk", [2, 128, 128])
        groups = [(g * 512, 512) for g in range(4)] + [(2048, 256)]
        bkc = [0]

        def nb(n=512):
            bkc[0] += 1
            return self.bank(bkc[0] % 6, n)

        tri = A.alloc("tri", [4, 128], F32)
        K.dma(tri[:], tri_d.rearrange("a s t -> s a t"))
        msk4 = A.alloc("msk4", [2, 2, 128], BF16)
        for dirn in range(2):
            for hh in range(2):
                K.dma(msk4[:, dirn, hh, :], msk_d[dirn, :, :], q="pool")
        ngrow = A.alloc("ngrow", [128], F32, parts=1)
        K.dma(ngrow[:], ng_d[:, :])
        ng2 = A.alloc("ng2", [2, 128], F32)
        psn = nb(128)
        K.matmul(psn, self.ones_f[0:1, :], ngrow[0:1, :])
        for hh in range(2):
            K.copy(ng2[:, hh, :], psn, eng="act")
        wg = [A.alloc("wg%d" % i, [256], F32) for i in range(2)]
        for i in range(2):
            K.memset(wg[i][:], 0.0)
            K.dma(wg[i][0:16, :], wg_d[i][0:16, :])
            K.dma(wg[i][32:33, :], wg_d[i][16:17, :])
        zT = [A.alloc("zT%d" % i, [T], F32) for i in range(2)]
        mz = A.mark()
        wz = A.alloc("wz", [8, 32], BF16)
        K.dma(wz[:], w_in[:, 3072:3104].rearrange("(kc q) n -> q kc n", q=128), q="pool")
        for i in range(2):
            K.memset(zT[i][:], 0.0)
            K.memset(zT[i][32:33, :], 1.0)
            for (t0, nt) in groups:
                ps = nb(nt)[0:16, :]
                for kc in range(8):
                    K.matmul(ps, wz[:, kc, i * 16:(i + 1) * 16], self.xinT[:, kc, t0:t0 + nt], start=(kc == 0), stop=(kc == 7))
                K.copy(zT[i][0:16, t0:t0 + nt], ps, eng="act")
        A.release(mz)

        for p in range(2):
            mp = A.mark()
            vg = A.alloc("vg", [NT, 256], BF16)
            sg = A.alloc("sg", [NT, 256], BF16)
            qtz = A.alloc("qtz", [2, 2, T], BF16)
            ktT = A.alloc("ktT", [2, T], BF16)
            Sin = A.alloc("Sin", [2, NT, 128], BF16)
            S = A.alloc("S", [128], F32)
            K.memset(qtz[64:128, :, 0, :], 0.0)
            K.memset(qtz[0:64, :, 1, :], 0.0, eng="dve")
            m1 = A.mark()
            qgT = A.alloc("qgT", [T], BF16)
            kgT = A.alloc("kgT", [T], BF16)
            kg = A.alloc("kg", [NT, 128], BF16)
            mw = A.mark()
            wgq = A.alloc("wgq", [8, 128], BF16)
            wgk = A.alloc("wgk", [8, 128], BF16)
            wgv = A.alloc("wgv", [8, 256], BF16)
            wgr = A.alloc("wgr", [8, 256], BF16)
            for (wt, c0, n) in ((wgq, 1536 + p * 128, 128), (wgk, 1792 + p * 128, 128),
                                (wgv, 2048 + p * 256, 256), (wgr, 2560 + p * 256, 256)):
                K.dma(wt[:], w_in[:, c0:c0 + n].rearrange("(kc q) n -> q kc n", q=128), q="pool")
            self.proj_fm(qgT, wgq, 128, 0.125, groups, 0)
            self.proj_fm(kgT, wgk, 128, 1.0, groups, 2)
            for t in range(NT):
                tsl = slice(t * 128, (t + 1) * 128)
                pk = nb(128)
                for kc in range(8):
                    K.matmul(pk, self.xinT[:, kc, tsl], wgk[:, kc, :], start=(kc == 0), stop=(kc == 7))
                K.copy(kg[:, t, :], pk, eng="act")
                pv = nb(256)
                for kc in range(8):
                    K.matmul(pv, self.xinT[:, kc, tsl], wgv[:, kc, :], start=(kc == 0), stop=(kc == 7))
                K.copy(vg[:, t, :], pv, eng="dve")
                pr = nb(256)
                for kc in range(8):
                    K.matmul(pr, self.xinT[:, kc, tsl], wgr[:, kc, :], start=(kc == 0), stop=(kc == 7))
                K.act(sg[:, t, :], pr, AF.Silu)
            A.release(mw)
            tmps = {}
            for nm, dt_ in (("e1", F32), ("sp", F32), ("Eb", F32), ("Enb", F32), ("Ed", F32), ("ku", BF16)):
                tmps[nm] = [A.alloc("%s%d" % (nm, i), [128], dt_) for i in range(2)]
            its = []
            for dirn in range(2):
                order = ([16, 17] + list(range(16))) if dirn == 0 else ([17, 16] + list(range(15, -1, -1)))
                its += [(dirn, n, i_ == 0) for i_, n in enumerate(order)]

            def gA(k):
                dirn, n, _ = its[k]
                triA = tri[:, 0 if dirn == 0 else 1, :]
                triB = tri[:, 2 if dirn == 0 else 3, :]
                tsl = slice(n * 128, (n + 1) * 128)
                e1, sp, Eb, Enb, Ed, ku = [tmps[kk][k % 2] for kk in ("e1", "sp", "Eb", "Enb", "Ed", "ku")]
                pz = nb(128)
                K.matmul(pz, zT[dirn][:, tsl], wg[dirn][:, p * 128:(p + 1) * 128])
                K.act(e1[:], pz, AF.Exp, scale=-1.0)
                K.act(sp[:], e1[:], AF.Ln, bias=1.0)
                pb = nb(128)
                K.matmul(pb, sp[:], triA)
                pd = nb(128)
                K.matmul(pd, triB, sp[:])
                K.act(Eb[:], pb, AF.Exp)
                K.act(Enb[:], pb, AF.Exp, scale=-1.0)
                K.act(Ed[:], pd, AF.Exp)
                for hh in range(2):
                    hb = hh * 64
                    K.tt(qtz[hb:hb + 64, dirn, hh, tsl], qgT[hb:hb + 64, tsl], Eb[hb:hb + 64, :], ALU.mult)
                K.tt(ktT[:, dirn, tsl], kgT[:, tsl], Enb[:], ALU.mult)
                K.tt(ku[:], kg[:, n, :], Ed[:], ALU.mult, eng="pool")

            def gB(k):
                dirn, n, first = its[k]
                lastcol = 127 if dirn == 0 else 0
                Eb, ku = tmps["Eb"][k % 2], tmps["ku"][k % 2]
                if first:
                    K.memset(S[:], 0.0)
                K.copy(Sin[:, dirn, n, :], S[:], eng="pool")
                pu = nb(256)
                K.matmul(pu, ku[:], vg[:, n, :])
                for hh in range(2):
                    hb = hh * 64
                    K.stt(S[hb:hb + 64, :], S[hb:hb + 64, :], Eb[hb:hb + 64, lastcol:lastcol + 1],
                          pu[hb:hb + 64, hh * 128:(hh + 1) * 128], ALU.mult, ALU.add)
            gA(0)
            for k in range(len(its)):
                if k + 1 < len(its):
                    gA(k + 1)
                gB(k)
            A.release(m1)
            attT = [A.alloc("attT%d" % i, [2, 2, 128], BF16) for i in range(2)]
            ss = [A.alloc("ss%d" % i, [2], F32) for i in range(2)]
            junk = A.alloc("junk", [128], BF16)
            tmpo = [A.alloc("tmpo%d" % i, [256], F32) for i in range(2)]
            og = [A.alloc("og%d" % i, [256], BF16) for i in range(2)]
            Obank = {}

            def hA(n):
                tsl = slice(n * 128, (n + 1) * 128)
                at = attT[n % 2]
                for dirn in range(2):
                    pa = nb(256)
                    for hh in range(2):
                        K.matmul(pa[:, hh * 128:(hh + 1) * 128], ktT[:, dirn, tsl], qtz[:, dirn, hh, tsl],
                                 start=(hh == 0), stop=(hh == 1))
                    K.tt(at[:, dirn, :, :], pa.rearrange("p (h t) -> p h t", h=2), msk4[:, dirn, :, :], ALU.mult)
                O = nb(256)
                Obank[n] = O
                first = True
                for hh in range(2):
                    for dirn in range(2):
                        K.matmul(O[:, hh * 128:(hh + 1) * 128], at[:, dirn, hh, :], vg[:, n, hh * 128:(hh + 1) * 128],
                                 start=first, stop=False)
                        first = False
                        K.matmul(O[:, hh * 128:(hh + 1) * 128], qtz[:, dirn, hh, tsl], Sin[:, dirn, n, :],
                                 start=False, stop=(dirn == 1))

            def hB(n):
                tsl = slice(n * 128, (n + 1) * 128)
                O = Obank.pop(n)
                s_ = ss[n % 2]
                for hh in range(2):
                    K.act(junk[:], O[:, hh * 128:(hh + 1) * 128], AF.Square, accum_out=s_[:, hh:hh + 1])
                K.ts(s_[:], s_[:], 1.0 / 128.0, LN_EPS, ALU.mult, ALU.add)
                K.act(s_[:], s_[:], AF.Sqrt)
                K.recip(s_[:], s_[:])
                tm = tmpo[n % 2]
                for hh in range(2):
                    K.stt(tm[:, hh * 128:(hh + 1) * 128], O[:, hh * 128:(hh + 1) * 128], s_[:, hh:hh + 1],
                          sg[:, n, hh * 128:(hh + 1) * 128], ALU.mult, ALU.mult)
                o_ = og[n % 2]
                K.tt(o_[:], tm[:], ng2[:].rearrange("p a b -> p (a b)"), ALU.mult, eng="pool")
                self.cnt += 1
                pt = self.psT[:, (self.cnt % 2) * 1024:(self.cnt % 2) * 1024 + 256]
                for hh in range(2):
                    K.transpose(pt[:, hh * 128:(hh + 1) * 128], o_[:, hh * 128:(hh + 1) * 128], self.ident_bf[:])
                K.copy(catT[:, 4 + 2 * p:6 + 2 * p, tsl], pt.rearrange("p (h t) -> p h t", h=2), eng="act")
            hA(0)
            for n in range(NT):
                if n + 1 < NT:
                    hA(n + 1)
                hB(n)
            A.release(mp)
        A.release(m0)


    def diff_stage(self, catT):
        K, A = self.K, self.A
        m0 = A.mark()
        LI = 0.8 - 0.6 * math.exp(-0.3)
        w_in = self.W(1, "mix_w_in")
        cos_d = self.dram_in("rope_cos", [128, SEQ])
        sin_d = self.dram_in("rope_sin", [128, SEQ])
        perm_d = self.dram_in("rope_perm", [128, 128])
        lam_d = self.dram_in("lamv", [1, 256])
        sub_d = self.dram_in("subln_g", [1, 128])
        xgroups = [(g * 512, 512) for g in range(4)]
        cosT = A.alloc("cosT", [SEQ], F32)
        sinT = A.alloc("sinT", [SEQ], F32)
        K.dma(cosT[:], cos_d[:, :])
        K.dma(sinT[:], sin_d[:, :])
        perm = A.alloc("perm", [128], BF16)
        K.dma(perm[:], perm_d[:, :], q="pool")
        lamv = A.alloc("lamv", [256], F32, parts=1)
        K.dma(lamv[:], lam_d[:, :])
        prod = A.alloc("prod", [128], F32, parts=1)
        K.tt(prod[0:1, 0:64], lamv[0:1, 0:64], lamv[0:1, 64:128], ALU.mult)
        K.tt(prod[0:1, 64:128], lamv[0:1, 128:192], lamv[0:1, 192:256], ALU.mult)
        s12 = A.alloc("s12", [4], F32, parts=1)
        K.reduce(s12[0:1, 0:2], prod[0:1, :].rearrange("p (a b) -> p a b", a=2), ALU.add)
        K.act(s12[0:1, 0:2], s12[0:1, 0:2], AF.Exp)
        K.tt(s12[0:1, 2:3], s12[0:1, 0:1], s12[0:1, 1:2], ALU.subtract)
        K.ts(s12[0:1, 3:4], s12[0:1, 2:3], -1.0, -LI, ALU.mult, ALU.add)
        nlam = A.alloc("nlam", [1], F32)
        psl = self.bank(5, 2)
        K.matmul(psl[:, 0:1], self.ones_f[0:1, :], s12[0:1, 3:4])
        K.copy(nlam[:], psl[:, 0:1], eng="act")
        subrow = A.alloc("subrow", [128], F32, parts=1)
        K.dma(subrow[:], sub_d[:, :])
        gsub = A.alloc("gsub", [128], F32)
        psg = self.bank(4, 128)
        K.matmul(psg, self.ones_f[0:1, :], subrow[0:1, :])
        K.act(gsub[:], psg, AF.Identity, scale=(1.0 - LI))

        wq = [A.alloc("dwq%d" % i, [8, 128], BF16) for i in range(2)]
        wk = [A.alloc("dwk%d" % i, [8, 128], BF16) for i in range(2)]
        wv = [A.alloc("dwv%d" % i, [8, 128], BF16) for i in range(2)]
        qTz = [[A.alloc("dqTz%d_%d" % (bb, i), [SEQ], BF16) for i in range(2)] for bb in range(2)]
        kT = [A.alloc("dkT%d" % bb, [T], BF16) for bb in range(2)]
        Vaug = [A.alloc("dVaug%d" % bb, [NT, 128], BF16) for bb in range(2)]
        for bb in range(2):
            K.memset(qTz[bb][0][64:128, :], 0.0)
            K.memset(qTz[bb][1][0:64, :], 0.0)
        ub = [A.alloc("ub%d" % i, [512], BF16) for i in range(2)]
        t1 = [A.alloc("rt1_%d" % i, [512], F32) for i in range(2)]
        t2 = [A.alloc("rt2_%d" % i, [512], F32) for i in range(2)]
        Pt = [A.alloc("Pt%d" % i, [512], BF16) for i in range(4)]
        o1 = A.alloc("do1", [512], F32)
        o2s = [A.alloc("do2_%d" % i, [512], F32) for i in range(2)]
        sqs = [A.alloc("dsq%d" % i, [512], BF16) for i in range(2)]
        deferred = []
        fi = 0
        Rt = A.alloc("dRt", [512], F32)
        rs = A.alloc("drs", [512], F32)
        onesb = A.alloc("onesb", [128], BF16)
        K.memset(onesb[:], 1.0)
        gcol = A.alloc("gcol", [1], F32)
        K.dma(gcol[:], sub_d.rearrange("o p -> p o"), allow_slow_non_contiguous=True)
        gvec = A.alloc("gvec", [1], F32)
        K.ts(gvec[:], gcol[:], (1.0 - LI), None, ALU.mult)
        bk = 0
        pi = 0
        rc = [0]

        def proj_steps(h):
            b = h % 2
            steps = []

            def wload():
                for (wt, c0) in ((wq[b], h * 128), (wk[b], 1024 + h * 128), (wv[b], 2048 + h * 128)):
                    K.dma(wt[:], w_in[:, c0:c0 + 128].rearrange("(kc q) n -> q kc n", q=128), q="pool")
            steps.append(wload)
            for which in range(2):
                for (t0, nt) in xgroups:
                    def rope(which=which, t0=t0, nt=nt):
                        wt = (wq, wk)[which][b]
                        rc[0] += 1
                        ri = rc[0]
                        ps = self.bank(7, nt)
                        for kc in range(8):
                            K.matmul(ps, wt[:, kc, :], self.xinT[:, kc, t0:t0 + nt], start=(kc == 0), stop=(kc == 7))
                        u_b, a1, a2 = ub[ri % 2], t1[ri % 2], t2[ri % 2]
                        K.copy(u_b[:], ps, eng="dve")
                        K.tt(a1[:], ps, cosT[:, t0:t0 + nt], ALU.mult)
                        pr = self.bank(7, nt)
                        K.matmul(pr, perm[:], u_b[:])
                        K.tt(a2[:], pr, sinT[:, t0:t0 + nt], ALU.mult)
                        if which == 0:
                            for m in range(2):
                                K.tt(qTz[b][m][m * 64:(m + 1) * 64, t0:t0 + nt], a1[m * 64:(m + 1) * 64, :],
                                     a2[m * 64:(m + 1) * 64, :], ALU.add, eng="pool")
                        else:
                            K.tt(kT[b][:, t0:t0 + nt], a1[:], a2[:], ALU.add, eng="pool")
                    steps.append(rope)

            def ctxk():
                ps = self.bank(7, 256)
                for kc in range(8):
                    K.matmul(ps, wk[b][:, kc, :], self.xinT[:, kc, 2048:2304], start=(kc == 0), stop=(kc == 7))
                K.copy(kT[b][:, 2048:2304], ps, eng="dve")
            steps.append(ctxk)
            for g4 in range(0, NT, 4):
                def vproj(g4=g4):
                    nt4 = min(4, NT - g4)
                    ps = self.bank(7)
                    for tt_ in range(nt4):
                        t = g4 + tt_
                        for kc in range(8):
                            K.matmul(ps[:, tt_ * 128:(tt_ + 1) * 128], self.xinT[:, kc, t * 128:(t + 1) * 128], wv[b][:, kc, :],
                                     start=(kc == 0), stop=(kc == 7))
                    K.copy(Vaug[b][:, g4:g4 + nt4, 0:128], ps[:, 0:nt4 * 128].rearrange("p (a e) -> p a e", e=128), eng="dve")
                steps.append(vproj)
            return steps

        for st_ in proj_steps(0):
            st_()
        for h in range(8):
            b = h % 2
            nsteps = proj_steps(h + 1) if h < 7 else []
            for qg in range(4):
                q0 = qg * 512
                for m in range(2):
                    bk += 1
                    O = self.bank(2 + 2 * (bk % 2))
                    Dn = self.bank(3 + 2 * (bk % 2))

                    def S_(kt):
                        ps_ = self.bank((0, 1, 6)[kt % 3])
                        K.matmul(ps_, kT[b][:, kt * 128:(kt + 1) * 128], qTz[b][m][:, q0:q0 + 512])
                        return ps_
                    sq_ = [S_(0), S_(1)]
                    for kt in range(NT):
                        cur = sq_.pop(0)
                        pi += 1
                        P = Pt[pi % 4]
                        K.act(P[:], cur, AF.Exp, scale=0.125)
                        if kt + 2 < NT:
                            sq_.append(S_(kt + 2))
                        K.matmul(O, Vaug[b][:, kt, 0:128], P[:], start=(kt == 0), stop=(kt == NT - 1))
                        K.matmul(Dn, onesb[:], P[:], start=(kt == 0), stop=(kt == NT - 1))
                        if kt == 8 and deferred:
                            deferred.pop(0)()
                        if kt in (3, 13) and nsteps:
                            nsteps.pop(0)()
                    K.recip(Rt[:], Dn)
                    if m == 0:
                        K.tt(o1[:], O, Rt[:], ALU.mult)
                    else:
                        o2 = o2s[fi % 2]
                        sq = sqs[fi % 2]
                        fi += 1
                        K.ts(Rt[:], Rt[:], nlam[:, 0:1], None, ALU.mult)
                        K.tt(o2[:], O, Rt[:], ALU.mult)
                        K.tt(o2[:], o2[:], o1[:], ALU.add, eng="pool")
                        K.tt(sq[:], o2[:], o2[:], ALU.mult, eng="pool")

                        def fin(o2=o2, sq=sq, Dn=Dn, h=h, q0=q0):
                            K.matmul(Dn, onesb[:], sq[:])
                            K.ts(rs[:], Dn, 1.0 / 128.0, LN_EPS, ALU.mult, ALU.add)
                            K.act(rs[:], rs[:], AF.Ln)
                            K.act(rs[:], rs[:], AF.Exp, scale=-0.5)
                            K.stt(catT[:, h, q0:q0 + 512], o2[:], gvec[:, 0:1], rs[:], ALU.mult, ALU.mult)
                        deferred.append(fin)
            while nsteps:
                nsteps.pop(0)()
        while deferred:
            deferred.pop(0)()
        A.release(m0)

    def mixer1(self):
        K, A = self.K, self.A
        m0 = A.mark()
        catT = A.alloc("catT1", [8, T], BF16)
        wo1 = A.alloc("wo1", [8, D], BF16)
        K.dma(wo1[:], self.W(1, "mix_w_out").rearrange("(c q) f -> q c f", q=128), q="pool")
        self.diff_stage(catT)
        if self.upto == "l1diff":
            dbg = self.dram_out("dbg_catT", [128, 8 * T], BF16)
            K.dma(dbg[:, :], catT[:].rearrange("p a b -> p (a b)"))
            return
        self.outproj(1, catT, list(range(16)), nxt=(1, 2), wo=wo1)
        A.release(m0)

    def mixer0(self):
        K, A = self.K, self.A
        m0 = A.mark()
        catT = A.alloc("catT", [8, T], BF16)
        self.na_stage(catT)
        if self.upto == "l0na":
            dbg = self.dram_out("dbg_catT", [128, 8 * T], BF16)
            K.dma(dbg[:, :], catT[:].rearrange("p a b -> p (a b)"))
            return
        self.gla_stage(catT)
        if self.upto == "l0gla":
            dbg = self.dram_out("dbg_catT", [128, 8 * T], BF16)
            K.dma(dbg[:, :], catT[:].rearrange("p a b -> p (a b)"))
            return
        self.outproj(0, catT, list(range(NT)), nxt=(0, 2))
        A.release(m0)

    def ffn(self, l, which, sub, tiles, first, nxt):
        K, A = self.K, self.A
        m0 = A.mark()
        w_in = self.W(l, which + "_w_in")
        w_out = self.W(l, which + "_w_out")
        HT = A.alloc("HT", [11, T], BF16)
        woutb = [A.alloc("wout%d" % p, [11, D], BF16) for p in range(2)]
        wa = [A.alloc("wa%d" % i, [8, 256], BF16) for i in range(2)]
        wg = [A.alloc("wg%d" % i, [8, 256], BF16) for i in range(2)]
        sa = [A.alloc("sa%d" % i, [512], F32) for i in range(2)]
        E = self.epi_alloc()
        self.load_gate_tiles(l, 3 * sub + 2, E["gate"])
        groups = []
        if 0 in tiles:
            groups += [(g * 512, 512) for g in range(4)]
        if 16 in tiles:
            groups += [(2048, 256)]
        ci = 0
        gi = 0
        allg = []
        for p in range(2):
            c0 = 11 * p
            allg += [(p, c0 + 2 * i, 2) for i in range(5)] + [(p, c0 + 10, 1)]
        issued = set()

        def issue(gidx):
            if gidx in issued or gidx >= len(allg):
                return
            issued.add(gidx)
            _, cs_, n_ = allg[gidx]
            K.dma(wa[gidx % 2][:, :, 0:n_ * 128], w_in[:, cs_ * 128:(cs_ + n_) * 128].rearrange("(kc q) n -> q kc n", q=128), q="pool")
            K.dma(wg[gidx % 2][:, :, 0:n_ * 128], w_in[:, DFF + cs_ * 128:DFF + (cs_ + n_) * 128].rearrange("(kc q) n -> q kc n", q=128), q="pool")
        issue(0)
        K.dma(woutb[0][:], w_out[0:1408, :].rearrange("(hc q) f -> q hc f", q=128), q="pool")
        for p in range(2):
            c0 = 11 * p
            for gl in range(6):
                gidx = 6 * p + gl
                _, cs, n = allg[gidx]
                issue(gidx)
                issue(gidx + 1) if gl < 5 else None
                a_t = wa[gidx % 2]
                g_t = wg[gidx % 2]
                if p == 0 and gl == 2:
                    K.dma(woutb[1][:], w_out[1408:2816, :].rearrange("(hc q) f -> q hc f", q=128), q="pool")
                if gl == 5:
                    issue(gidx + 1)
                for j in range(n):
                    hl = cs + j - c0
                    for (t0, nt) in groups:
                        pa = self.bank(2 * (gi % 2), nt)
                        pg = self.bank(2 * (gi % 2) + 1, nt)
                        s_t = sa[gi % 2]
                        gi += 1
                        for kc in range(8):
                            K.matmul(pa, a_t[:, kc, j * 128:(j + 1) * 128], self.xinT[:, kc, t0:t0 + nt],
                                     start=(kc == 0), stop=(kc == 7))
                        for kc in range(8):
                            K.matmul(pg, g_t[:, kc, j * 128:(j + 1) * 128], self.xinT[:, kc, t0:t0 + nt],
                                     start=(kc == 0), stop=(kc == 7))
                        K.act(s_t[:, 0:nt], pa, AF.Silu)
                        K.tt(HT[:, hl, t0:t0 + nt], s_t[:, 0:nt], pg, ALU.mult)
            partial = (p == 0)
            srcs = [self.src_rows(t, first and p == 0) for t in tiles]
            if partial or nxt != "final":
                dsts = [self.XS[t * 128:(t + 1) * 128, :] for t in tiles]
            else:
                dsts = [self.out_d[t * 128:(t + 1) * 128, :] for t in tiles]

            def mm(t, Y, p=p):
                for half in range(2):
                    for hl in range(11):
                        K.matmul(Y[:, half * 512:(half + 1) * 512], HT[:, hl, t * 128:(t + 1) * 128],
                                 woutb[p][:, hl, half * 512:(half + 1) * 512], start=(hl == 0), stop=(hl == 10))
            self.run_tiles(E, tiles, srcs, mm, partial, dsts, nxt)
        A.release(m0)


_CACHE = {}


def _get_program(upto="all", debug=False):
    key = (upto, debug)
    if key not in _CACHE:
        b = Builder(upto=upto, debug=debug)
        nc = b.build()
        _CACHE[key] = (nc, b)
    return _CACHE[key]


def _na_table(rpb):
    rpb = np.asarray(rpb, dtype=np.float32)
    kc = np.arange(64)[:, None]
    qc = np.arange(64)[None, :]
    col_start = np.clip(qc - 8, 0, 48)
    col_ok = (kc >= col_start) & (kc < col_start + 16)
    dc = np.clip(kc - qc, -15, 15) + 15
    tab = np.full((3, 2, 64, 8, 14, 64), NEG, dtype=np.float32)
    for v in range(3):
        for ki in range(2):
            for idx in range(14):
                drr = idx - 3
                dr = (10 if ki == 0 else 11) - drr
                if dr < 0 or dr > 14:
                    continue
                if v in (1, 2) and drr < ki:
                    continue
                if v in (0, 1) and drr > ki + 7:
                    continue
                g = rpb[:, dr][:, dc]
                g = np.where(col_ok[None], g, np.float32(NEG))
                tab[v, ki, :, :, idx, :] = g.transpose(1, 0, 2)
    return np.ascontiguousarray(tab.reshape(3, 128, 8 * 14 * 64))


ALL_INPUT_NAMES = (
    "x", "c", "ctx", "c_ctx",
    "l0_w_ada", "l0_b_ada", "l0_ffn1_w_in", "l0_ffn1_w_out", "l0_mix_w_in", "l0_mix_w_out", "l0_na_rpb",
    "l0_gla_w_gate_f", "l0_gla_b_gate_f", "l0_gla_w_gate_b", "l0_gla_b_gate_b", "l0_gla_norm_g",
    "l0_ffn2_w_in", "l0_ffn2_w_out",
    "l1_w_ada", "l1_b_ada", "l1_ffn1_w_in", "l1_ffn1_w_out", "l1_mix_w_in", "l1_mix_w_out",
    "l1_lambda_q1", "l1_lambda_k1", "l1_lambda_q2", "l1_lambda_k2", "l1_subln_g",
    "l1_ffn2_w_in", "l1_ffn2_w_out",
)


def _in_maps(inputs, n_cores=8, names=None):
    ident = np.eye(128, dtype=np.float32)
    maps = []
    shared = {}
    for l in (0, 1):
        for nm in W_NAMES[l]:
            a = np.ascontiguousarray(inputs["l%d_%s" % (l, nm)], dtype=np.float32)
            if nm == "b_ada":
                a = a.reshape(1, -1)
            shared["l%d_%s" % (l, nm)] = a
    shared["ident"] = ident
    if names is None or "gla_tri" in names:
        si = np.arange(128)[:, None]
        ti = np.arange(128)[None, :]
        g = np.float32(-1.0 / 16.0)
        shared["gla_tri"] = np.stack([(si <= ti) * g, (si >= ti) * g, (si > ti) * g, (si < ti) * g]).astype(np.float32)
        shared["gla_msk"] = np.stack([(si <= ti), (si > ti)]).astype(np.float32)
        shared["gla_wgf"] = np.concatenate([np.asarray(inputs["l0_gla_w_gate_f"], np.float32),
                                            np.asarray(inputs["l0_gla_b_gate_f"], np.float32).reshape(1, -1)], 0)
        shared["gla_wgb"] = np.concatenate([np.asarray(inputs["l0_gla_w_gate_b"], np.float32),
                                            np.asarray(inputs["l0_gla_b_gate_b"], np.float32).reshape(1, -1)], 0)
        shared["gla_ng"] = np.asarray(inputs["l0_gla_norm_g"], np.float32).reshape(1, 128)
    if names is None or "rope_cos" in names:
        t = np.arange(SEQ)
        row = (t // 64).astype(np.float32)
        col = (t % 64).astype(np.float32)
        inv = (np.float32(10000.0) ** (-np.arange(0, 32, 2, dtype=np.float32) / np.float32(32))).astype(np.float32)
        ar = row[:, None] * inv
        ac = col[:, None] * inv
        ang = np.concatenate([ar, ar, ac, ac], -1).astype(np.float32)
        sgn = np.where((np.arange(64) % 32) < 16, -1.0, 1.0).astype(np.float32)
        cosT = np.cos(ang).astype(np.float32).T
        sinT = (np.sin(ang).astype(np.float32) * sgn[None, :]).T
        shared["rope_cos"] = np.ascontiguousarray(np.concatenate([cosT, cosT], 0))
        shared["rope_sin"] = np.ascontiguousarray(np.concatenate([sinT, sinT], 0))
        pm = np.zeros((128, 128), np.float32)
        for dst in range(128):
            src = dst + 16 if (dst % 32) < 16 else dst - 16
            pm[src, dst] = 1.0
        shared["rope_perm"] = pm
        shared["lamv"] = np.concatenate([np.asarray(inputs["l1_lambda_" + k], np.float32).reshape(-1)
                                         for k in ("q1", "k1", "q2", "k2")]).reshape(1, 256)
        shared["subln_g"] = np.asarray(inputs["l1_subln_g"], np.float32).reshape(1, 128)
    if names is None or "na_tab" in names:
        shared["na_tab"] = _na_table(inputs["l0_na_rpb"])
    shared["c_ctx"] = np.ascontiguousarray(inputs["c_ctx"], dtype=np.float32).reshape(1, D)
    for b in range(n_cores):
        m = dict(shared)
        m["x"] = np.ascontiguousarray(inputs["x"][b], dtype=np.float32)
        m["ctx"] = np.ascontiguousarray(inputs["ctx"][b], dtype=np.float32)
        m["c"] = np.ascontiguousarray(inputs["c"][b], dtype=np.float32).reshape(1, D)
        if names is not None:
            m = {k: v for k, v in m.items() if k in names}
        maps.append(m)
    return maps


def kernel(**inputs):
    nc, b = _get_program()
    maps = _in_maps(inputs, names=set(b.din.keys()))
    res = run_bass_kernel_spmd(nc, maps, core_ids=list(range(8)))
    out = np.stack([np.asarray(r["out"], dtype=np.float32) for r in res.results], axis=0)
    return out
```

```python
import contextlib
import math
import numpy as np
import concourse.bass as bass
import concourse.mybir as mybir
from concourse.bass_utils import run_bass_kernel_spmd

F32 = mybir.dt.float32
BF16 = mybir.dt.bfloat16
AF = mybir.ActivationFunctionType
ALU = mybir.AluOpType
AX = mybir.AxisListType

D = 1024
SEQ = 2048
CTX = 256
T = SEQ + CTX
NT = T // 128
DFF = 2816
NHC = DFF // 128
ALPHA = 4.0 ** 0.25
LN_EPS = 1e-6
NEG = -1e30

COMPUTE = ("pe", "act", "dve", "pool")
REG = {}


def _isz(dt):
    return 2 if dt == BF16 else 4


class Op:
    __slots__ = ("eng", "fn", "tl", "seq", "waits", "signal", "clock", "count", "is_dma", "order")

    def __init__(self, eng, fn):
        self.eng = eng
        self.fn = fn
        self.tl = None
        self.seq = 0
        self.waits = []
        self.signal = False
        self.clock = None
        self.count = 0
        self.is_dma = False
        self.order = 0


def _region(ap):
    t = ap.tensor
    key, base, isz = REG[t.name]
    isz = _isz(ap.dtype)
    pat = ap.ap
    off = int(ap.offset)
    if key == "DR":
        ext = 1
        for st, cnt in pat:
            ext += (cnt - 1) * abs(st)
        return (t.name, 0, 1, off, off + ext)
    row = 1
    for s in list(t.shape)[1:]:
        row *= s
    p0 = off // row
    f0 = off % row
    st0, c0 = pat[0]
    pstep = max(1, abs(st0) // row) if c0 > 1 else 1
    p1 = p0 + (c0 - 1) * pstep + 1
    ext = 1
    for st, cnt in pat[1:]:
        ext += (cnt - 1) * abs(st)
    if key == "SB":
        return ("SB", p0, p1, base + f0 * isz, base + (f0 + ext) * isz)
    b0 = base + f0 * isz
    b1 = base + (f0 + ext) * isz
    return ("PS", 0, 128, (b0 // 2048) * 2048, ((b1 - 1) // 2048 + 1) * 2048)


class Kern:
    def __init__(self, nc, same_engine_sync=True, max_lanes=48):
        self.nc = nc
        self.ops = {e: [] for e in ("pe", "act", "dve", "pool", "sp")}
        self.known = {e: {} for e in self.ops}
        self.tl_len = {}
        self.hist = {}
        self.same_engine_sync = same_engine_sync
        self.lanes = []
        self.max_lanes = max_lanes
        self.lane_issued = {}
        self.lane_last = {}
        self.out_dmas = []
        self.nwaits = 0

    def _deps_for(self, op, reads, writes):
        deps = []
        for ap, is_w in [(a, False) for a in reads] + [(a, True) for a in writes]:
            name, p0, p1, f0, f1 = _region(ap)
            lst = self.hist.get(name, [])
            keep = []
            for ent in lst:
                (q0, q1, g0, g1), eop, ew = ent
                ov = not (q1 <= p0 or p1 <= q0 or g1 <= f0 or f1 <= g0)
                if name == "PS":
                    if ov and eop is not op and (is_w or ew or eop.tl != op.tl):
                        deps.append((eop, ew, is_w))
                    if ov and q0 >= p0 and q1 <= p1 and g0 >= f0 and g1 <= f1 and eop is not op:
                        continue
                    keep.append(ent)
                    continue
                if ov and (is_w or ew) and eop is not op:
                    deps.append((eop, ew, is_w))
                if is_w and ov and q0 >= p0 and q1 <= p1 and g0 >= f0 and g1 <= f1:
                    continue
                if (not is_w) and (not ew) and (not eop.is_dma) and eop.tl == op.tl \
                        and (q0, q1, g0, g1) == (p0, p1, f0, f1):
                    continue
                keep.append(ent)
            keep.append([(p0, p1, f0, f1), op, is_w])
            self.hist[name] = keep
        return deps

    def _wait_on(self, op, d, known):
        if known.get(d.tl, 0) >= d.seq:
            return
        d.signal = True
        op.waits.append(d)
        self.nwaits += 1
        for k, v in d.clock.items():
            if known.get(k, 0) < v:
                known[k] = v

    def _add(self, eng, fn, reads, writes, is_dma=False):
        op = Op(eng, fn)
        op.is_dma = is_dma
        self.norder = getattr(self, 'norder', 0) + 1
        op.order = self.norder
        known = self.known[eng]
        if is_dma:
            lane = None
            for L in self.lanes:
                if known.get(L, 0) >= self.lane_issued[L]:
                    lane = L
                    break
            if lane is None:
                if len(self.lanes) < self.max_lanes:
                    lane = "lane%d" % len(self.lanes)
                    self.lanes.append(lane)
                    self.lane_issued[lane] = 0
                else:
                    lane = min(self.lanes, key=lambda L: self.lane_last[L].order)
                    self._wait_on(op, self.lane_last[lane], known)
            op.tl = lane
        else:
            op.tl = eng
        deps = self._deps_for(op, reads, writes)
        if eng == "pe" and getattr(self, "_pe_pending", None) is not None:
            self._wait_on(op, self._pe_pending, known)
            self._pe_pending = None
        for d, d_w, me_w in deps:
            if d.tl == op.tl and not is_dma:
                if eng == "pe" and d_w and me_w:
                    continue
                if not self.same_engine_sync:
                    continue
            self._wait_on(op, d, known)
        n = self.tl_len.get(op.tl, 0) + 1
        self.tl_len[op.tl] = n
        op.seq = n
        if is_dma:
            self.lane_issued[op.tl] = n
            self.lane_last[op.tl] = op
            op.signal = True
        ck = dict(known)
        ck[op.tl] = n
        op.clock = ck
        self.ops[eng].append(op)
        return op

    @staticmethod
    def _pe_mode(stat):
        shp = list(stat.shape)
        k = shp[0]
        m = 1
        for s_ in shp[1:]:
            m *= s_
        r = lambda v: 32 if v <= 32 else (64 if v <= 64 else 128)
        return (r(k), r(m))

    def _pe_drain(self, stat):
        mode = self._pe_mode(stat)
        last = getattr(self, "_pe_last", None)
        self._pe_pending = None
        if last is not None and getattr(self, "_pe_lastmode", None) != mode:
            self._pe_pending = last
        self._pe_lastmode = mode

    def matmul(self, out, lhsT, rhs, start=True, stop=True, **kw):
        self._pe_drain(lhsT)
        op = self._add("pe", lambda e: e.matmul(out, lhsT, rhs, start=start, stop=stop, **kw),
                       [lhsT, rhs], [out])
        self._pe_last = op
        return op

    def transpose(self, out, in_, ident):
        self._pe_drain(in_)
        op = self._add("pe", lambda e: e.transpose(out, in_, ident), [in_, ident], [out])
        self._pe_last = op
        return op

    def act(self, out, in_, func, bias=None, scale=1.0, accum_out=None, eng="act"):
        reads = [in_]
        kw = {}
        if bias is not None:
            kw["bias"] = bias
            if not isinstance(bias, (int, float)):
                reads.append(bias)
        if not isinstance(scale, (int, float)):
            reads.append(scale)
        kw["scale"] = scale
        writes = [out]
        if accum_out is not None:
            kw["accum_out"] = accum_out
            writes.append(accum_out)
        return self._add(eng, lambda e: e.activation(out, in_, func, **kw), reads, writes)

    def tt(self, out, in0, in1, op, eng="dve"):
        return self._add(eng, lambda e: e.tensor_tensor(out, in0, in1, op), [in0, in1], [out])

    def ts(self, out, in0, s1, s2, op0, op1=None, eng="dve", accum_out=None):
        reads = [in0]
        for s in (s1, s2):
            if s is not None and not isinstance(s, (int, float)):
                reads.append(s)
        writes = [out]
        kw = {}
        if accum_out is not None:
            kw["accum_out"] = accum_out
            writes.append(accum_out)
        if op1 is None:
            return self._add(eng, lambda e: e.tensor_scalar(out, in0, s1, None, op0, **kw), reads, writes)
        return self._add(eng, lambda e: e.tensor_scalar(out, in0, s1, s2, op0, op1, **kw), reads, writes)

    def stt(self, out, in0, scalar, in1, op0, op1, eng="dve"):
        reads = [in0, in1]
        if not isinstance(scalar, (int, float)):
            reads.append(scalar)
        return self._add(eng, lambda e: e.scalar_tensor_tensor(out, in0, scalar, in1, op0, op1), reads, [out])

    def copy(self, out, in_, eng="dve"):
        if eng == "act":
            return self._add(eng, lambda e: e.copy(out, in_), [in_], [out])
        return self._add(eng, lambda e: e.tensor_copy(out, in_), [in_], [out])

    def memset(self, ap, val, eng="pool"):
        return self._add(eng, lambda e: e.memset(ap, val), [], [ap])

    def recip(self, out, in_):
        return self._add("dve", lambda e: e.reciprocal(out, in_), [in_], [out])

    def reduce(self, out, in_, op, axis=AX.X, eng="dve"):
        return self._add(eng, lambda e: e.tensor_reduce(out, in_, axis, op), [in_], [out])

    def bn_stats(self, out, in_):
        return self._add("dve", lambda e: e.bn_stats(out, in_), [in_], [out])

    def bn_aggr(self, out, in_):
        return self._add("dve", lambda e: e.bn_aggr(out, in_), [in_], [out])

    def dma(self, out, in_, q="sp", is_output=False, **kw):
        op = self._add(q, lambda e: e.dma_start(out=out, in_=in_, **kw), [in_], [out], is_dma=True)
        if is_output:
            self.out_dmas.append(op)
        return op

    def emit(self):
        nc = self.nc
        fin = Op("sp", None)
        fin.tl = "sp"
        for L in self.lanes:
            fin.waits.append(self.lane_last[L])
        self.ops["sp"].append(fin)
        for e in COMPUTE:
            c = 0
            for op in self.ops[e]:
                if op.is_dma:
                    continue
                if op.signal:
                    c += 1
                    op.count = c
        with contextlib.ExitStack() as es:
            sems = {}
            for e in COMPUTE:
                sems[e] = es.enter_context(nc.semaphore("s_" + e))
            for L in self.lanes:
                sems[L] = es.enter_context(nc.semaphore("s_" + L))
            block = es.enter_context(nc.Block())

            def run(eng_name):
                def body(e):
                    for op in self.ops[eng_name]:
                        waits = list(op.waits)
                        fused = None
                        if waits and op.fn is not None and not op.is_dma:
                            fused = waits.pop()
                        for d in waits:
                            if d.is_dma:
                                e.wait_ge(sems[d.tl], 16 * d.seq)
                            else:
                                e.wait_ge(sems[d.tl], d.count)
                        if op.fn is None:
                            continue
                        ins = op.fn(e)
                        if fused is not None:
                            ins._wait_ge(sems[fused.tl], 16 * fused.seq if fused.is_dma else fused.count)
                        if op.is_dma:
                            ins.then_inc(sems[op.tl], 16)
                        elif op.signal:
                            ins.then_inc(sems[op.tl], 1)
                return body

            block.tensor(run("pe"))
            block.scalar(run("act"))
            block.vector(run("dve"))
            block.gpsimd(run("pool"))
            block.sync(run("sp"))


class Arena:
    def __init__(self, nc, lo=16640, hi=229376):
        self.nc = nc
        self.lo = lo
        self.hi = hi
        self.top = lo
        self.n = 0
        self.peak = lo

    def alloc(self, name, free_shape, dtype, parts=128):
        n = 1
        for s in free_shape:
            n *= s
        nbytes = n * _isz(dtype)
        off = (self.top + 63) // 64 * 64
        assert off + nbytes <= self.hi, "SBUF arena overflow: %s needs %d at %d" % (name, nbytes, off)
        self.n += 1
        h = self.nc.alloc_sbuf_tensor_at("%s_%d" % (name, self.n), [parts] + list(free_shape), dtype, offset=off)
        REG[h.name] = ("SB", off, _isz(dtype))
        self.top = off + nbytes
        self.peak = max(self.peak, self.top)
        return h

    def mark(self):
        return self.top

    def release(self, m):
        self.top = m


W_NAMES = {
    0: ["w_ada", "b_ada", "ffn1_w_in", "ffn1_w_out", "mix_w_in", "mix_w_out", "ffn2_w_in", "ffn2_w_out"],
    1: ["w_ada", "b_ada", "ffn1_w_in", "ffn1_w_out", "mix_w_in", "mix_w_out", "ffn2_w_in", "ffn2_w_out"],
}
W_SHAPES = {
    "w_ada": [D, 9 * D], "b_ada": [1, 9 * D], "ffn1_w_in": [D, 2 * DFF], "ffn1_w_out": [DFF, D],
    "ffn2_w_in": [D, 2 * DFF], "ffn2_w_out": [DFF, D], "mix_w_out": [D, D],
}


class Builder:
    def __init__(self, upto="all", debug=False):
        self.upto = upto
        self.debug = debug
        REG.clear()
        self.nc = nc = bass.Bass("TRN2", target_bir_lowering=False)
        self.K = Kern(nc)
        self.A = Arena(nc)
        self.din = {}
        self.cnt = 0

    def dram_in(self, name, shape, dtype=F32):
        h = self.nc.dram_tensor(name, list(shape), dtype, kind="ExternalInput")
        REG[h.name] = ("DR", 0, _isz(dtype))
        self.din[name] = h.ap()
        return h.ap()

    def dram_out(self, name, shape, dtype=F32):
        h = self.nc.dram_tensor(name, list(shape), dtype, kind="ExternalOutput")
        REG[h.name] = ("DR", 0, _isz(dtype))
        return h.ap()

    def dram_tmp(self, name, shape, dtype=F32):
        kind = "ExternalOutput" if self.debug else "Internal"
        h = self.nc.dram_tensor(name, list(shape), dtype, kind=kind)
        REG[h.name] = ("DR", 0, _isz(dtype))
        return h.ap()

    def W(self, l, nm):
        if (l, nm) not in self.Wd:
            if nm == "mix_w_in":
                shp = [D, 3104] if l == 0 else [D, 3072]
            else:
                shp = W_SHAPES[nm]
            self.Wd[(l, nm)] = self.dram_in("l%d_%s" % (l, nm), shp)
        return self.Wd[(l, nm)]

    def psum(self, name, shape, dtype):
        h = self.nc.alloc_psum_tensor(name, list(shape), dtype)
        REG[h.name] = ("PS", int(self.nc.lookup_mloc(h).bank) * 2048, _isz(dtype))
        return h

    def build(self):
        nc, K, A = self.nc, self.K, self.A
        x_d = self.dram_in("x", [SEQ, D])
        ctx_d = self.dram_in("ctx", [CTX, D])
        c_d = self.dram_in("c", [1, D])
        cctx_d = self.dram_in("c_ctx", [1, D])
        self.Wd = {}
        ident_d = self.dram_in("ident", [128, 128])
        self.out_d = self.dram_out("out", [SEQ, D])
        self.XS = self.dram_tmp("xs", [T, D])
        self.MODD = [self.dram_tmp("modd%d" % l, [2, 9 * D]) for l in (0, 1)]

        self.psA = self.psum("psA", [128, 8 * 512], F32)
        self.psT = self.psA[:, 6 * 512:8 * 512].bitcast(BF16)

        self.ident_bf = A.alloc("ident_bf", [128], BF16)
        self.ident_f = A.alloc("ident_f", [128], F32)
        K.dma(self.ident_f[:], ident_d[:, :])
        K.dma(self.ident_bf[:], ident_d[:, :], q="pool")
        self.modT = [A.alloc("modT%d" % l, [72, 2], F32) for l in (0, 1)]
        self.ones_f = A.alloc("ones_f", [128], F32, parts=1)
        K.memset(self.ones_f[:], 1.0)
        self.grow = [A.alloc("grow", [D], F32, parts=1)] * 2
        self.xinT = A.alloc("xinT", [8, T], BF16)
        self.base_mark = A.mark()

        self.stage_mod(0)
        self.stage_mod(1)
        if self.upto == "mod":
            return self.finish()
        self.stage_first_xin()
        if self.upto == "xin":
            dbg = self.dram_out("dbg_xinT", [128, 8 * T], BF16)
            K.dma(dbg[:, :], self.xinT[:].rearrange("p a b -> p (a b)"))
            return self.finish()
        self.ffn(0, "ffn1", sub=0, tiles=list(range(NT)), first=True, nxt=(0, 1))
        if self.upto == "l0ffn1":
            return self.finish()
        self.mixer0()
        if self.upto in ("l0na", "l0gla", "l0mix"):
            return self.finish()
        self.ffn(0, "ffn2", sub=2, tiles=list(range(NT)), first=False, nxt=(1, 0))
        if self.upto == "l0":
            return self.finish()
        self.ffn(1, "ffn1", sub=0, tiles=list(range(NT)), first=False, nxt=(1, 1))
        if self.upto == "l1ffn1":
            return self.finish()
        self.mixer1()
        if self.upto in ("l1diff", "l1mix"):
            return self.finish()
        self.ffn(1, "ffn2", sub=2, tiles=list(range(16)), first=False, nxt="final")
        return self.finish()

    def finish(self):
        self.K.emit()
        return self.nc

    def bank(self, i, n=512):
        return self.psA[:, i * 512:i * 512 + n]

    def stage_mod(self, l):
        K, A = self.K, self.A
        m0 = A.mark()
        w_ada = self.W(l, "w_ada")
        b_ada = self.W(l, "b_ada")
        craw = A.alloc("craw", [8, 2], F32)
        sc = A.alloc("sc", [8, 2], F32)
        K.dma(craw[:, :, 0], self.din["c"].rearrange("o (kc p) -> p (o kc)", p=128), allow_slow_non_contiguous=True)
        K.dma(craw[:, :, 1], self.din["c_ctx"].rearrange("o (kc p) -> p (o kc)", p=128), allow_slow_non_contiguous=True)
        K.act(sc[:], craw[:], AF.Silu)
        modrow = A.alloc("modrow", [9 * D], F32, parts=2)
        brow = A.alloc("brow", [9 * D], F32, parts=2)
        K.dma(brow[0:1, :], b_ada[:, :])
        K.dma(brow[1:2, :], b_ada[:, :])
        wb = [A.alloc("wada%d" % i, [8, 1024], F32) for i in range(2)]
        for cg in range(9):
            buf = wb[cg % 2]
            for hf in range(2):
                K.dma(buf[:, hf * 4:(hf + 1) * 4, :],
                      w_ada[hf * 512:(hf + 1) * 512, cg * 1024:(cg + 1) * 1024].rearrange("(kc p) n -> p kc n", p=128),
                      q=("sp", "act", "pool", "sp")[(2 * cg + hf) % 4])
            for half in range(2):
                ps = self.psA[0:2, (4 + half) * 512:(5 + half) * 512]
                for kc in range(8):
                    K.matmul(ps, sc[:, kc, :], buf[:, kc, half * 512:(half + 1) * 512],
                             start=(kc == 0), stop=(kc == 7))
                sl = slice(cg * 1024 + half * 512, cg * 1024 + half * 512 + 512)
                K.tt(modrow[:, sl], ps, brow[:, sl], ALU.add)
        for idx in (1, 4, 7):
            K.ts(modrow[:, idx * D:(idx + 1) * D], modrow[:, idx * D:(idx + 1) * D], 1.0, None, ALU.add, eng="pool")
        for idx in (2, 5, 8):
            K.ts(modrow[:, idx * D:(idx + 1) * D], modrow[:, idx * D:(idx + 1) * D],
                 (1.0 if idx == 5 else 0.5) / ALPHA, None, ALU.mult, eng="pool")
        K.dma(self.MODD[l][:, :], modrow[:])
        pst = self.psA[:, 4 * 512:4 * 512 + 144]
        for j in range(72):
            K.transpose(pst[:, 2 * j:2 * j + 2], modrow[:, j * 128:(j + 1) * 128], self.ident_f[0:2, 0:2])
        K.copy(self.modT[l][:].rearrange("p j r -> p (j r)"), pst)
        A.release(m0)

    def load_gate_tiles(self, l, idx, tiles2):
        K = self.K
        for r in range(2):
            K.dma(self.grow[r][:], self.MODD[l][r:r + 1, idx * D:(idx + 1) * D])
            for half in range(2):
                ps = self.bank(4 + half)
                K.matmul(ps, self.ones_f[0:1, :], self.grow[r][0:1, half * 512:(half + 1) * 512])
                K.copy(tiles2[r][:, half * 512:(half + 1) * 512], ps, eng="act")

    def make_xinT(self, xb, t, l, sub):
        K = self.K
        r = 0 if t < 16 else 1
        self.cnt += 1
        pt = self.psT[:, (self.cnt % 2) * 1024:(self.cnt % 2) * 1024 + 1024]
        for kc in range(8):
            K.transpose(pt[:, kc * 128:(kc + 1) * 128], xb[:, kc * 128:(kc + 1) * 128], self.ident_bf[:])
        mt = self.modT[l]
        for kc in range(8):
            if kc < 8:
                K.act(self.xinT[:, kc, t * 128:(t + 1) * 128], pt[:, kc * 128:(kc + 1) * 128], AF.Identity,
                      bias=mt[:, (3 * sub) * 8 + kc, r:r + 1], scale=mt[:, (3 * sub + 1) * 8 + kc, r:r + 1])
            else:
                K.ts(self.xinT[:, kc, t * 128:(t + 1) * 128], pt[:, kc * 128:(kc + 1) * 128],
                     mt[:, (3 * sub + 1) * 8 + kc, r:r + 1], mt[:, (3 * sub) * 8 + kc, r:r + 1], ALU.mult, ALU.add)

    def src_rows(self, t, first):
        if first:
            if t < 16:
                return self.din["x"][t * 128:(t + 1) * 128, :]
            return self.din["ctx"][(t - 16) * 128:(t - 15) * 128, :]
        return self.XS[t * 128:(t + 1) * 128, :]

    def stage_first_xin(self):
        K, A = self.K, self.A
        m0 = A.mark()
        xf = [A.alloc("xf%d" % i, [D], F32) for i in range(2)]
        xb = [A.alloc("xb%d" % i, [D], BF16) for i in range(2)]
        for t in range(NT):
            K.dma(xf[t % 2][:], self.src_rows(t, True))
            K.copy(xb[t % 2][:], xf[t % 2][:], eng="pool")
            self.make_xinT(xb[t % 2], t, 0, 0)
        A.release(m0)

    def epi_alloc(self):
        A = self.A
        E = {}
        E["gate"] = [A.alloc("gate%d" % r, [D], F32) for r in range(2)]
        E["xr"] = [A.alloc("xr%d" % i, [D], F32) for i in range(4)]
        E["t1"] = [A.alloc("t1_%d" % i, [D], F32) for i in range(2)]
        E["xn"] = [A.alloc("xn%d" % i, [D], F32) for i in range(2)]
        E["xb"] = [A.alloc("xbb%d" % i, [D], BF16) for i in range(2)]
        E["st"] = [A.alloc("st%d" % i, [16], F32) for i in range(2)]
        E["i"] = 0
        return E

    def epi_front(self, E, Y, t, xr):
        K = self.K
        i = E["i"]
        E["i"] += 1
        r = 0 if t < 16 else 1
        t1 = E["t1"][i % 2]
        K.tt(t1[:], Y, E["gate"][r][:], ALU.mult)
        K.tt(xr[:], t1[:], xr[:], ALU.add, eng="pool")
        return i

    def epi_back(self, E, i, t, xr, partial, dst, nxt):
        K = self.K
        if partial:
            K.dma(dst, xr[:], q="sp")
            return
        st = E["st"][i % 2]
        K.bn_stats(st[:, 0:6], xr[:, 0:512])
        K.bn_stats(st[:, 6:12], xr[:, 512:1024])
        K.bn_aggr(st[:, 12:14], st[:, 0:12].rearrange("p (a b) -> p a b", b=6))
        K.ts(st[:, 14:15], st[:, 13:14], LN_EPS / (ALPHA * ALPHA), None, ALU.add)
        K.act(st[:, 14:15], st[:, 14:15], AF.Sqrt)
        K.recip(st[:, 14:15], st[:, 14:15])
        K.stt(st[:, 15:16], st[:, 12:13], -1.0, st[:, 14:15], ALU.mult, ALU.mult)
        xn = E["xn"][i % 2]
        K.ts(xn[:], xr[:], st[:, 14:15], st[:, 15:16], ALU.mult, ALU.add)
        K.dma(dst, xn[:], q="sp", is_output=(nxt == "final"))
        if nxt is None or nxt == "final":
            return
        xb = E["xb"][i % 2]
        K.act(xb[:], xr[:], AF.Identity, bias=st[:, 15:16], scale=st[:, 14:15])
        self.make_xinT(xb, t, nxt[0], nxt[1])

    def run_tiles(self, E, tiles, srcs, mm, partial, dsts, nxt):
        K = self.K
        LAG = 2
        NX = 4
        npre = 1
        for k in range(min(npre, len(tiles))):
            K.dma(E["xr"][k % NX][:], srcs[k], q="sp")
        pend = []
        for k, t in enumerate(tiles):
            Y = self.psA[:, (k % 2) * 1024:(k % 2) * 1024 + 1024]
            mm(t, Y)
            if k + npre < len(tiles):
                K.dma(E["xr"][(k + npre) % NX][:], srcs[k + npre], q="sp")
            if len(pend) >= LAG:
                pi_, pt_, pk_ = pend.pop(0)
                self.epi_back(E, pi_, pt_, E["xr"][pk_ % NX], partial, dsts[pk_], nxt)
            i = self.epi_front(E, Y, t, E["xr"][k % NX])
            pend.append((i, t, k))
        for (pi_, pt_, pk_) in pend:
            self.epi_back(E, pi_, pt_, E["xr"][pk_ % NX], partial, dsts[pk_], nxt)

    def outproj(self, l, catT, tiles, nxt, wo=None):
        K, A = self.K, self.A
        m0 = A.mark()
        w_out = self.W(l, "mix_w_out")
        if wo is None:
            wo = A.alloc("wo", [8, D], BF16)
            K.dma(wo[:], w_out.rearrange("(c q) f -> q c f", q=128), q="pool")
        E = self.epi_alloc()
        self.load_gate_tiles(l, 5, E["gate"])
        srcs = [self.src_rows(t, False) for t in tiles]
        dsts = [self.XS[t * 128:(t + 1) * 128, :] for t in tiles]

        def mm(t, Y):
            for half in range(2):
                for c in range(8):
                    K.matmul(Y[:, half * 512:(half + 1) * 512], catT[:, c, t * 128:(t + 1) * 128],
                             wo[:, c, half * 512:(half + 1) * 512], start=(c == 0), stop=(c == 7))
        self.run_tiles(E, tiles, srcs, mm, False, dsts, nxt)
        A.release(m0)

    def proj_fm(self, dst, wt, ncols, scale, groups, bank_i):
        K = self.K
        for gi, (t0, nt) in enumerate(groups):
            ps = self.psA[0:ncols, ((bank_i + gi) % 4) * 512:((bank_i + gi) % 4) * 512 + nt]
            for kc in range(8):
                K.matmul(ps, wt[:, kc, 0:ncols], self.xinT[:, kc, t0:t0 + nt], start=(kc == 0), stop=(kc == 7))
            if scale == 1.0:
                K.copy(dst[0:ncols, t0:t0 + nt], ps, eng="act")
            else:
                K.act(dst[0:ncols, t0:t0 + nt], ps, AF.Identity, scale=scale)

    def na_stage(self, catT):
        K, A = self.K, self.A
        m0 = A.mark()
        w_in = self.W(0, "mix_w_in")
        tab_d = self.dram_in("na_tab", [3, 128, 8 * 14 * 64])
        groups = [(g * 512, 512) for g in range(4)] + [(2048, 256)]
        wq = A.alloc("wq", [8, 128], BF16)
        wk = A.alloc("wk", [8, 128], BF16)
        wv = A.alloc("wv", [8, 128], BF16)
        qT = A.alloc("qT", [T], BF16)
        kT = A.alloc("kT", [T], BF16)
        Vaug = A.alloc("Vaug", [NT, 2, 128], BF16)
        btab = A.alloc("btab", [3, 2, 14 * 64], BF16)
        PT = [A.alloc("PT%d" % i, [140 * 64], BF16) for i in range(2)]
        PTc = [A.alloc("PTc%d" % i, [2, T], BF16) for i in range(2)]
        Rr = [A.alloc("Rr%d" % i, [512], F32) for i in range(2)]
        K.memset(Vaug[:], 1.0)
        qTz = [A.alloc("qTz%d" % i, [T], BF16) for i in range(2)]
        K.memset(qTz[0][64:128, :], 0.0)
        K.memset(qTz[1][0:64, :], 0.0)

        def rrange(i):
            if i <= 7:
                return 0, i + 4
            if i <= 23:
                return i - 3, i + 4
            return i - 3, 31
        tiles_geo = []
        off = 0
        for j in range(16):
            lo = min(rrange(2 * j)[0], rrange(2 * j + 1)[0])
            hi = max(rrange(2 * j)[1], rrange(2 * j + 1)[1])
            var = 0 if j < 4 else (1 if j < 12 else 2)
            tiles_geo.append((lo, hi, off, var))
            off += (hi - lo + 1) * 64
        assert off == 140 * 64
        hcount = 0
        bk = 0
        for hp in range(4):
            K.dma(wq[:], w_in[:, hp * 128:(hp + 1) * 128].rearrange("(kc q) n -> q kc n", q=128), q="pool")
            K.dma(wk[:], w_in[:, 512 + hp * 128:512 + (hp + 1) * 128].rearrange("(kc q) n -> q kc n", q=128), q="pool")
            K.dma(wv[:], w_in[:, 1024 + hp * 128:1024 + (hp + 1) * 128].rearrange("(kc q) n -> q kc n", q=128), q="pool")
            for v in range(3):
                K.dma(btab[:, v, :, :], tab_d[v, :, hp * 2 * 896:(hp + 1) * 2 * 896].rearrange("p (h f) -> p h f", h=2), q="pool")
            self.proj_fm(qT, wq, 128, 0.125, groups, 0)
            self.proj_fm(kT, wk, 128, 1.0, groups, 1)
            K.copy(qTz[0][0:64, :], qT[0:64, :], eng="pool")
            K.copy(qTz[1][64:128, :], qT[64:128, :], eng="pool")
            for g4 in range(0, NT, 4):
                nt4 = min(4, NT - g4)
                ps = self.bank(bk % 4)
                bk += 1
                for tt_ in range(nt4):
                    t = g4 + tt_
                    for kc in range(8):
                        K.matmul(ps[:, tt_ * 128:(tt_ + 1) * 128], self.xinT[:, kc, t * 128:(t + 1) * 128], wv[:, kc, :],
                                 start=(kc == 0), stop=(kc == 7))
                K.copy(Vaug[:, g4:g4 + nt4, :, 0:64],
                       ps[:, 0:nt4 * 128].rearrange("p (a h d) -> p a h d", h=2, d=64), eng="act")
            for h2 in range(2):
                hb = h2 * 64
                hcount += 1
                pt = PT[hcount % 2]
                ptc = PTc[hcount % 2]
                for j in range(16):
                    lo, hi, poff, var = tiles_geo[j]
                    r = lo
                    while r <= hi:
                        nr = min(8, hi - r + 1)
                        ps = self.bank(bk % 4, nr * 64)
                        bk += 1
                        K.matmul(ps, kT[:, j * 128:(j + 1) * 128], qTz[h2][:, r * 64:(r + nr) * 64],
                                 start=True, stop=False)
                        d0 = (r - 2 * j + 3) + 3
                        K.matmul(ps, self.ident_bf[:], btab[:, var, h2, d0 * 64:(d0 + nr) * 64], start=False, stop=True)
                        K.act(pt[:, poff + (r - lo) * 64:poff + (r - lo + nr) * 64], ps, AF.Exp)
                        r += nr
                for ct in range(2):
                    for (t0, nt) in groups:
                        ps = self.bank(bk % 4, nt)
                        bk += 1
                        K.matmul(ps, kT[:, 2048 + ct * 128:2048 + (ct + 1) * 128], qTz[h2][:, t0:t0 + nt])
                        K.act(ptc[:, ct, t0:t0 + nt], ps, AF.Exp)
                for qb, (t0, nt) in enumerate(groups):
                    O = self.bank(4 + (bk % 2), nt)
                    bk += 1
                    K.matmul(O, Vaug[:, 16, h2, :], ptc[:, 0, t0:t0 + nt], start=True, stop=False)
                    last_is_ctx = (qb == 4)
                    K.matmul(O, Vaug[:, 17, h2, :], ptc[:, 1, t0:t0 + nt], start=False, stop=last_is_ctx)
                    if qb < 4:
                        R0, R1 = 8 * qb, 8 * qb + 7
                        js = [j for j in range(16) if not (tiles_geo[j][1] < R0 or tiles_geo[j][0] > R1)]
                        for jj, j in enumerate(js):
                            lo, hi, poff, var = tiles_geo[j]
                            a, b = max(lo, R0), min(hi, R1)
                            K.matmul(O[:, (a - R0) * 64:(b - R0 + 1) * 64], Vaug[:, j, h2, :],
                                     pt[:, poff + (a - lo) * 64:poff + (b - lo + 1) * 64],
                                     start=False, stop=(jj == len(js) - 1))
                    rr = Rr[bk % 2]
                    K.recip(rr[64:128, 0:nt], O[64:128, :])
                    K.tt(catT[hb:hb + 64, hp, t0:t0 + nt], O[0:64, :], rr[64:128, 0:nt], ALU.mult)
        A.release(m0)


    def gla_stage(self, catT):
        K, A = self.K, self.A
        m0 = A.mark()
        w_in = self.W(0, "mix_w_in")
        wg_d = [self.dram_in("gla_wgf", [17, 256]), self.dram_in("gla_wgb", [17, 256])]
        ng_d = self.dram_in("gla_ng", [1, 128])
        tri_d = self.dram_in("gla_tri", [4, 128, 128])
        msk_d = self.dram_in("gla_msk", [2, 128, 128])
        groups = [(g * 512, 512) for g in range(4)] + [(2048, 256)]
        bkc = [0]

        def nb(n=512):
            bkc[0] += 1
            return self.bank(bkc[0] % 6, n)

        tri = A.alloc("tri", [4, 128], F32)
        K.dma(tri[:], tri_d.rearrange("a s t -> s a t"))
        msk4 = A.alloc("msk4", [2, 2, 128], BF16)
        for dirn in range(2):
            for hh in range(2):
                K.dma(msk4[:, dirn, hh, :], msk_d[dirn, :, :], q="pool")
        ngrow = A.alloc("ngrow", [128], F32, parts=1)
        K.dma(ngrow[:], ng_d[:, :])
        ng2 = A.alloc("ng2", [2, 128], F32)
        psn = nb(128)
        K.matmul(psn, self.ones_f[0:1, :], ngrow[0:1, :])
        for hh in range(2):
            K.copy(ng2[:, hh, :], psn, eng="act")
        wg = [A.alloc("wg%d" % i, [256], F32) for i in range(2)]
        for i in range(2):
            K.memset(wg[i][:], 0.0)
            K.dma(wg[i][0:16, :], wg_d[i][0:16, :])
            K.dma(wg[i][32:33, :], wg_d[i][16:17, :])
        zT = [A.alloc("zT%d" % i, [T], F32) for i in range(2)]
        mz = A.mark()
        wz = A.alloc("wz", [8, 32], BF16)
        K.dma(wz[:], w_in[:, 3072:3104].rearrange("(kc q) n -> q kc n", q=128), q="pool")
        for i in range(2):
            K.memset(zT[i][:], 0.0)
            K.memset(zT[i][32:33, :], 1.0)
            for (t0, nt) in groups:
                ps = nb(nt)[0:16, :]
                for kc in range(8):
                    K.matmul(ps, wz[:, kc, i * 16:(i + 1) * 16], self.xinT[:, kc, t0:t0 + nt], start=(kc == 0), stop=(kc == 7))
                K.copy(zT[i][0:16, t0:t0 + nt], ps, eng="act")
        A.release(mz)

        for p in range(2):
            mp = A.mark()
            vg = A.alloc("vg", [NT, 256], BF16)
            sg = A.alloc("sg", [NT, 256], BF16)
            qtz = A.alloc("qtz", [2, 2, T], BF16)
            ktT = A.alloc("ktT", [2, T], BF16)
            Sin = A.alloc("Sin", [2, NT, 128], BF16)
            S = A.alloc("S", [128], F32)
            K.memset(qtz[64:128, :, 0, :], 0.0)
            K.memset(qtz[0:64, :, 1, :], 0.0, eng="dve")
            m1 = A.mark()
            qgT = A.alloc("qgT", [T], BF16)
            kgT = A.alloc("kgT", [T], BF16)
            kg = A.alloc("kg", [NT, 128], BF16)
            mw = A.mark()
            wgq = A.alloc("wgq", [8, 128], BF16)
            wgk = A.alloc("wgk", [8, 128], BF16)
            wgv = A.alloc("wgv", [8, 256], BF16)
            wgr = A.alloc("wgr", [8, 256], BF16)
            for (wt, c0, n) in ((wgq, 1536 + p * 128, 128), (wgk, 1792 + p * 128, 128),
                                (wgv, 2048 + p * 256, 256), (wgr, 2560 + p * 256, 256)):
                K.dma(wt[:], w_in[:, c0:c0 + n].rearrange("(kc q) n -> q kc n", q=128), q="pool")
            self.proj_fm(qgT, wgq, 128, 0.125, groups, 0)
            self.proj_fm(kgT, wgk, 128, 1.0, groups, 2)
            for t in range(NT):
                tsl = slice(t * 128, (t + 1) * 128)
                pk = nb(128)
                for kc in range(8):
                    K.matmul(pk, self.xinT[:, kc, tsl], wgk[:, kc, :], start=(kc == 0), stop=(kc == 7))
                K.copy(kg[:, t, :], pk, eng="act")
                pv = nb(256)
                for kc in range(8):
                    K.matmul(pv, self.xinT[:, kc, tsl], wgv[:, kc, :], start=(kc == 0), stop=(kc == 7))
                K.copy(vg[:, t, :], pv, eng="dve")
                pr = nb(256)
                for kc in range(8):
                    K.matmul(pr, self.xinT[:, kc, tsl], wgr[:, kc, :], start=(kc == 0), stop=(kc == 7))
                K.act(sg[:, t, :], pr, AF.Silu)
            A.release(mw)
            tmps = {}
            for nm, dt_ in (("e1", F32), ("sp", F32), ("Eb", F32), ("Enb", F32), ("Ed", F32), ("ku", BF16)):
                tmps[nm] = [A.alloc("%s%d" % (nm, i), [128], dt_) for i in range(2)]
            its = []
            for dirn in range(2):
                order = ([16, 17] + list(range(16))) if dirn == 0 else ([17, 16] + list(range(15, -1, -1)))
                its += [(dirn, n, i_ == 0) for i_, n in enumerate(order)]

            def gA(k):
                dirn, n, _ = its[k]
                triA = tri[:, 0 if dirn == 0 else 1, :]
                triB = tri[:, 2 if dirn == 0 else 3, :]
                tsl = slice(n * 128, (n + 1) * 128)
                e1, sp, Eb, Enb, Ed, ku = [tmps[kk][k % 2] for kk in ("e1", "sp", "Eb", "Enb", "Ed", "ku")]
                pz = nb(128)
                K.matmul(pz, zT[dirn][:, tsl], wg[dirn][:, p * 128:(p + 1) * 128])
                K.act(e1[:], pz, AF.Exp, scale=-1.0)
                K.act(sp[:], e1[:], AF.Ln, bias=1.0)
                pb = nb(128)
                K.matmul(pb, sp[:], triA)
                pd = nb(128)
                K.matmul(pd, triB, sp[:])
                K.act(Eb[:], pb, AF.Exp)
                K.act(Enb[:], pb, AF.Exp, scale=-1.0)
                K.act(Ed[:], pd, AF.Exp)
                for hh in range(2):
                    hb = hh * 64
                    K.tt(qtz[hb:hb + 64, dirn, hh, tsl], qgT[hb:hb + 64, tsl], Eb[hb:hb + 64, :], ALU.mult)
                K.tt(ktT[:, dirn, tsl], kgT[:, tsl], Enb[:], ALU.mult)
                K.tt(ku[:], kg[:, n, :], Ed[:], ALU.mult, eng="pool")

            def gB(k):
                dirn, n, first = its[k]
                lastcol = 127 if dirn == 0 else 0
                Eb, ku = tmps["Eb"][k % 2], tmps["ku"][k % 2]
                if first:
                    K.memset(S[:], 0.0)
                K.copy(Sin[:, dirn, n, :], S[:], eng="pool")
                pu = nb(256)
                K.matmul(pu, ku[:], vg[:, n, :])
                for hh in range(2):
                    hb = hh * 64
                    K.stt(S[hb:hb + 64, :], S[hb:hb + 64, :], Eb[hb:hb + 64, lastcol:lastcol + 1],
                          pu[hb:hb + 64, hh * 128:(hh + 1) * 128], ALU.mult, ALU.add)
            gA(0)
            for k in range(len(its)):
                if k + 1 < len(its):
                    gA(k + 1)
                gB(k)
            A.release(m1)
            attT = [A.alloc("attT%d" % i, [2, 2, 128], BF16) for i in range(2)]
            ss = [A.alloc("ss%d" % i, [2], F32) for i in range(2)]
            junk = A.alloc("junk", [128], BF16)
            tmpo = [A.alloc("tmpo%d" % i, [256], F32) for i in range(2)]
            og = [A.alloc("og%d" % i, [256], BF16) for i in range(2)]
            Obank = {}

            def hA(n):
                tsl = slice(n * 128, (n + 1) * 128)
                at = attT[n % 2]
                for dirn in range(2):
                    pa = nb(256)
                    for hh in range(2):
                        K.matmul(pa[:, hh * 128:(hh + 1) * 128], ktT[:, dirn, tsl], qtz[:, dirn, hh, tsl],
                                 start=(hh == 0), stop=(hh == 1))
                    K.tt(at[:, dirn, :, :], pa.rearrange("p (h t) -> p h t", h=2), msk4[:, dirn, :, :], ALU.mult)
                O = nb(256)
                Obank[n] = O
                first = True
                for hh in range(2):
                    for dirn in range(2):
                        K.matmul(O[:, hh * 128:(hh + 1) * 128], at[:, dirn, hh, :], vg[:, n, hh * 128:(hh + 1) * 128],
                                 start=first, stop=False)
                        first = False
                        K.matmul(O[:, hh * 128:(hh + 1) * 128], qtz[:, dirn, hh, tsl], Sin[:, dirn, n, :],
                                 start=False, stop=(dirn == 1))

            def hB(n):
                tsl = slice(n * 128, (n + 1) * 128)
                O = Obank.pop(n)
                s_ = ss[n % 2]
                for hh in range(2):
                    K.act(junk[:], O[:, hh * 128:(hh + 1) * 128], AF.Square, accum_out=s_[:, hh:hh + 1])
                K.ts(s_[:], s_[:], 1.0 / 128.0, LN_EPS, ALU.mult, ALU.add)
                K.act(s_[:], s_[:], AF.Sqrt)
                K.recip(s_[:], s_[:])
                tm = tmpo[n % 2]
                for hh in range(2):
                    K.stt(tm[:, hh * 128:(hh + 1) * 128], O[:, hh * 128:(hh + 1) * 128], s_[:, hh:hh + 1],
                          sg[:, n, hh * 128:(hh + 1) * 128], ALU.mult, ALU.mult)
                o_ = og[n % 2]
                K.tt(o_[:], tm[:], ng2[:].rearrange("p a b -> p (a b)"), ALU.mult, eng="pool")
                self.cnt += 1
                pt = self.psT[:, (self.cnt % 2) * 1024:(self.cnt % 2) * 1024 + 256]
                for hh in range(2):
                    K.transpose(pt[:, hh * 128:(hh + 1) * 128], o_[:, hh * 128:(hh + 1) * 128], self.ident_bf[:])
                K.copy(catT[:, 4 + 2 * p:6 + 2 * p, tsl], pt.rearrange("p (h t) -> p h t", h=2), eng="act")
            hA(0)
            for n in range(NT):
                if n + 1 < NT:
                    hA(n + 1)
                hB(n)
            A.release(mp)
        A.release(m0)


    def diff_stage(self, catT):
        K, A = self.K, self.A
        m0 = A.mark()
        LI = 0.8 - 0.6 * math.exp(-0.3)
        w_in = self.W(1, "mix_w_in")
        cos_d = self.dram_in("rope_cos", [128, SEQ])
        sin_d = self.dram_in("rope_sin", [128, SEQ])
        perm_d = self.dram_in("rope_perm", [128, 128])
        lam_d = self.dram_in("lamv", [1, 256])
        sub_d = self.dram_in("subln_g", [1, 128])
        xgroups = [(g * 512, 512) for g in range(4)]
        cosT = A.alloc("cosT", [SEQ], F32)
        sinT = A.alloc("sinT", [SEQ], F32)
        K.dma(cosT[:], cos_d[:, :])
        K.dma(sinT[:], sin_d[:, :])
        perm = A.alloc("perm", [128], BF16)
        K.dma(perm[:], perm_d[:, :], q="pool")
        lamv = A.alloc("lamv", [256], F32, parts=1)
        K.dma(lamv[:], lam_d[:, :])
        prod = A.alloc("prod", [128], F32, parts=1)
        K.tt(prod[0:1, 0:64], lamv[0:1, 0:64], lamv[0:1, 64:128], ALU.mult)
        K.tt(prod[0:1, 64:128], lamv[0:1, 128:192], lamv[0:1, 192:256], ALU.mult)
        s12 = A.alloc("s12", [4], F32, parts=1)
        K.reduce(s12[0:1, 0:2], prod[0:1, :].rearrange("p (a b) -> p a b", a=2), ALU.add)
        K.act(s12[0:1, 0:2], s12[0:1, 0:2], AF.Exp)
        K.tt(s12[0:1, 2:3], s12[0:1, 0:1], s12[0:1, 1:2], ALU.subtract)
        K.ts(s12[0:1, 3:4], s12[0:1, 2:3], -1.0, -LI, ALU.mult, ALU.add)
        nlam = A.alloc("nlam", [1], F32)
        psl = self.bank(5, 2)
        K.matmul(psl[:, 0:1], self.ones_f[0:1, :], s12[0:1, 3:4])
        K.copy(nlam[:], psl[:, 0:1], eng="act")
        subrow = A.alloc("subrow", [128], F32, parts=1)
        K.dma(subrow[:], sub_d[:, :])
        gsub = A.alloc("gsub", [128], F32)
        psg = self.bank(4, 128)
        K.matmul(psg, self.ones_f[0:1, :], subrow[0:1, :])
        K.act(gsub[:], psg, AF.Identity, scale=(1.0 - LI))

        wq = [A.alloc("dwq%d" % i, [8, 128], BF16) for i in range(2)]
        wk = [A.alloc("dwk%d" % i, [8, 128], BF16) for i in range(2)]
        wv = [A.alloc("dwv%d" % i, [8, 128], BF16) for i in range(2)]
        qTz = [[A.alloc("dqTz%d_%d" % (bb, i), [SEQ], BF16) for i in range(2)] for bb in range(2)]
        kT = [A.alloc("dkT%d" % bb, [T], BF16) for bb in range(2)]
        Vaug = [A.alloc("dVaug%d" % bb, [NT, 128], BF16) for bb in range(2)]
        for bb in range(2):
            K.memset(qTz[bb][0][64:128, :], 0.0)
            K.memset(qTz[bb][1][0:64, :], 0.0)
        ub = [A.alloc("ub%d" % i, [512], BF16) for i in range(2)]
        t1 = [A.alloc("rt1_%d" % i, [512], F32) for i in range(2)]
        t2 = [A.alloc("rt2_%d" % i, [512], F32) for i in range(2)]
        Pt = [A.alloc("Pt%d" % i, [512], BF16) for i in range(4)]
        o1 = A.alloc("do1", [512], F32)
        o2s = [A.alloc("do2_%d" % i, [512], F32) for i in range(2)]
        sqs = [A.alloc("dsq%d" % i, [512], BF16) for i in range(2)]
        deferred = []
        fi = 0
        Rt = A.alloc("dRt", [512], F32)
        rs = A.alloc("drs", [512], F32)
        onesb = A.alloc("onesb", [128], BF16)
        K.memset(onesb[:], 1.0)
        gcol = A.alloc("gcol", [1], F32)
        K.dma(gcol[:], sub_d.rearrange("o p -> p o"), allow_slow_non_contiguous=True)
        gvec = A.alloc("gvec", [1], F32)
        K.ts(gvec[:], gcol[:], (1.0 - LI), None, ALU.mult)
        bk = 0
        pi = 0
        rc = [0]

        def proj_steps(h):
            b = h % 2
            steps = []

            def wload():
                for (wt, c0) in ((wq[b], h * 128), (wk[b], 1024 + h * 128), (wv[b], 2048 + h * 128)):
                    K.dma(wt[:], w_in[:, c0:c0 + 128].rearrange("(kc q) n -> q kc n", q=128), q="pool")
            steps.append(wload)
            for which in range(2):
                for (t0, nt) in xgroups:
                    def rope(which=which, t0=t0, nt=nt):
                        wt = (wq, wk)[which][b]
                        rc[0] += 1
                        ri = rc[0]
                        ps = self.bank(7, nt)
                        for kc in range(8):
                            K.matmul(ps, wt[:, kc, :], self.xinT[:, kc, t0:t0 + nt], start=(kc == 0), stop=(kc == 7))
                        u_b, a1, a2 = ub[ri % 2], t1[ri % 2], t2[ri % 2]
                        K.copy(u_b[:], ps, eng="dve")
                        K.tt(a1[:], ps, cosT[:, t0:t0 + nt], ALU.mult)
                        pr = self.bank(7, nt)
                        K.matmul(pr, perm[:], u_b[:])
                        K.tt(a2[:], pr, sinT[:, t0:t0 + nt], ALU.mult)
                        if which == 0:
                            for m in range(2):
                                K.tt(qTz[b][m][m * 64:(m + 1) * 64, t0:t0 + nt], a1[m * 64:(m + 1) * 64, :],
                                     a2[m * 64:(m + 1) * 64, :], ALU.add, eng="pool")
                        else:
                            K.tt(kT[b][:, t0:t0 + nt], a1[:], a2[:], ALU.add, eng="pool")
                    steps.append(rope)

            def ctxk():
                ps = self.bank(7, 256)
                for kc in range(8):
                    K.matmul(ps, wk[b][:, kc, :], self.xinT[:, kc, 2048:2304], start=(kc == 0), stop=(kc == 7))
                K.copy(kT[b][:, 2048:2304], ps, eng="dve")
            steps.append(ctxk)
            for g4 in range(0, NT, 4):
                def vproj(g4=g4):
                    nt4 = min(4, NT - g4)
                    ps = self.bank(7)
                    for tt_ in range(nt4):
                        t = g4 + tt_
                        for kc in range(8):
                            K.matmul(ps[:, tt_ * 128:(tt_ + 1) * 128], self.xinT[:, kc, t * 128:(t + 1) * 128], wv[b][:, kc, :],
                                     start=(kc == 0), stop=(kc == 7))
                    K.copy(Vaug[b][:, g4:g4 + nt4, 0:128], ps[:, 0:nt4 * 128].rearrange("p (a e) -> p a e", e=128), eng="dve")
                steps.append(vproj)
            return steps

        for st_ in proj_steps(0):
            st_()
        for h in range(8):
            b = h % 2
            nsteps = proj_steps(h + 1) if h < 7 else []
            for qg in range(4):
                q0 = qg * 512
                for m in range(2):
                    bk += 1
                    O = self.bank(2 + 2 * (bk % 2))
                    Dn = self.bank(3 + 2 * (bk % 2))

                    def S_(kt):
                        ps_ = self.bank((0, 1, 6)[kt % 3])
                        K.matmul(ps_, kT[b][:, kt * 128:(kt + 1) * 128], qTz[b][m][:, q0:q0 + 512])
                        return ps_
                    sq_ = [S_(0), S_(1)]
                    for kt in range(NT):
                        cur = sq_.pop(0)
                        pi += 1
                        P = Pt[pi % 4]
                        K.act(P[:], cur, AF.Exp, scale=0.125)
                        if kt + 2 < NT:
                            sq_.append(S_(kt + 2))
                        K.matmul(O, Vaug[b][:, kt, 0:128], P[:], start=(kt == 0), stop=(kt == NT - 1))
                        K.matmul(Dn, onesb[:], P[:], start=(kt == 0), stop=(kt == NT - 1))
                        if kt == 8 and deferred:
                            deferred.pop(0)()
                        if kt in (3, 13) and nsteps:
                            nsteps.pop(0)()
                    K.recip(Rt[:], Dn)
                    if m == 0:
                        K.tt(o1[:], O, Rt[:], ALU.mult)
                    else:
                        o2 = o2s[fi % 2]
                        sq = sqs[fi % 2]
                        fi += 1
                        K.ts(Rt[:], Rt[:], nlam[:, 0:1], None, ALU.mult)
                        K.tt(o2[:], O, Rt[:], ALU.mult)
                        K.tt(o2[:], o2[:], o1[:], ALU.add, eng="pool")
                        K.tt(sq[:], o2[:], o2[:], ALU.mult, eng="pool")

                        def fin(o2=o2, sq=sq, Dn=Dn, h=h, q0=q0):
                            K.matmul(Dn, onesb[:], sq[:])
                            K.ts(rs[:], Dn, 1.0 / 128.0, LN_EPS, ALU.mult, ALU.add)
                            K.act(rs[:], rs[:], AF.Ln)
                            K.act(rs[:], rs[:], AF.Exp, scale=-0.5)
                            K.stt(catT[:, h, q0:q0 + 512], o2[:], gvec[:, 0:1], rs[:], ALU.mult, ALU.mult)
                        deferred.append(fin)
            while nsteps:
                nsteps.pop(0)()
        while deferred:
            deferred.pop(0)()
        A.release(m0)

    def mixer1(self):
        K, A = self.K, self.A
        m0 = A.mark()
        catT = A.alloc("catT1", [8, T], BF16)
        wo1 = A.alloc("wo1", [8, D], BF16)
        K.dma(wo1[:], self.W(1, "mix_w_out").rearrange("(c q) f -> q c f", q=128), q="pool")
        self.diff_stage(catT)
        if self.upto == "l1diff":
            dbg = self.dram_out("dbg_catT", [128, 8 * T], BF16)
            K.dma(dbg[:, :], catT[:].rearrange("p a b -> p (a b)"))
            return
        self.outproj(1, catT, list(range(16)), nxt=(1, 2), wo=wo1)
        A.release(m0)

    def mixer0(self):
        K, A = self.K, self.A
        m0 = A.mark()
        catT = A.alloc("catT", [8, T], BF16)
        self.na_stage(catT)
        if self.upto == "l0na":
            dbg = self.dram_out("dbg_catT", [128, 8 * T], BF16)
            K.dma(dbg[:, :], catT[:].rearrange("p a b -> p (a b)"))
            return
        self.gla_stage(catT)
        if self.upto == "l0gla":
            dbg = self.dram_out("dbg_catT", [128, 8 * T], BF16)
            K.dma(dbg[:, :], catT[:].rearrange("p a b -> p (a b)"))
            return
        self.outproj(0, catT, list(range(NT)), nxt=(0, 2))
        A.release(m0)

    def ffn(self, l, which, sub, tiles, first, nxt):
        K, A = self.K, self.A
        m0 = A.mark()
        w_in = self.W(l, which + "_w_in")
        w_out = self.W(l, which + "_w_out")
        HT = A.alloc("HT", [11, T], BF16)
        woutb = [A.alloc("wout%d" % p, [11, D], BF16) for p in range(2)]
        wa = [A.alloc("wa%d" % i, [8, 256], BF16) for i in range(2)]
        wg = [A.alloc("wg%d" % i, [8, 256], BF16) for i in range(2)]
        sa = [A.alloc("sa%d" % i, [512], F32) for i in range(2)]
        E = self.epi_alloc()
        self.load_gate_tiles(l, 3 * sub + 2, E["gate"])
        groups = []
        if 0 in tiles:
            groups += [(g * 512, 512) for g in range(4)]
        if 16 in tiles:
            groups += [(2048, 256)]
        ci = 0
        gi = 0
        allg = []
        for p in range(2):
            c0 = 11 * p
            allg += [(p, c0 + 2 * i, 2) for i in range(5)] + [(p, c0 + 10, 1)]
        issued = set()

        def issue(gidx):
            if gidx in issued or gidx >= len(allg):
                return
            issued.add(gidx)
            _, cs_, n_ = allg[gidx]
            K.dma(wa[gidx % 2][:, :, 0:n_ * 128], w_in[:, cs_ * 128:(cs_ + n_) * 128].rearrange("(kc q) n -> q kc n", q=128), q="pool")
            K.dma(wg[gidx % 2][:, :, 0:n_ * 128], w_in[:, DFF + cs_ * 128:DFF + (cs_ + n_) * 128].rearrange("(kc q) n -> q kc n", q=128), q="pool")
        issue(0)
        K.dma(woutb[0][:], w_out[0:1408, :].rearrange("(hc q) f -> q hc f", q=128), q="pool")
        for p in range(2):
            c0 = 11 * p
            for gl in range(6):
                gidx = 6 * p + gl
                _, cs, n = allg[gidx]
                issue(gidx)
                issue(gidx + 1) if gl < 5 else None
                a_t = wa[gidx % 2]
                g_t = wg[gidx % 2]
                if p == 0 and gl == 2:
                    K.dma(woutb[1][:], w_out[1408:2816, :].rearrange("(hc q) f -> q hc f", q=128), q="pool")
                if gl == 5:
                    issue(gidx + 1)
                for j in range(n):
                    hl = cs + j - c0
                    for (t0, nt) in groups:
                        pa = self.bank(2 * (gi % 2), nt)
                        pg = self.bank(2 * (gi % 2) + 1, nt)
                        s_t = sa[gi % 2]
                        gi += 1
                        for kc in range(8):
                            K.matmul(pa, a_t[:, kc, j * 128:(j + 1) * 128], self.xinT[:, kc, t0:t0 + nt],
                                     start=(kc == 0), stop=(kc == 7))
                        for kc in range(8):
                            K.matmul(pg, g_t[:, kc, j * 128:(j + 1) * 128], self.xinT[:, kc, t0:t0 + nt],
                                     start=(kc == 0), stop=(kc == 7))
                        K.act(s_t[:, 0:nt], pa, AF.Silu)
                        K.tt(HT[:, hl, t0:t0 + nt], s_t[:, 0:nt], pg, ALU.mult)
            partial = (p == 0)
            srcs = [self.src_rows(t, first and p == 0) for t in tiles]
            if partial or nxt != "final":
                dsts = [self.XS[t * 128:(t + 1) * 128, :] for t in tiles]
            else:
                dsts = [self.out_d[t * 128:(t + 1) * 128, :] for t in tiles]

            def mm(t, Y, p=p):
                for half in range(2):
                    for hl in range(11):
                        K.matmul(Y[:, half * 512:(half + 1) * 512], HT[:, hl, t * 128:(t + 1) * 128],
                                 woutb[p][:, hl, half * 512:(half + 1) * 512], start=(hl == 0), stop=(hl == 10))
            self.run_tiles(E, tiles, srcs, mm, partial, dsts, nxt)
        A.release(m0)


_CACHE = {}


def _get_program(upto="all", debug=False):
    key = (upto, debug)
    if key not in _CACHE:
        b = Builder(upto=upto, debug=debug)
        nc = b.build()
        _CACHE[key] = (nc, b)
    return _CACHE[key]


def _na_table(rpb):
    rpb = np.asarray(rpb, dtype=np.float32)
    kc = np.arange(64)[:, None]
    qc = np.arange(64)[None, :]
    col_start = np.clip(qc - 8, 0, 48)
    col_ok = (kc >= col_start) & (kc < col_start + 16)
    dc = np.clip(kc - qc, -15, 15) + 15
    tab = np.full((3, 2, 64, 8, 14, 64), NEG, dtype=np.float32)
    for v in range(3):
        for ki in range(2):
            for idx in range(14):
                drr = idx - 3
                dr = (10 if ki == 0 else 11) - drr
                if dr < 0 or dr > 14:
                    continue
                if v in (1, 2) and drr < ki:
                    continue
                if v in (0, 1) and drr > ki + 7:
                    continue
                g = rpb[:, dr][:, dc]
                g = np.where(col_ok[None], g, np.float32(NEG))
                tab[v, ki, :, :, idx, :] = g.transpose(1, 0, 2)
    return np.ascontiguousarray(tab.reshape(3, 128, 8 * 14 * 64))


ALL_INPUT_NAMES = (
    "x", "c", "ctx", "c_ctx",
    "l0_w_ada", "l0_b_ada", "l0_ffn1_w_in", "l0_ffn1_w_out", "l0_mix_w_in", "l0_mix_w_out", "l0_na_rpb",
    "l0_gla_w_gate_f", "l0_gla_b_gate_f", "l0_gla_w_gate_b", "l0_gla_b_gate_b", "l0_gla_norm_g",
    "l0_ffn2_w_in", "l0_ffn2_w_out",
    "l1_w_ada", "l1_b_ada", "l1_ffn1_w_in", "l1_ffn1_w_out", "l1_mix_w_in", "l1_mix_w_out",
    "l1_lambda_q1", "l1_lambda_k1", "l1_lambda_q2", "l1_lambda_k2", "l1_subln_g",
    "l1_ffn2_w_in", "l1_ffn2_w_out",
)


def _in_maps(inputs, n_cores=8, names=None):
    ident = np.eye(128, dtype=np.float32)
    maps = []
    shared = {}
    for l in (0, 1):
        for nm in W_NAMES[l]:
            a = np.ascontiguousarray(inputs["l%d_%s" % (l, nm)], dtype=np.float32)
            if nm == "b_ada":
                a = a.reshape(1, -1)
            shared["l%d_%s" % (l, nm)] = a
    shared["ident"] = ident
    if names is None or "gla_tri" in names:
        si = np.arange(128)[:, None]
        ti = np.arange(128)[None, :]
        g = np.float32(-1.0 / 16.0)
        shared["gla_tri"] = np.stack([(si <= ti) * g, (si >= ti) * g, (si > ti) * g, (si < ti) * g]).astype(np.float32)
        shared["gla_msk"] = np.stack([(si <= ti), (si > ti)]).astype(np.float32)
        shared["gla_wgf"] = np.concatenate([np.asarray(inputs["l0_gla_w_gate_f"], np.float32),
                                            np.asarray(inputs["l0_gla_b_gate_f"], np.float32).reshape(1, -1)], 0)
        shared["gla_wgb"] = np.concatenate([np.asarray(inputs["l0_gla_w_gate_b"], np.float32),
                                            np.asarray(inputs["l0_gla_b_gate_b"], np.float32).reshape(1, -1)], 0)
        shared["gla_ng"] = np.asarray(inputs["l0_gla_norm_g"], np.float32).reshape(1, 128)
    if names is None or "rope_cos" in names:
        t = np.arange(SEQ)
        row = (t // 64).astype(np.float32)
        col = (t % 64).astype(np.float32)
        inv = (np.float32(10000.0) ** (-np.arange(0, 32, 2, dtype=np.float32) / np.float32(32))).astype(np.float32)
        ar = row[:, None] * inv
        ac = col[:, None] * inv
        ang = np.concatenate([ar, ar, ac, ac], -1).astype(np.float32)
        sgn = np.where((np.arange(64) % 32) < 16, -1.0, 1.0).astype(np.float32)
        cosT = np.cos(ang).astype(np.float32).T
        sinT = (np.sin(ang).astype(np.float32) * sgn[None, :]).T
        shared["rope_cos"] = np.ascontiguousarray(np.concatenate([cosT, cosT], 0))
        shared["rope_sin"] = np.ascontiguousarray(np.concatenate([sinT, sinT], 0))
        pm = np.zeros((128, 128), np.float32)
        for dst in range(128):
            src = dst + 16 if (dst % 32) < 16 else dst - 16
            pm[src, dst] = 1.0
        shared["rope_perm"] = pm
        shared["lamv"] = np.concatenate([np.asarray(inputs["l1_lambda_" + k], np.float32).reshape(-1)
                                         for k in ("q1", "k1", "q2", "k2")]).reshape(1, 256)
        shared["subln_g"] = np.asarray(inputs["l1_subln_g"], np.float32).reshape(1, 128)
    if names is None or "na_tab" in names:
        shared["na_tab"] = _na_table(inputs["l0_na_rpb"])
    shared["c_ctx"] = np.ascontiguousarray(inputs["c_ctx"], dtype=np.float32).reshape(1, D)
    for b in range(n_cores):
        m = dict(shared)
        m["x"] = np.ascontiguousarray(inputs["x"][b], dtype=np.float32)
        m["ctx"] = np.ascontiguousarray(inputs["ctx"][b], dtype=np.float32)
        m["c"] = np.ascontiguousarray(inputs["c"][b], dtype=np.float32).reshape(1, D)
        if names is not None:
            m = {k: v for k, v in m.items() if k in names}
        maps.append(m)
    return maps


def kernel(**inputs):
    nc, b = _get_program()
    maps = _in_maps(inputs, names=set(b.din.keys()))
    res = run_bass_kernel_spmd(nc, maps, core_ids=list(range(8)))
    out = np.stack([np.asarray(r["out"], dtype=np.float32) for r in res.results], axis=0)
    return out
```
